# Optimizing a Trainium2 kernel written in Bass

```python
import jax, jax.numpy as jnp
from jax import lax
import numpy as np

D_MODEL = 1024
BATCH = 8
SEQ = 2048
DEPTH = 4
DEC_BATCH = 128
DEC_SEQ = 1
PAST_LEN = 2048
PAGE_SIZE = 128

HEAD_DIM = 64
A_WIDTH = D_MODEL // 4
A_GROUPS = A_WIDTH // HEAD_DIM
CHUNK = 128
B_WIDTH = D_MODEL // 2
B_HEADS = B_WIDTH // HEAD_DIM
DECAY_LORA = 64
ICLR_LORA = 64
GATE_LORA = 128
B_COLS = 3 * B_WIDTH + DECAY_LORA + ICLR_LORA + GATE_LORA
C_WIDTH = D_MODEL // 4
C_HEADS = C_WIDTH // HEAD_DIM
SB_BLOCK = 128
SB_SCALE = HEAD_DIM ** -0.5
SB_BIAS_INIT = -6.0
MIX_WIDTH = A_WIDTH + B_WIDTH + C_WIDTH
IN_COLS = 2 * A_WIDTH + B_COLS + 3 * C_WIDTH
D_FF = -(-8 * D_MODEL // (3 * 256)) * 256
PLE_DIM = 256
RMS_EPS = 1e-6
LN_EPS = 1e-5
GN_EPS = 64e-5

kernel_name = 'hybrid_gmlp_rwkv7_stickbreak_decoder_step'


def rmsnorm(x, g):
    xf = x.astype(jnp.float32)
    y = xf * lax.rsqrt(jnp.mean(xf * xf, axis=-1, keepdims=True) + RMS_EPS)
    return (y * g.astype(jnp.float32)).astype(x.dtype)


def layernorm(x, g, b):
    xf = x.astype(jnp.float32)
    xc = xf - jnp.mean(xf, axis=-1, keepdims=True)
    y = xc * lax.rsqrt(jnp.mean(xc * xc, axis=-1, keepdims=True) + LN_EPS)
    return (y * g.astype(jnp.float32) + b.astype(jnp.float32)).astype(x.dtype)


def chunk_spatial_gate(u, v, w_s, b_s):
    bsz, t = u.shape[:2]
    n_chunks = -(-t // CHUNK)
    pad = n_chunks * CHUNK - t
    vp = jnp.pad(v, ((0, 0), (0, pad), (0, 0), (0, 0))).reshape(bsz, n_chunks, CHUNK, A_GROUPS, HEAD_DIM)
    causal = jnp.tril(jnp.ones((CHUNK, CHUNK), dtype=bool))
    w = jnp.where(causal[None], w_s, 0.0)
    mixed = jnp.einsum('gij,bcjgd->bcigd', w, vp) + b_s.T[None, None, :, :, None]
    mixed = mixed.reshape(bsz, n_chunks * CHUNK, A_GROUPS, HEAD_DIM)[:, :t]
    return u * mixed


def rwkv7_mix(pb, shift_prev, wkv_prev, mu, w0, w2, a0, a2, g2, k_k, k_a, r_k, ln_g, ln_b):
    bsz, t, _ = pb.shape
    prev = jnp.concatenate([shift_prev[:, None].astype(pb.dtype), pb[:, :-1]], axis=1)
    xs = pb + (prev - pb) * mu
    r, k, v, wl, al, gl = jnp.split(
        xs, [B_WIDTH, 2 * B_WIDTH, 3 * B_WIDTH, 3 * B_WIDTH + DECAY_LORA,
             3 * B_WIDTH + DECAY_LORA + ICLR_LORA], axis=-1)
    w_log = -jax.nn.softplus(-(w0 + jnp.tanh(wl) @ w2)) - 0.5
    decay = jnp.exp(-jnp.exp(w_log.astype(jnp.float32)))
    a = jax.nn.sigmoid(a0 + al @ a2)
    g = jax.nn.sigmoid(gl) @ g2

    def heads(z):
        return z.reshape(bsz, t, B_HEADS, HEAD_DIM).astype(jnp.float32)

    kk = heads(k * k_k)
    kk = kk * lax.rsqrt(jnp.sum(kk * kk, axis=-1, keepdims=True) + 1e-12)
    k = k * (1.0 + (a - 1.0) * k_a)
    rh, kh, vh, ah, wh = heads(r), heads(k), heads(v), heads(a), heads(decay)

    def step(S, inp):
        r_t, w_t, k_t, v_t, kk_t, a_t = inp
        s_kk = jnp.einsum('bhij,bhj->bhi', S, kk_t)
        S = (S * w_t[:, :, None, :]
             - s_kk[:, :, :, None] * (kk_t * a_t)[:, :, None, :]
             + v_t[:, :, :, None] * k_t[:, :, None, :])
        return S, jnp.einsum('bhij,bhj->bhi', S, r_t)

    seq_in = tuple(jnp.moveaxis(z, 1, 0) for z in (rh, wh, kh, vh, kk, ah))
    s_fin, ys = lax.scan(step, wkv_prev.astype(jnp.float32), seq_in)
    y = jnp.moveaxis(ys, 0, 1)
    yc = y - jnp.mean(y, axis=-1, keepdims=True)
    yn = yc * lax.rsqrt(jnp.mean(yc * yc, axis=-1, keepdims=True) + GN_EPS)
    yn = yn.reshape(bsz, t, B_WIDTH) * ln_g + ln_b
    bonus = jnp.sum(rh * kh * r_k, axis=-1, keepdims=True) * vh
    out = (yn + bonus.reshape(bsz, t, B_WIDTH)) * g
    return out.astype(pb.dtype), s_fin, pb[:, -1]


def sb_attend(q, k, v, q_pos, k_pos, bias):
    s = (jnp.einsum('bqhd,bkhd->bhqk', q.astype(jnp.float32), k.astype(jnp.float32)) * SB_SCALE
         + bias.astype(jnp.float32)[None, :, None, None])
    causal = k_pos[None, :] < q_pos[:, None]
    log_beta = jax.nn.log_sigmoid(s)
    log_keep = jnp.where(causal, jax.nn.log_sigmoid(-s), 0.0)
    log_keep_after = lax.cumsum(log_keep, axis=3, reverse=True) - log_keep
    att = jnp.where(causal, jnp.exp(log_beta + log_keep_after), 0.0)
    return jnp.einsum('bhqk,bkhd->bqhd', att, v.astype(jnp.float32))


def sb_prompt(q, k, v, bias):
    bsz, t = q.shape[:2]
    n_blk = -(-t // SB_BLOCK)
    qp = jnp.pad(q, ((0, 0), (0, n_blk * SB_BLOCK - t), (0, 0), (0, 0)))
    qb = jnp.moveaxis(qp.reshape(bsz, n_blk, SB_BLOCK, C_HEADS, HEAD_DIM), 1, 0)
    k_pos = jnp.arange(t)

    def block(args):
        q_blk, b_idx = args
        q_pos = b_idx * SB_BLOCK + jnp.arange(SB_BLOCK)
        return sb_attend(q_blk, k, v, q_pos, k_pos, bias)

    out = lax.map(block, (qb, jnp.arange(n_blk)))
    return jnp.moveaxis(out, 0, 1).reshape(bsz, n_blk * SB_BLOCK, C_HEADS, HEAD_DIM)[:, :t]


def run_trunk(x, pe, shift0, wkv0, paged, P):
    h = x
    bsz, t, _ = x.shape
    k_rows, v_rows, wkv_fin, shift_fin, av_rows = [], [], [], [], []
    for i in range(DEPTH):
        hn = rmsnorm(h, P['mix_norm'][i])
        proj = hn @ P['w_in'][i]
        pa, pb, pc = jnp.split(proj, [2 * A_WIDTH, 2 * A_WIDTH + B_COLS], axis=-1)
        a_u, a_v = jnp.split(pa, 2, axis=-1)
        a_v = layernorm(a_v, P['a_ln_g'][i], P['a_ln_b'][i])
        ya = chunk_spatial_gate(a_u.reshape(bsz, t, A_GROUPS, HEAD_DIM),
                                a_v.reshape(bsz, t, A_GROUPS, HEAD_DIM),
                                P['a_ws'][i], P['a_bs'][i]).reshape(bsz, t, A_WIDTH)
        yb, s_new, sh_new = rwkv7_mix(pb, shift0[i], wkv0[i], P['b_mu'][i], P['b_w0'][i], P['b_w2'][i],
                                      P['b_a0'][i], P['b_a2'][i], P['b_g2'][i], P['b_kk'][i],
                                      P['b_ka'][i], P['b_rk'][i], P['b_ln_g'][i], P['b_ln_b'][i])
        q, k, v = (z.reshape(bsz, t, C_HEADS, HEAD_DIM) for z in jnp.split(pc, 3, axis=-1))
        q = rmsnorm(q, P['c_qn'][i])
        k = rmsnorm(k, P['c_kn'][i])
        if paged is None:
            yc = sb_prompt(q, k, v, P['c_bias'][i])
        else:
            cache_k, cache_v, page_table = paged
            past = page_table.shape[1] * cache_k.shape[2]
            k_past = cache_k[i][page_table].reshape(bsz, past, C_HEADS, HEAD_DIM).astype(k.dtype)
            v_past = cache_v[i][page_table].reshape(bsz, past, C_HEADS, HEAD_DIM).astype(v.dtype)
            k_all = jnp.concatenate([k_past, k], axis=1)
            v_all = jnp.concatenate([v_past, v], axis=1)
            yc = sb_attend(q, k_all, v_all, past + jnp.arange(t), jnp.arange(past + t), P['c_bias'][i])
            av_rows.append(a_v)
        yc = yc.astype(h.dtype).reshape(bsz, t, C_WIDTH)
        mix = jnp.concatenate([ya.astype(h.dtype), yb.astype(h.dtype), yc], axis=-1)
        h = h + mix @ P['w_out'][i]
        hn = rmsnorm(h, P['ffn_norm'][i])
        h = h + (jax.nn.silu(hn @ P['w_gate'][i]) * (hn @ P['w_up'][i])) @ P['w_down'][i]
        gate = jax.nn.sigmoid(rmsnorm(h, P['ple_norm'][i]) @ P['w_ple_gate'][i])
        h = h + (pe[i] @ P['w_ple'][i]) * gate
        k_rows.append(k)
        v_rows.append(v)
        wkv_fin.append(s_new)
        shift_fin.append(sh_new)
    av = jnp.stack(av_rows) if av_rows else None
    return h, jnp.stack(k_rows), jnp.stack(v_rows), jnp.stack(wkv_fin), jnp.stack(shift_fin), av


def setup_inputs(seed: int = 0) -> dict:
    key = jax.random.key(seed)
    ks = iter(jax.random.split(key, 64))

    def nrm(shape, scale=1.0):
        return jax.random.normal(next(ks), shape, jnp.float32) * scale

    def unif(shape, lo, hi):
        return jax.random.uniform(next(ks), shape, jnp.float32, lo, hi)

    def gain(shape):
        return 1.0 + nrm(shape, 0.02)

    n_pages = PAST_LEN // PAGE_SIZE
    n_used = DEC_BATCH * n_pages
    n_phys = n_used + max(n_used // 4, 1)
    page_table = jax.random.permutation(next(ks), n_phys)[:n_used].reshape(DEC_BATCH, n_pages).astype(jnp.int32)
    L = DEPTH
    return {
        'x_prompt': nrm((BATCH, SEQ, D_MODEL)),
        'x_sample': nrm((DEC_BATCH, DEC_SEQ, D_MODEL)),
        'cache_k': nrm((L, n_phys, PAGE_SIZE, C_HEADS, HEAD_DIM)),
        'cache_v': nrm((L, n_phys, PAGE_SIZE, C_HEADS, HEAD_DIM)),
        'state_wkv': nrm((L, DEC_BATCH, B_HEADS, HEAD_DIM, HEAD_DIM), 0.5),
        'state_shift': nrm((L, DEC_BATCH, B_COLS)),
        'page_table': page_table,
        'p_prompt': nrm((L, BATCH, SEQ, PLE_DIM)),
        'p_sample': nrm((L, DEC_BATCH, DEC_SEQ, PLE_DIM)),
        'mix_norm': gain((L, D_MODEL)),
        'w_in': nrm((L, D_MODEL, IN_COLS), D_MODEL ** -0.5),
        'a_ln_g': gain((L, A_WIDTH)),
        'a_ln_b': nrm((L, A_WIDTH), 0.02),
        'a_ws': nrm((L, A_GROUPS, CHUNK, CHUNK), 0.5 * CHUNK ** -0.5),
        'a_bs': 1.0 + nrm((L, A_GROUPS, CHUNK), 0.1),
        'b_mu': unif((L, B_COLS), 0.0, 1.0),
        'b_w0': unif((L, B_WIDTH), -6.0, -1.0),
        'b_w2': nrm((L, DECAY_LORA, B_WIDTH), 0.5 * DECAY_LORA ** -0.5),
        'b_a0': nrm((L, B_WIDTH), 0.5),
        'b_a2': nrm((L, ICLR_LORA, B_WIDTH), 0.5 * ICLR_LORA ** -0.5),
        'b_g2': nrm((L, GATE_LORA, B_WIDTH), GATE_LORA ** -0.5),
        'b_kk': 0.85 + nrm((L, B_WIDTH), 0.05),
        'b_ka': 1.0 + nrm((L, B_WIDTH), 0.05),
        'b_rk': nrm((L, B_HEADS, HEAD_DIM), 0.1),
        'b_ln_g': gain((L, B_WIDTH)),
        'b_ln_b': nrm((L, B_WIDTH), 0.02),
        'c_qn': gain((L, HEAD_DIM)),
        'c_kn': gain((L, HEAD_DIM)),
        'c_bias': SB_BIAS_INIT + nrm((L, C_HEADS), 0.1),
        'w_out': nrm((L, MIX_WIDTH, D_MODEL), MIX_WIDTH ** -0.5),
        'ffn_norm': gain((L, D_MODEL)),
        'w_gate': nrm((L, D_MODEL, D_FF), D_MODEL ** -0.5),
        'w_up': nrm((L, D_MODEL, D_FF), D_MODEL ** -0.5),
        'w_down': nrm((L, D_FF, D_MODEL), D_FF ** -0.5),
        'ple_norm': gain((L, D_MODEL)),
        'w_ple_gate': nrm((L, D_MODEL, D_MODEL), D_MODEL ** -0.5),
        'w_ple': nrm((L, PLE_DIM, D_MODEL), PLE_DIM ** -0.5),
    }


def reference(x_prompt, x_sample, cache_k, cache_v, state_wkv, state_shift, page_table, p_prompt, p_sample,
              mix_norm, w_in, a_ln_g, a_ln_b, a_ws, a_bs, b_mu, b_w0, b_w2, b_a0, b_a2, b_g2, b_kk, b_ka,
              b_rk, b_ln_g, b_ln_b, c_qn, c_kn, c_bias, w_out, ffn_norm, w_gate, w_up, w_down, ple_norm,
              w_ple_gate, w_ple):
    P = dict(mix_norm=mix_norm, w_in=w_in, a_ln_g=a_ln_g, a_ln_b=a_ln_b, a_ws=a_ws, a_bs=a_bs,
             b_mu=b_mu, b_w0=b_w0, b_w2=b_w2, b_a0=b_a0, b_a2=b_a2, b_g2=b_g2, b_kk=b_kk, b_ka=b_ka,
             b_rk=b_rk, b_ln_g=b_ln_g, b_ln_b=b_ln_b, c_qn=c_qn, c_kn=c_kn, c_bias=c_bias, w_out=w_out,
             ffn_norm=ffn_norm, w_gate=w_gate, w_up=w_up, w_down=w_down, ple_norm=ple_norm,
             w_ple_gate=w_ple_gate, w_ple=w_ple)
    bp = x_prompt.shape[0]
    shift0_p = jnp.zeros((DEPTH, bp, B_COLS), x_prompt.dtype)
    wkv0_p = jnp.zeros((DEPTH, bp, B_HEADS, HEAD_DIM, HEAD_DIM), jnp.float32)
    y_prompt, k_prompt, v_prompt, wkv_prompt, shift_prompt, _ = run_trunk(
        x_prompt, p_prompt, shift0_p, wkv0_p, None, P)
    y_sample, k_sample, v_sample, wkv_sample, shift_sample, av_sample = run_trunk(
        x_sample, p_sample, state_shift, state_wkv, (cache_k, cache_v, page_table), P)
    return (y_prompt, y_sample, k_prompt, v_prompt, wkv_prompt, shift_prompt,
            k_sample, v_sample, wkv_sample, shift_sample, av_sample)
```

```python
import numpy as np
import concourse.bass as bass
import concourse.mybir as mybir

F32 = mybir.dt.float32
BF16 = mybir.dt.bfloat16
I32 = mybir.dt.int32
AF = mybir.ActivationFunctionType
ALU = mybir.AluOpType
AX = mybir.AxisListType


class _Rec:
    __slots__ = ("lw", "rd")

    def __init__(self):
        self.lw = None
        self.rd = {}


class _Sem:
    def __init__(self, h, name):
        self.h = h
        self.name = name
        self.n = 0


class K:
    __slots__ = ("ap", "key")

    def __init__(self, ap, key):
        self.ap = ap
        self.key = key


def _ap(x):
    return x.ap if isinstance(x, K) else x


def _key(x):
    if isinstance(x, K):
        return x.key
    return x.tensor.name


class FW:
    def __init__(self, nc, n_dma_sems=16):
        self.nc = nc
        self.recs = {}
        self.engs = {}
        for nm, e in (("pe", nc.tensor), ("act", nc.scalar), ("dve", nc.vector),
                      ("pool", nc.gpsimd), ("sp", nc.sync)):
            s = _Sem(nc.alloc_semaphore("s_" + nm), nm)
            self.engs[nm] = (e, s)
        self.waited = {nm: {} for nm in self.engs}
        self.dsems = {}
        self.dnext = {}
        self.sems = {s.name: s for (_, s) in self.engs.values()}
        for q in ("sp", "pool", "act"):
            self.dsems[q] = [_Sem(nc.alloc_semaphore("d%s%d" % (q, i)), "d%s%d" % (q, i)) for i in range(n_dma_sems)]
            self.dnext[q] = 0
            for s in self.dsems[q]:
                self.sems[s.name] = s
        self.n_inst = 0
        self.psum_names = set()

    def rec(self, key):
        r = self.recs.get(key)
        if r is None:
            r = self.recs[key] = _Rec()
        return r

    def _need(self, en, writes, reads):
        need = {}

        def add(sv, same_ok):
            if sv is None:
                return
            sn, v = sv
            if sn == en and same_ok:
                return
            if need.get(sn, 0) < v:
                need[sn] = v

        pe = en == "pe"
        for x in reads:
            r = self.rec(_key(x))
            add(r.lw, pe)
            if _ap(x).tensor.name in self.psum_names:
                for sn, v in r.rd.items():
                    if sn != en:
                        add((sn, v), False)
        for x in writes:
            r = self.rec(_key(x))
            add(r.lw, pe)
            for sn, v in r.rd.items():
                add((sn, v), pe)
        return need

    def _emit_waits(self, en, need):
        e, _ = self.engs[en]
        w = self.waited[en]
        for sn, v in need.items():
            if w.get(sn, 0) >= v:
                continue
            e.wait_ge(self.sems[sn].h, v)
            w[sn] = v

    def op(self, en, fn, writes, reads):
        need = self._need(en, writes, reads)
        self._emit_waits(en, need)
        e, s = self.engs[en]
        ins = fn(e)
        s.n += 1
        ins.then_inc(s.h, 1)
        self.n_inst += 1
        for x in reads:
            self.rec(_key(x)).rd[en] = s.n
        for x in writes:
            r = self.rec(_key(x))
            r.lw = (en, s.n)
            r.rd = {}
        return ins

    def dma(self, out, in_, q="sp", fn=None, extra_reads=()):
        writes = [out]
        reads = [in_] + list(extra_reads)
        need = self._need("dma", writes, reads)
        ds = self.dsems[q][self.dnext[q]]
        self.dnext[q] = (self.dnext[q] + 1) % len(self.dsems[q])
        if ds.n > 0:
            need[ds.name] = max(need.get(ds.name, 0), ds.n)
        self._emit_waits(q, need)
        e, _ = self.engs[q]
        if fn is None:
            ins = e.dma_start(out=_ap(out), in_=_ap(in_))
        else:
            ins = fn(e)
        ds.n += 16
        ins.then_inc(ds.h, 16)
        self.n_inst += 1
        for x in reads:
            self.rec(_key(x)).rd[ds.name] = ds.n
        r = self.rec(_key(out))
        r.lw = (ds.name, ds.n)
        r.rd = {}
        return ins

    def barrier(self):
        vals = {s.name: s.n for s in self.sems.values() if s.n > 0}
        for en in self.engs:
            self._emit_waits(en, dict(vals))

    def finish(self):
        vals = {s.name: s.n for s in self.sems.values() if s.n > 0}
        self._emit_waits("sp", vals)

    def mm(self, out, lhsT, rhs, start=True, stop=True, **kw):
        return self.op("pe", lambda e: e.matmul(_ap(out), _ap(lhsT), _ap(rhs), start=start, stop=stop, **kw),
                       [out], [lhsT, rhs] + ([] if start else [out]))

    def transpose(self, out, in_, ident):
        return self.op("pe", lambda e: e.transpose(_ap(out), _ap(in_), _ap(ident)), [out], [in_, ident])

    def act(self, out, in_, func, bias=None, scale=1.0, en="act", accum_out=None):
        reads = [in_]
        kw = {}
        if bias is not None:
            if not isinstance(bias, (int, float)):
                reads.append(bias)
                kw["bias"] = _ap(bias)
            else:
                kw["bias"] = float(bias)
        if not isinstance(scale, (int, float)):
            reads.append(scale)
            kw["scale"] = _ap(scale)
        else:
            kw["scale"] = float(scale)
        writes = [out]
        if accum_out is not None:
            writes.append(accum_out)
            kw["accum_out"] = _ap(accum_out)
        return self.op(en, lambda e: e.activation(_ap(out), _ap(in_), func, **kw), writes, reads)

    def tt(self, out, a, b, op, en="dve"):
        return self.op(en, lambda e: e.tensor_tensor(_ap(out), _ap(a), _ap(b), op), [out], [a, b])

    def ts(self, out, a, s1, op0, s2=None, op1=None, en="dve", accum_out=None):
        reads = [a]
        v1 = s1
        v2 = s2
        if not isinstance(s1, (int, float)):
            reads.append(s1)
            v1 = _ap(s1)
        if s2 is not None and not isinstance(s2, (int, float)):
            reads.append(s2)
            v2 = _ap(s2)
        kw = {}
        writes = [out]
        if accum_out is not None:
            writes.append(accum_out)
            kw["accum_out"] = _ap(accum_out)
        if op1 is None:
            return self.op(en, lambda e: e.tensor_scalar(_ap(out), _ap(a), v1, None, op0, **kw), writes, reads)
        return self.op(en, lambda e: e.tensor_scalar(_ap(out), _ap(a), v1, v2, op0, op1, **kw), writes, reads)

    def stt(self, out, a, s, b, op0, op1, en="dve"):
        reads = [a, b]
        sv = s
        if not isinstance(s, (int, float)):
            reads.append(s)
            sv = _ap(s)
        return self.op(en, lambda e: e.scalar_tensor_tensor(_ap(out), _ap(a), sv, _ap(b), op0, op1), [out], reads)

    def copy(self, out, in_, en="dve"):
        if en == "act":
            return self.op(en, lambda e: e.copy(_ap(out), _ap(in_)), [out], [in_])
        return self.op(en, lambda e: e.tensor_copy(_ap(out), _ap(in_)), [out], [in_])

    def reduce(self, out, in_, op=None, axis=None, en="dve"):
        op = op or ALU.add
        axis = axis or AX.X
        return self.op(en, lambda e: e.tensor_reduce(_ap(out), _ap(in_), axis, op), [out], [in_])

    def memset(self, out, val, en="pool"):
        return self.op(en, lambda e: e.memset(_ap(out), val), [out], [])


import math
import os
import numpy as np

D = 1024
HD = 64
DFF = 2816
SB_SCALE = HD ** -0.5
EXPM05 = math.exp(-0.5)


def make_consts(NPG):
    c = {}
    i = np.arange(128)
    c["ident"] = np.eye(128, dtype=np.float32)
    c["ones"] = np.ones((128, 128), np.float32)
    c["ntincl"] = -(i[:, None] >= i[None, :]).astype(np.float32)
    ms = (i[:, None] < i[None, :]).astype(np.float32)
    c["mstrict"] = np.tile(ms, (1, 4))
    c["triljt"] = (i[:, None] <= i[None, :]).astype(np.float32)
    s = np.arange(64)
    su = (s[:, None] < s[None, :]).astype(np.float32)
    iu = (s[:, None] <= s[None, :]).astype(np.float32)
    c["maskA"] = np.tile(np.concatenate([-su, iu], 1)[:, None, :], (1, 8, 1)).reshape(64, 1024)
    c["maskB"] = np.tile(np.concatenate([su, iu], 1)[:, None, :], (1, 8, 1)).reshape(64, 1024)
    sl = -(s[None, :] < s[:, None]).astype(np.float32)
    c["maskC"] = np.tile(sl[:, None, :], (1, 8, 1)).reshape(64, 512)
    c["eye8"] = np.tile(np.eye(64, dtype=np.float32)[:, None, :], (1, 8, 1)).reshape(64, 512)
    c["triblk"] = ((i[:, None] // 64 == i[None, :] // 64) & (i[:, None] <= i[None, :])).astype(np.float32)
    c["chind"] = (i[:, None] // 64 == np.arange(2)[None, :]).astype(np.float32)
    c["shiftA"] = (i[None, :] == i[:, None] + 1).astype(np.float32)
    sb = np.zeros((128, 128), np.float32)
    sb[127, 0] = 1.0
    c["shiftB"] = sb
    G = min(128, 16 * NPG)
    g = np.arange(G)
    c["pgsuf"] = ((g[:, None] // NPG == g[None, :] // NPG) & (g[:, None] > g[None, :])).astype(np.float32)
    SG = G // NPG
    c["seqind"] = (g[:, None] // NPG == np.arange(SG)[None, :]).astype(np.float32)
    return c


WNAMES = ["mix_norm", "w_in", "a_ln_g", "a_ln_b", "a_ws", "a_bs", "b_mu", "b_w0", "b_w2", "b_a0", "b_a2", "b_g2",
          "b_kk", "b_ka", "b_rk", "b_ln_g", "b_ln_b", "c_qn", "c_kn", "c_bias", "w_out", "ffn_norm", "w_gate",
          "w_up", "w_down", "ple_norm", "w_ple_gate", "w_ple"]


def build(L, T, NS, NPG, NPHYS, wshapes, stages=("A", "B", "C", "F", "P")):
    nc = bass.Bass("TRN2", target_bir_lowering=False)
    NTK = T // 128
    NT = T + NS
    G = min(128, NS * NPG)
    SG = G // NPG
    NG = NS // SG

    def din(name, shape, dt=F32):
        return nc.dram_tensor(name, list(shape), dt, kind="ExternalInput").ap()

    def dout(name, shape):
        return nc.dram_tensor(name, list(shape), F32, kind="ExternalOutput").ap()

    def dscr(name, shape, dt=F32):
        return nc.dram_tensor(name, list(shape), dt, kind="Internal").ap()

    I = {}
    I["xp"] = din("xp", [T, D])
    I["xs"] = din("xs", [NS, D])
    I["cache_k"] = din("cache_k", [L, NPHYS, 128 * 256])
    I["cache_v"] = din("cache_v", [L, NPHYS, 128 * 256])
    I["swkv"] = din("swkv", [L, NS * 8, 4096])
    I["sshift"] = din("sshift", [L, NS, 1792])
    I["ptab"] = din("ptab", [NS * NPG, 1], I32)
    I["pp"] = din("pp", [L, T, 256])
    I["psm"] = din("psm", [L, NS, 256])
    W = {n: din(n, wshapes[n]) for n in WNAMES}
    CN = make_consts(NPG)
    C = {n: din("c_" + n, CN[n].shape) for n in CN}

    O = {}
    O["y_p"] = dout("y_p", [T, D])
    O["y_s"] = dout("y_s", [NS, D])
    O["k_p"] = dout("k_p", [L, T, 256])
    O["v_p"] = dout("v_p", [L, T, 256])
    O["wkv_p"] = dout("wkv_p", [L, 8, 64, 64])
    O["shift_p"] = dout("shift_p", [L, 1792])
    O["k_s"] = dout("k_s", [L, NS, 256])
    O["v_s"] = dout("v_s", [L, NS, 256])
    O["wkv_s"] = dout("wkv_s", [L, NS * 8, 4096])
    O["shift_s"] = dout("shift_s", [L, NS, 1792])
    O["av_s"] = dout("av_s", [L, NS, 256])

    scr_q = dscr("scr_q", [NS, 256])
    scr_b = dscr("scr_b", [NS, 8, 8, 64])
    scr_y = dscr("scr_y", [NS * 8, 64])
    scr_f = dscr("scr_f", [NS, 2, 512])

    f = FW(nc)
    f.marks = []
    def mark(lbl):
        f.marks.append((lbl, f.sems['pe'].n))
    import contextlib
    es = contextlib.ExitStack()
    es.enter_context(nc.allow_non_contiguous_dma(reason="small parameter loads"))

    def sb(name, shape, dt=F32):
        return nc.alloc_sbuf_tensor(name, list(shape), dt).ap()

    banks = [nc.alloc_psum_tensor("bank%d" % i, [128, 512], F32).ap() for i in range(7)]
    ptb_ = nc.alloc_psum_tensor("ptb", [128, 1024], BF16).ap()

    def bankb(i):
        assert i == 7
        return ptb_

    f.psum_names = {b.tensor.name for b in banks} | {ptb_.tensor.name}

    ident_f = sb("ident_f", [128, 128]); f.dma(ident_f, C["ident"])
    ident_b = sb("ident_b", [128, 128], BF16); f.dma(ident_b, C["ident"], q="pool")
    ones_b = sb("ones_b", [128, 128], BF16); f.dma(ones_b, C["ones"], q="pool")
    negones_b = sb("negones_b", [128, 128], BF16)
    ntincl_b = sb("ntincl_b", [128, 128], BF16); f.dma(ntincl_b, C["ntincl"], q="pool")
    mstrict_b = sb("mstrict_b", [128, 512], BF16); f.dma(mstrict_b, C["mstrict"], q="pool")
    triljt = sb("triljt", [128, 128]); f.dma(triljt, C["triljt"])
    maskA = sb("maskA", [64, 1024]); f.dma(maskA, C["maskA"])
    maskB = sb("maskB", [64, 1024]); f.dma(maskB, C["maskB"])
    maskC = sb("maskC", [64, 512]); f.dma(maskC, C["maskC"])
    eye8 = sb("eye8", [64, 512]); f.dma(eye8, C["eye8"])
    triblk = sb("triblk", [128, 128]); f.dma(triblk, C["triblk"])
    chind = sb("chind", [128, 2]); f.dma(chind, C["chind"])
    shiftA = sb("shiftA", [128, 128], BF16); f.dma(shiftA, C["shiftA"], q="pool")
    shiftB = sb("shiftB", [128, 128], BF16); f.dma(shiftB, C["shiftB"], q="pool")
    pgsuf = sb("pgsuf", [G, G]); f.dma(pgsuf, C["pgsuf"])
    seqind = sb("seqind", [G, SG]); f.dma(seqind, C["seqind"])
    f.ts(negones_b, ones_b, -1.0, ALU.mult)
    onesf = sb("onesf", [128, 128]); f.dma(onesf, C["ones"])
    eps_rms = sb("eps_rms", [128, 1]); f.memset(eps_rms, 1e-6)
    eps_ln = sb("eps_ln", [128, 1]); f.memset(eps_ln, 1e-5)
    eps_gn = sb("eps_gn", [128, 1]); f.memset(eps_gn, 64e-5)
    eps_kk = sb("eps_kk", [128, 1]); f.memset(eps_kk, 1e-12)
    one_c = sb("one_c", [128, 1]); f.memset(one_c, 1.0)
    zero_c = sb("zero_c", [128, 1]); f.memset(zero_c, 0.0)

    hT = sb("hT", [128, 8, NT])
    class _Holder:
        pass
    hn = _Holder()
    hn.cm = None

    def hn_alloc(tag):
        hn.cm = nc.sbuf_tensor("hnT_" + tag, [128, 8, NT], BF16)
        hn.t = hn.cm.__enter__().ap()

    def hn_free():
        f.barrier()
        hn.cm.__exit__(None, None, None)
        hn.cm = None
        f.recs.clear()

    pb_scr = dscr("pb_scr", [NT, 1792])

    def rsqrt_(out, in_, eps_t, scale, np_):
        f.act(out, in_, AF.Ln, bias=eps_t[0:np_, :], scale=scale)
        f.act(out, out, AF.Exp, scale=-0.5)

    def sigmoid_(out, in_, np_, scale=1.0, tmp=None):
        f.act(out, in_, AF.Exp, scale=-scale)
        f.ts(out, out, 1.0, ALU.add)
        f.op("dve", lambda e: e.reciprocal(_ap(out), _ap(out)), [out], [out])

    tblocks = []
    c0 = 0
    while c0 < NT:
        n = min(512, NT - c0)
        tblocks.append((c0, n))
        c0 += n

    xt_cm = nc.sbuf_tensor("xt", [128, D], F32)
    xt = xt_cm.__enter__().ap()
    for i in range(NTK + 1):
        n = 128 if i < NTK else NS
        src = I["xp"][i * 128:(i + 1) * 128, :] if i < NTK else I["xs"]
        f.dma(xt[0:n, :], src)
        for half in range(2):
            pst = banks[half]
            for c in range(4):
                m = half * 4 + c
                f.transpose(pst[:, c * 128:c * 128 + n], xt[0:n, m * 128:(m + 1) * 128], ident_f[0:n, 0:n])
            f.copy(hT[:, half * 4:(half + 1) * 4, i * 128:i * 128 + n],
                   pst.rearrange("p (c t) -> p c t", c=4)[:, :, 0:n], en="act" if half else "dve")
    f.barrier()
    xt_cm.__exit__(None, None, None)
    f.recs.clear()

    def rmsnorm_all(gname, l):
        tag = "_%s_%d" % (gname, l)
        with nc.sbuf_tensor("gvec" + tag, [128, 8], F32) as gvec_h, nc.sbuf_tensor("sqb" + tag, [128, 8, 512], BF16) as sqb_h, \
                nc.sbuf_tensor("rstd_t" + tag, [128, 512], F32) as rstd_h:
            _rmsnorm_body(gname, l, gvec_h.ap(), sqb_h.ap(), rstd_h.ap())
            f.barrier()
        f.recs.clear()

    def _rmsnorm_body(gname, l, gvec, sqb, rstd_t):
        f.dma(gvec, W[gname][l].rearrange("(c p) -> p c", p=128))
        for (c0, n) in tblocks:
            f.act(sqb[:, :, 0:n], hT[:, :, c0:c0 + n], AF.Square)
            ps = banks[0]
            for k in range(8):
                f.mm(ps[:, 0:n], ones_b, sqb[:, k, 0:n], start=(k == 0), stop=(k == 7))
            rsqrt_(rstd_t[:, 0:n], ps[:, 0:n], eps_rms, 1.0 / D, 128)
            for k in range(8):
                f.stt(hn.t[:, k, c0:c0 + n], hT[:, k, c0:c0 + n], gvec[:, k:k + 1], rstd_t[:, 0:n],
                      ALU.mult, ALU.mult)

    def bcload(dst, row_ap, q="sp"):
        P = dst.shape[0]
        src = row_ap.unsqueeze(0).broadcast_to([P] + list(row_ap.shape))
        f.dma(dst, src, q=q)

    def proj_T(ps, ntok, col0, wt, wc0, wc1):
        for k in range(8):
            f.mm(ps[0:ntok, 0:wc1 - wc0], hn.t[:, k, col0:col0 + ntok], wt[:, k, wc0:wc1], start=(k == 0), stop=(k == 7))

    def add_h(m, c0, n, ps):
        f.tt(hT[:, m, c0:c0 + n], hT[:, m, c0:c0 + n], ps, ALU.add)

    for l in range(L):
        mark('L%d norm' % l)
        hn_alloc('m%d' % l)
        rmsnorm_all("mix_norm", l)
        mark('L%d pass1-prompt' % l)
        f.barrier()
        with contextlib.ExitStack() as p1:
            def sb1(name, shape, dt=F32):
                return p1.enter_context(nc.sbuf_tensor(name + "_%d" % l, list(shape), dt)).ap()
            p1a = contextlib.ExitStack()
            p1b = contextlib.ExitStack()

            def sb1a(name, shape, dt=F32):
                return p1a.enter_context(nc.sbuf_tensor(name + "_%d" % l, list(shape), dt)).ap()

            def sb1b(name, shape, dt=F32):
                return p1b.enter_context(nc.sbuf_tensor(name + "_%d" % l, list(shape), dt)).ap()
            wAC = sb1("wAC", [128, 8, 1280], BF16)
            win = W["w_in"][l].rearrange("(k p) n -> p k n", p=128)
            f.dma(wAC[:, :, 0:512], win[:, :, 0:512], q="pool")
            f.dma(wAC[:, :, 512:1280], win[:, :, 2304:3072], q="pool")
            woA = sb1("woA", [128, 2, D], BF16)
            f.dma(woA, W["w_out"][l][0:256, :].rearrange("(k p) n -> p k n", p=128), q="pool")
            lng = sb1("lng", [128, 256]); bcload(lng, W["a_ln_g"][l])
            lnb = sb1("lnb", [128, 256]); bcload(lnb, W["a_ln_b"][l])
            bs_t = sb1("bs_t", [128, 4]); f.dma(bs_t, W["a_bs"][l].rearrange("g i -> i g"))
            ws0 = sb1("ws0", [NS, 4]); f.dma(ws0, W["a_ws"][l][:, 0, 0:1].rearrange("g o -> o g").broadcast_to([NS, 4]))
            bs0 = sb1("bs0", [NS, 4]); f.dma(bs0, W["a_bs"][l][:, 0:1].rearrange("g o -> o g").broadcast_to([NS, 4]))
            gqk = sb1("gqk", [128, 512])
            f.dma(gqk[:, 0:256].rearrange("p (h d) -> p h d", h=4),
                  W["c_qn"][l].unsqueeze(0).unsqueeze(0).broadcast_to([128, 4, 64]))
            f.dma(gqk[:, 256:512].rearrange("p (h d) -> p h d", h=4),
                  W["c_kn"][l].unsqueeze(0).unsqueeze(0).broadcast_to([128, 4, 64]))
            f.ts(gqk[:, 0:256], gqk[:, 0:256], SB_SCALE, ALU.mult)
            cb1 = sb1("cb1", [1, 4]); f.dma(cb1, W["c_bias"][l].unsqueeze(0))
            biasrow = sb1("biasrow", [1, 4, 128], BF16)
            f.copy(biasrow, cb1.unsqueeze(2).broadcast_to([1, 4, 128]))
            cbG = sb1("cbG", [G, 4]); bcload(cbG, W["c_bias"][l])

            u_sb = sb1("u_sb", [128, 256])
            vc = sb1("vc", [128, 256])
            st1 = sb1("st1", [128, 8])
            st2 = sb1("st2", [128, 8])
            ya = sb1("ya", [128, 256])
            yab = sb1("yab", [128, 256], BF16)
            sq1 = sb1("sq1", [128, 512])
            qkn = sb1("qkn", [128, 512])
            qkb = sb1("qkb", [128, 512], BF16)
            vf = sb1("vf", [128, 256])
            woC = sb1a("woC", [64, 4, D], BF16)
            WmT = sb1a("WmT", [128, 4, 128], BF16)
            wsl = sb1a("wsl", [128, 4, 128])
            KT = sb1a("KT", [128, 2, T], BF16)
            Vall = sb1a("Vall", [128, NTK, 256], BF16)
            QT = sb1a("QT", [128, 2, 128], BF16)
            vnb = sb1a("vnb", [128, 256], BF16)
            yaT = sb1a("yaT", [128, 2, 128], BF16)
            e_t2 = [sb1a("e_t%d" % j, [128, 512], BF16) for j in range(2)]
            sp_all = sb1a("sp_all", [128, NTK, 512], BF16)
            att_t2 = [sb1a("att_t%d" % j, [128, 512], BF16) for j in range(2)]
            Cs2 = [sb1a("Cs%d" % j, [128, 512], BF16) for j in range(2)]
            ycT = sb1a("ycT", [64, 512], BF16)
            f.dma(woC, W["w_out"][l][768:1024, :].rearrange("(h p) n -> p h n", p=64), q="pool")
            f.dma(wsl, W["a_ws"][l].rearrange("g i j -> i g j"))
            for g in range(4):
                f.transpose(banks[0][:, g * 128:(g + 1) * 128], wsl[:, g, :], ident_f)
            f.tt(WmT, banks[0].rearrange("p (g i) -> p g i", g=4), triljt.unsqueeze(1).broadcast_to([128, 4, 128]), ALU.mult)

            def mixer_A(n, col0, sample):
                ps = banks[0]
                BIS3 = int(os.environ.get("BIS3", "99"))
                proj_T(ps, n, col0, wAC, 0, 512)
                if BIS3 < 1: return
                f.copy(u_sb[0:n], ps[0:n, 0:256], en="act")
                f.reduce(st1[0:n, 0:1], ps[0:n, 256:512])
                f.ts(st1[0:n, 0:1], st1[0:n, 0:1], -1.0 / 256, ALU.mult)
                if BIS3 < 2: return
                f.ts(vc[0:n], ps[0:n, 256:512], st1[0:n, 0:1], ALU.add)
                f.tt(ya[0:n], vc[0:n], vc[0:n], ALU.mult)
                f.reduce(st1[0:n, 1:2], ya[0:n])
                if BIS3 < 3: return
                rsqrt_(st1[0:n, 1:2], st1[0:n, 1:2], eps_ln, 1.0 / 256, n)
                f.ts(vc[0:n], vc[0:n], st1[0:n, 1:2], ALU.mult)
                if BIS3 < 4: return
                f.tt(vc[0:n], vc[0:n], lng[0:n], ALU.mult)
                f.tt(vc[0:n], vc[0:n], lnb[0:n], ALU.add)
                if BIS3 < 5: return
                if sample:
                    f.dma(O["av_s"][l], vc[0:n])
                    v3 = vc[0:n].rearrange("p (g d) -> p g d", g=4)
                    f.tt(ya[0:n].rearrange("p (g d) -> p g d", g=4), v3, ws0.unsqueeze(2).broadcast_to([NS, 4, 64]), ALU.mult)
                    f.tt(ya[0:n].rearrange("p (g d) -> p g d", g=4), ya[0:n].rearrange("p (g d) -> p g d", g=4),
                         bs0.unsqueeze(2).broadcast_to([NS, 4, 64]), ALU.add)
                    f.tt(yab[0:n], ya[0:n], u_sb[0:n], ALU.mult)
                else:
                    f.copy(vnb, vc, en="pool")
                    if BIS3 < 6: return
                    pm = banks[1]
                    for g in range(4):
                        f.mm(pm[:, g * 64:(g + 1) * 64], WmT[:, g, :], vnb[:, g * 64:(g + 1) * 64])
                    if BIS3 < 7: return
                    f.tt(ya.rearrange("p (g d) -> p g d", g=4), pm[:, 0:256].rearrange("p (g d) -> p g d", g=4),
                         bs_t.unsqueeze(2).broadcast_to([128, 4, 64]), ALU.add)
                    f.tt(yab, ya, u_sb, ALU.mult)

            def qkv(n, col0):
                pq = banks[2]
                pv = banks[3]
                proj_T(pq, n, col0, wAC, 512, 1024)
                proj_T(pv, n, col0, wAC, 1024, 1280)
                f.act(sq1[0:n], pq[0:n], AF.Square)
                f.reduce(st2[0:n], sq1[0:n].rearrange("p (h d) -> p h d", h=8))
                rsqrt_(st2[0:n], st2[0:n], eps_rms, 1.0 / 64, n)
                f.tt(qkn[0:n].rearrange("p (h d) -> p h d", h=8), pq[0:n].rearrange("p (h d) -> p h d", h=8),
                     st2[0:n].unsqueeze(2).broadcast_to([n, 8, 64]), ALU.mult)
                f.tt(qkn[0:n], qkn[0:n], gqk[0:n], ALU.mult, en="pool")
                f.copy(qkb[0:n], qkn[0:n], en="pool")
                f.copy(vf[0:n], pv[0:n, 0:256], en="act")

            BIS = int(os.environ.get("BIS", "9"))
            for i in range(NTK if (("A" in stages or "C" in stages) and BIS >= 2) else 0):
                t0 = i * 128
                col0 = t0
                BIS2 = int(os.environ.get("BIS2", "9"))
                if "A" in stages:
                    mixer_A(128, col0, False)
                    if BIS2 >= 2:
                        pt = bankb(7)
                        for k in range(2):
                            f.transpose(pt[:, k * 128:(k + 1) * 128], yab[:, k * 128:(k + 1) * 128], ident_b)
                        f.copy(yaT, pt[:, 0:256].rearrange("p (k t) -> p k t", k=2), en="act")
                    else:
                        f.memset(yaT, 0.0)
                if BIS2 < 3:
                    continue
                if "C" in stages:
                    qkv(128, col0)
                    f.dma(O["k_p"][l, t0:t0 + 128, :], qkn[:, 256:512])
                    f.dma(O["v_p"][l, t0:t0 + 128, :], vf)
                    f.copy(Vall[:, i, :], vf, en="pool")
                    pt = bankb(7)
                    for k in range(2):
                        f.transpose(pt[:, 256 + k * 128:256 + (k + 1) * 128], qkb[:, k * 128:(k + 1) * 128], ident_b)
                        f.transpose(pt[:, 512 + k * 128:512 + (k + 1) * 128], qkb[:, 256 + k * 128:256 + (k + 1) * 128], ident_b)
                    f.copy(QT, pt[:, 256:512].rearrange("p (k t) -> p k t", k=2), en="act")
                    f.copy(KT[:, :, t0:t0 + 128], pt[:, 512:768].rearrange("p (k t) -> p k t", k=2), en="dve")
                    pO = banks[6]

                    def qk_into(ps, kb, last):
                        for h in range(4):
                            r0 = (h % 2) * 64
                            f.mm(ps[:, h * 128:(h + 1) * 128], KT[r0:r0 + 64, h // 2, kb * 128:(kb + 1) * 128],
                                 QT[r0:r0 + 64, h // 2, :], start=(h == 0), stop=False, skip_group_check=True)
                            f.mm(ps[:, h * 128:(h + 1) * 128], ones_b[0:1, :], biasrow[0:1, h, :], start=False,
                                 stop=(last and h == 3), skip_group_check=True)

                    def spk(kb):
                        return K(sp_all[:, kb, :], "sp_all%d" % kb)

                    for n1, kb in enumerate(range(i, -1, -1)):
                        pS = banks[4 + n1 % 2]
                        et = e_t2[n1 % 2]
                        qk_into(pS, kb, True)
                        f.act(et, pS, AF.Exp)
                        f.act(spk(kb), et, AF.Ln, bias=one_c)
                        if kb == i:
                            f.tt(spk(kb), spk(kb), mstrict_b, ALU.mult)
                    pend = None
                    for n2, kb in enumerate(range(i, -1, -1)):
                        pE = banks[4 + n2 % 2]
                        at = att_t2[n2 % 2]
                        cs_cur, cs_nxt = Cs2[n2 % 2], Cs2[(n2 + 1) % 2]
                        qk_into(pE, kb, False)
                        f.mm(pE, ntincl_b, spk(kb), start=False, stop=(kb == i), skip_group_check=True)
                        if kb < i:
                            f.mm(pE, negones_b, cs_cur, start=False, stop=True, skip_group_check=True)
                        if pend is not None:
                            pend()
                        f.act(at, pE, AF.Exp)
                        if kb == i:
                            f.tt(at, at, mstrict_b, ALU.mult)
                        if kb > 0:
                            if kb == i:
                                f.copy(cs_nxt, spk(kb), en="pool")
                            else:
                                f.tt(cs_nxt, cs_cur, spk(kb), ALU.add, en="pool")

                        def av(kb=kb, at=at):
                            for h in range(4):
                                f.mm(pO[0:64, h * 128:(h + 1) * 128], Vall[:, kb, h * 64:(h + 1) * 64],
                                     at[:, h * 128:(h + 1) * 128], start=(kb == i and h == 0),
                                     stop=(kb == 0 and h == 3), skip_group_check=True)
                        pend = av
                    pend()
                    f.copy(ycT, pO[0:64, :], en="act")
                for half in range(2):
                    ph = banks[half]
                    for mm_ in range(4):
                        m = half * 4 + mm_
                        ops = []
                        if "A" in stages:
                            ops += [(woA[:, k, m * 128:(m + 1) * 128], yaT[:, k, :]) for k in range(2)]
                        if "C" in stages:
                            ops += [(woC[:, h, m * 128:(m + 1) * 128], ycT[:, h * 128:(h + 1) * 128]) for h in range(4)]
                        for j, (a, b) in enumerate(ops):
                            f.mm(ph[:, mm_ * 128:(mm_ + 1) * 128], a, b, start=(j == 0), stop=(j == len(ops) - 1))
                    f.tt(hT[:, half * 4:(half + 1) * 4, t0:t0 + 128], hT[:, half * 4:(half + 1) * 4, t0:t0 + 128],
                         ph.rearrange("p (c t) -> p c t", c=4), ALU.add)

            f.barrier()
            p1a.close()
            f.recs.clear()
            mark('L%d pass1-sample' % l)
            col0 = T
            yaT_s = sb1b("yaT_s", [128, 2, NS], BF16)
            ycT_s = sb1b("ycT_s", [128, 2, NS], BF16)
            f.memset(yaT_s, 0.0)
            f.memset(ycT_s, 0.0)
            if "A" in stages and BIS >= 3:
                mixer_A(NS, col0, True)
                pt = bankb(7)
                for k in range(2):
                    f.transpose(pt[:, k * 128:k * 128 + NS], yab[0:NS, k * 128:(k + 1) * 128], ident_b[0:NS, 0:NS])
                f.copy(yaT_s, pt[:, 0:256].rearrange("p (k t) -> p k t", k=2)[:, :, 0:NS], en="act")
            if "C" in stages:
                qkv(NS, col0)
                f.dma(O["k_s"][l], qkn[0:NS, 256:512])
                f.dma(O["v_s"][l], vf[0:NS])
                f.dma(scr_q, qkn[0:NS, 0:256])
                PSL = 8
                NSL = 128 // PSL
                qrep = sb1b("qrep", [G, 256])
                idx = sb1b("idx", [G, 1], I32)
                idx8 = sb1b("idx8", [G, 1], I32)
                kv_t = [sb1b("kv%d" % j, [G, PSL * 256]) for j in range(2)]
                prod = sb1b("prod", [G, PSL * 256])
                s_all = sb1b("s_all", [G, 128, 4])
                e_d = sb1b("e_d", [G, 128, 4])
                sp_d = sb1b("sp_d", [G, 128, 4])
                cum_d = sb1b("cum_d", [G, 128, 4])
                tot_d = sb1b("tot_d", [G, 4])
                R_d = sb1b("R_d", [G, 4])
                att_d = sb1b("att_d", [G, 128, 4])
                yacc = sb1b("yacc", [G, 256])
                ypart = sb1b("ypart", [G, 256])
                for g in range(NG):
                    f.dma(idx, I["ptab"][g * G:(g + 1) * G, :])
                    f.ts(idx8, idx, NSL, ALU.mult)
                    f.dma(qrep, scr_q[g * SG:(g + 1) * SG, :].unsqueeze(1).broadcast_to([SG, NPG, 256]))
                    for sl in range(NSL):
                        kt = kv_t[sl % 2]
                        src = bass.AP(I["cache_k"].tensor, 0, [[PSL * 256, NPHYS * NSL], [1, PSL * 256]])
                        eo = l * NPHYS * 32768 + sl * PSL * 256
                        f.dma(kt, I["cache_k"], q="pool", extra_reads=[idx8],
                              fn=lambda e, kt=kt, src=src, eo=eo: e.indirect_dma_start(
                                  out=kt, out_offset=None, in_=src,
                                  in_offset=bass.IndirectOffsetOnAxis(ap=idx8[:, :], axis=0), element_offset=eo))
                        f.tt(prod.rearrange("p (s c) -> p s c", s=PSL), kt.rearrange("p (s c) -> p s c", s=PSL),
                             qrep.unsqueeze(1).broadcast_to([G, PSL, 256]), ALU.mult)
                        f.reduce(s_all[:, sl * PSL:(sl + 1) * PSL, :], prod.rearrange("p (s h d) -> p s h d", s=PSL, h=4))
                    f.tt(s_all, s_all, cbG.unsqueeze(1).broadcast_to([G, 128, 4]), ALU.add)
                    f.act(e_d, s_all, AF.Exp)
                    f.act(sp_d, e_d, AF.Ln, bias=one_c[0:G])
                    f.copy(cum_d[:, 127:128, :], sp_d[:, 127:128, :])
                    cur, nxt = sp_d, cum_d
                    sh = 1
                    bufs = [cum_d, e_d]
                    bi = 0
                    src_t = sp_d
                    while sh < 128:
                        dst_t = bufs[bi]
                        f.tt(dst_t[:, 0:128 - sh, :], src_t[:, 0:128 - sh, :], src_t[:, sh:128, :], ALU.add)
                        f.copy(dst_t[:, 128 - sh:128, :], src_t[:, 128 - sh:128, :], en="pool")
                        src_t = dst_t
                        bi ^= 1
                        sh *= 2
                    cumI = src_t
                    pR = banks[4]
                    f.copy(tot_d, cumI[:, 0, :])
                    f.mm(pR[0:G, 0:4], pgsuf, tot_d)
                    f.copy(R_d, pR[0:G, 0:4], en="act")
                    f.tt(att_d, s_all, cumI, ALU.subtract)
                    f.tt(att_d, att_d, R_d.unsqueeze(1).broadcast_to([G, 128, 4]), ALU.subtract)
                    f.act(att_d, att_d, AF.Exp)
                    for sl in range(NSL):
                        vt = kv_t[sl % 2]
                        src = bass.AP(I["cache_v"].tensor, 0, [[PSL * 256, NPHYS * NSL], [1, PSL * 256]])
                        eo = l * NPHYS * 32768 + sl * PSL * 256
                        f.dma(vt, I["cache_v"], q="pool", extra_reads=[idx8],
                              fn=lambda e, vt=vt, src=src, eo=eo: e.indirect_dma_start(
                                  out=vt, out_offset=None, in_=src,
                                  in_offset=bass.IndirectOffsetOnAxis(ap=idx8[:, :], axis=0), element_offset=eo))
                        f.tt(prod.rearrange("p (s h d) -> p s h d", s=PSL, h=4),
                             vt.rearrange("p (s h d) -> p s h d", s=PSL, h=4),
                             att_d[:, sl * PSL:(sl + 1) * PSL, :].unsqueeze(3).broadcast_to([G, PSL, 4, 64]), ALU.mult)
                        dst = yacc if sl == 0 else ypart
                        f.reduce(dst, prod.rearrange("p (s c) -> p c s", s=PSL))
                        if sl > 0:
                            f.tt(yacc, yacc, ypart, ALU.add, en="pool")
                    pY = banks[5]
                    for k in range(2):
                        f.mm(pY[:, k * SG:(k + 1) * SG], yacc[:, k * 128:(k + 1) * 128], seqind)
                    f.copy(ycT_s[:, :, g * SG:(g + 1) * SG], pY[:, 0:2 * SG].rearrange("p (k s) -> p k s", k=2), en="act")
            woC2 = sb1b("woC2", [128, 2, D], BF16)
            f.dma(woC2, W["w_out"][l][768:1024, :].rearrange("(k p) n -> p k n", p=128), q="pool")
            ph = banks[0]
            for m in range(8):
                ops = []
                if "A" in stages:
                    ops += [(woA[:, k, m * 128:(m + 1) * 128], yaT_s[:, k, :]) for k in range(2)]
                if "C" in stages:
                    ops += [(woC2[:, k, m * 128:(m + 1) * 128], ycT_s[:, k, :]) for k in range(2)]
                for j, (a, b) in enumerate(ops):
                    f.mm(ph[:, m * NS:(m + 1) * NS], a, b, start=(j == 0), stop=(j == len(ops) - 1))
            if "A" in stages or "C" in stages:
                f.tt(hT[:, :, T:T + NS], hT[:, :, T:T + NS], ph[:, 0:8 * NS].rearrange("p (c t) -> p c t", c=8), ALU.add)
            f.barrier()
            p1b.close()
        f.recs.clear()

        mark('L%d pass15' % l)
        if "B" in stages:
            with contextlib.ExitStack() as p15:
                wBf = p15.enter_context(nc.sbuf_tensor("wBf_%d" % l, [128, 8, 1792], BF16)).ap()
                stg = [p15.enter_context(nc.sbuf_tensor("stg%d_%d" % (j, l), [128, 1792], F32)).ap() for j in range(2)]
                f.dma(wBf, W["w_in"][l].rearrange("(k p) n -> p k n", p=128)[:, :, 512:2304], q="pool")
                cnt15 = 0
                for i in range(NTK + 1):
                    n = 128 if i < NTK else NS
                    st_ = stg[i % 2]
                    for (c0, ncol) in [(0, 512), (512, 512), (1024, 512), (1536, 256)]:
                        ps = banks[cnt15 % 4]
                        proj_T(ps, n, i * 128, wBf, c0, c0 + ncol)
                        f.copy(st_[0:n, c0:c0 + ncol], ps[0:n, 0:ncol], en="act" if cnt15 % 2 else "dve")
                        cnt15 += 1
                    f.dma(pb_scr[i * 128:i * 128 + n, :], st_[0:n, :])
                f.dma(O["shift_p"][l].unsqueeze(0), pb_scr[T - 1:T, :])
                f.dma(O["shift_s"][l], pb_scr[T:T + NS, :])
                f.barrier()
            f.recs.clear()
        hn_free()
        mark('L%d pass2' % l)
        if "B" in stages:
            with contextlib.ExitStack() as p2:
                def sb2(name, shape, dt=F32):
                    return p2.enter_context(nc.sbuf_tensor(name + "_%d" % l, list(shape), dt)).ap()
                _rwkv_pass(nc, f, l, L, T, NS, NTK, W, I, O, sb2, banks, bankb, hT, pb_scr, scr_b, scr_y, scr_f,
                           dict(ident_f=ident_f, ident_b=ident_b, maskA=maskA, maskB=maskB, maskC=maskC, eye8=eye8,
                                triblk=triblk, chind=chind, shiftA=shiftA, shiftB=shiftB, eps_gn=eps_gn, eps_kk=eps_kk,
                                one_c=one_c, onesf=onesf), rsqrt_, sigmoid_, bcload, proj_T)
                f.barrier()
            f.recs.clear()

        mark('L%d ffn' % l)
        if "F" in stages:
            hn_alloc('f%d' % l)
            rmsnorm_all("ffn_norm", l)
            with contextlib.ExitStack() as p3:
                def sb3(name, shape, dt=F32):
                    return p3.enter_context(nc.sbuf_tensor(name + "_%d" % l, list(shape), dt)).ap()
                wg = [sb3("wg%d" % j, [128, 8, 512], BF16) for j in range(2)]
                wu = [sb3("wu%d" % j, [128, 8, 512], BF16) for j in range(2)]
                wd = [sb3("wd%d" % j, [128, 4, D], BF16) for j in range(2)]
                silu_t = sb3("silu_t", [128, 512])
                actT = [sb3("actT%d" % j, [128, 4, 512], BF16) for j in range(2)]
                nblk = (DFF + 511) // 512
                gsrc = W["w_gate"][l].rearrange("(k p) n -> p k n", p=128)
                usrc = W["w_up"][l].rearrange("(k p) n -> p k n", p=128)
                steps = []
                for b in range(nblk):
                    for (c0, n) in tblocks:
                        steps.append((b, c0, n))
                loaded = set()

                def load_block(b):
                    if b in loaded or b >= nblk:
                        return
                    loaded.add(b)
                    c0f = b * 512
                    nf = min(512, DFF - c0f)
                    j = b % 2
                    f.dma(wg[j][:, :, 0:nf], gsrc[:, :, c0f:c0f + nf], q="pool")
                    f.dma(wu[j][:, :, 0:nf], usrc[:, :, c0f:c0f + nf], q="pool")
                    f.dma(wd[j][:, 0:nf // 128, :], W["w_down"][l][c0f:c0f + nf, :].rearrange("(c p) n -> p c n", p=128), q="pool")

                def gateup(si):
                    b, c0, n = steps[si]
                    load_block(b)
                    j = b % 2
                    ncf = min(512, DFF - b * 512) // 128
                    aT = actT[si % 2]
                    for c in range(ncf):
                        pg = banks[(2 * c) % 4]
                        pu = banks[(2 * c + 1) % 4]
                        for k in range(8):
                            f.mm(pg[:, 0:n], wg[j][:, k, c * 128:(c + 1) * 128], hn.t[:, k, c0:c0 + n],
                                 start=(k == 0), stop=(k == 7))
                        for k in range(8):
                            f.mm(pu[:, 0:n], wu[j][:, k, c * 128:(c + 1) * 128], hn.t[:, k, c0:c0 + n],
                                 start=(k == 0), stop=(k == 7))
                        f.act(silu_t[:, 0:n], pg[:, 0:n], AF.Silu)
                        f.tt(aT[:, c, 0:n], silu_t[:, 0:n], pu[:, 0:n], ALU.mult)

                def down(si):
                    b, c0, n = steps[si]
                    j = b % 2
                    ncf = min(512, DFF - b * 512) // 128
                    aT = actT[si % 2]
                    for m in range(8):
                        pd = banks[4 + m % 3]
                        for c in range(ncf):
                            f.mm(pd[:, 0:n], wd[j][:, c, m * 128:(m + 1) * 128], aT[:, c, 0:n],
                                 start=(c == 0), stop=(c == ncf - 1))
                        add_h(m, c0, n, pd[:, 0:n])

                for si in range(len(steps) + 1):
                    if si < len(steps):
                        gateup(si)
                    if si > 0:
                        down(si - 1)
                f.barrier()
            f.recs.clear()
            hn_free()

        mark('L%d ple' % l)
        if "P" in stages:
            hn_alloc('p%d' % l)
            rmsnorm_all("ple_norm", l)
            with contextlib.ExitStack() as p4:
                def sb4(name, shape, dt=F32):
                    return p4.enter_context(nc.sbuf_tensor(name + "_%d" % l, list(shape), dt)).ap()
                wpg = sb4("wpg", [128, 8, D], BF16)
                wpl = sb4("wpl", [128, 2, D], BF16)
                f.dma(wpg, W["w_ple_gate"][l].rearrange("(k p) n -> p k n", p=128), q="pool")
                f.dma(wpl, W["w_ple"][l].rearrange("(k p) n -> p k n", p=128), q="pool")
                peT = sb4("peT", [128, 2, NT], BF16)
                pet = [sb4("pet%d" % j, [128, 256]) for j in range(2)]
                for i in range(NTK + 1):
                    n = 128 if i < NTK else NS
                    src = I["pp"][l, i * 128:(i + 1) * 128, :] if i < NTK else I["psm"][l]
                    pe_ = pet[i % 2]
                    f.dma(pe_[0:n], src)
                    pt = banks[i % 2]
                    for k in range(2):
                        f.transpose(pt[:, k * 128:k * 128 + n], pe_[0:n, k * 128:(k + 1) * 128], ident_f[0:n, 0:n])
                    f.copy(peT[:, :, i * 128:i * 128 + n], pt[:, 0:256].rearrange("p (k t) -> p k t", k=2)[:, :, 0:n],
                           en="act" if i % 2 else "dve")
                gate_t = [sb4("gate_t%d" % j, [128, 512]) for j in range(2)]
                cnt = 0
                for (c0, n) in tblocks:
                    for m in range(8):
                        pa = banks[2 + (cnt % 2) * 2]
                        pb_ = banks[3 + (cnt % 2) * 2]
                        gt = gate_t[cnt % 2]
                        cnt += 1
                        for k in range(8):
                            f.mm(pa[:, 0:n], wpg[:, k, m * 128:(m + 1) * 128], hn.t[:, k, c0:c0 + n],
                                 start=(k == 0), stop=(k == 7))
                        for k in range(2):
                            f.mm(pb_[:, 0:n], wpl[:, k, m * 128:(m + 1) * 128], peT[:, k, c0:c0 + n],
                                 start=(k == 0), stop=(k == 1))
                        f.act(gt[:, 0:n], pa[:, 0:n], AF.Sigmoid)
                        f.tt(gt[:, 0:n], gt[:, 0:n], pb_[:, 0:n], ALU.mult)
                        f.tt(hT[:, m, c0:c0 + n], hT[:, m, c0:c0 + n], gt[:, 0:n], ALU.add, en="pool")
                f.barrier()
            f.recs.clear()
            hn_free()

    mark('final')
    yt = [sb("yt%d" % j, [128, D]) for j in range(2)]
    for i in range(NTK + 1):
        n = 128 if i < NTK else NS
        y_ = yt[i % 2]
        for half in range(2):
            pst = banks[(i % 2) * 2 + half]
            for c in range(4):
                m = half * 4 + c
                f.transpose(pst[0:n, c * 128:(c + 1) * 128], hT[:, m, i * 128:i * 128 + n], ident_f)
            f.copy(y_[0:n, half * 512:(half + 1) * 512], pst[0:n, :], en="act" if half else "dve")
        dst = O["y_p"][i * 128:(i + 1) * 128, :] if i < NTK else O["y_s"]
        f.dma(dst, y_[0:n])
    f.finish()
    es.close()
    return nc, f


def _rwkv_pass(nc, f, l, L, T, NS, NTK, W, I, O, sb2_outer, banks, bankb, hT, pb_scr, scr_b, scr_y, scr_f, CT, rsqrt_, sigmoid_,
               bcload, proj_T):
    import contextlib
    CH = BF16
    NH = 8
    CW = NH * 64
    ident_f, ident_b = CT["ident_f"], CT["ident_b"]
    maskA, maskB, maskC, eye8 = CT["maskA"], CT["maskB"], CT["maskC"], CT["eye8"]
    triblk, chind, shiftA, shiftB = CT["triblk"], CT["chind"], CT["shiftA"], CT["shiftB"]
    eps_gn, eps_kk, one_c = CT["eps_gn"], CT["eps_kk"], CT["one_c"]
    ptb = bankb(7)
    win = W["w_in"][l].rearrange("(k p) n -> p k n", p=128)

    def v3(ap, h=NH):
        return ap.rearrange("p (h d) -> p h d", h=h)

    def bc3(ap, n, h=NH):
        return ap.unsqueeze(2).broadcast_to([n, h, 64])

    for hg in range(1):
      with contextlib.ExitStack() as ph_:
        def sb2(name, shape, dt=F32):
            return ph_.enter_context(nc.sbuf_tensor(name + "_%d_%d" % (l, hg), list(shape), dt)).ap()
        blocks = [(0, 512), (512, 512), (1024, 512), (1536, 256)]
        mu = sb2("mu", [128, 1792])
        bcload(mu, W["b_mu"][l])
        woB = sb2("woB", [128, 4, D], BF16)
        f.dma(woB, W["w_out"][l][256:768, :].rearrange("(k p) n -> p k n", p=128), q="pool")
        cs = slice(0, CW)
        w2 = sb2("w2", [64, CW], BF16); f.dma(w2, W["b_w2"][l][:, cs], q="pool")
        a2 = sb2("a2", [64, CW], BF16); f.dma(a2, W["b_a2"][l][:, cs], q="pool")
        g2 = sb2("g2", [128, CW], BF16); f.dma(g2, W["b_g2"][l][:, cs], q="pool")
        w0 = sb2("w0", [128, CW]); bcload(w0, W["b_w0"][l][cs])
        a0 = sb2("a0", [128, CW]); bcload(a0, W["b_a0"][l][cs])
        kkp = sb2("kkp", [128, CW]); bcload(kkp, W["b_kk"][l][cs])
        kap = sb2("kap", [128, CW]); bcload(kap, W["b_ka"][l][cs])
        rkp = sb2("rkp", [128, CW]); bcload(rkp, W["b_rk"][l].rearrange("h d -> (h d)")[cs])
        lng = sb2("blng", [128, CW]); bcload(lng, W["b_ln_g"][l][cs])
        lnb = sb2("blnb", [128, CW]); bcload(lnb, W["b_ln_b"][l][cs])

        ST = sb2("ST", [64, NH, 64]); f.memset(ST, 0.0)
        STb = sb2("STb", [64, NH, 64], BF16); f.memset(STb, 0.0)
        xs_bufs = [sb2("xs%d" % j, [128, 1792]) for j in range(2)]
        cur_b = [sb2("cur_b%d" % j, [128, 1792], BF16) for j in range(2)]
        f.memset(cur_b[1], 0.0)
        dtmp = sb2("dtmp", [128, CW])
        lb = sb2("lb", [128, 128], BF16)
        lb2 = sb2("lb2", [128, 128], BF16)
        lT = sb2("lT", [128, 3, 128], BF16)
        logw = sb2("logw", [128, CW])
        a_t = sb2("a_t", [128, CW])
        PKf = sb2("PKf", [128, 2, CW])
        PKf1 = sb2("PKf1", [64, 2, CW])
        PKb = sb2("PKb", [128, 3, CW], BF16)
        PKb1 = sb2("PKb1", [64, 3, CW], BF16)
        kk = sb2("kk", [128, CW])
        kmod = sb2("kmod", [128, CW])
        t1 = sb2("t1", [128, CW])
        t2 = sb2("t2", [128, CW])
        st8 = sb2("st8", [128, NH])
        st8b = sb2("st8b", [128, NH])
        eL = sb2("eL", [128, CW])
        enL = sb2("enL", [128, CW])
        fm_b = [sb2("fm_b%d" % j, [128, CW], BF16) for j in range(4)]
        QRT = sb2("QRT", [64, NH, 2, 2, 64], BF16)
        BKT = sb2("BKT", [64, NH, 2, 2, 64], BF16)
        GC = sb2("GC", [64, NH, 2])
        G1 = sb2("G1", [64, NH, 128], BF16)
        G2 = sb2("G2", [64, NH, 128], BF16)
        XA = [sb2("XA%d" % j, [64, NH, 64], CH) for j in range(2)]
        XT = [sb2("XT%d" % j, [64, NH, 64], CH) for j in range(2)]
        Pm = [sb2("Pm%d" % j, [64, NH, 64], CH) for j in range(2)]
        Wn = sb2("Wn", [64, NH, 64], BF16)
        U = sb2("U", [64, NH, 64], BF16)
        yc_t = sb2("yc_t", [64, CW])
        y2 = sb2("y2", [64, CW])
        ybb = sb2("ybb", [64, CW], BF16)
        ybT = sb2("ybT", [128, 4, 128], BF16)
        ss = sb2("ss", [NS, CW])

        def prep(n, xs):
            f.act(t1[0:n, 0:64], xs[0:n, 3 * CW:3 * CW + 64], AF.Exp, scale=-2.0)
            f.ts(t1[0:n, 0:64], t1[0:n, 0:64], 1.0, ALU.add)
            f.op("dve", lambda e: e.reciprocal(t1[0:n, 0:64], t1[0:n, 0:64]), [t1], [t1])
            f.ts(lb[0:n, 0:64], t1[0:n, 0:64], 2.0, ALU.mult, -1.0, ALU.add)
            f.copy(lb[0:n, 64:128], xs[0:n, 3 * CW + 64:3 * CW + 128], en="pool")
            sigmoid_(t1[0:n, 128:256], xs[0:n, 3 * CW + 128:3 * CW + 256], n)
            f.copy(lb2[0:n], t1[0:n, 128:256], en="pool")
            f.transpose(ptb[0:64, 0:n], lb[0:n, 0:64], ident_b[0:n, 0:n])
            f.transpose(ptb[0:64, 128:128 + n], lb[0:n, 64:128], ident_b[0:n, 0:n])
            f.transpose(ptb[:, 256:256 + n], lb2[0:n, :], ident_b[0:n, 0:n])
            f.copy(lT[0:64, 0:2, 0:n], ptb[0:64, 0:256].rearrange("p (k t) -> p k t", k=2)[:, :, 0:n], en="act")
            f.copy(lT[:, 2, 0:n], ptb[:, 256:256 + n], en="act")
            ps_w, ps_a, ps_g = banks[2], banks[3], banks[4]
            f.mm(ps_w[0:n, 0:CW], lT[0:64, 0, 0:n], w2)
            f.mm(ps_a[0:n, 0:CW], lT[0:64, 1, 0:n], a2)
            f.mm(ps_g[0:n, 0:CW], lT[:, 2, 0:n], g2)
            f.tt(t1[0:n], ps_w[0:n, 0:CW], w0[0:n], ALU.add)
            sigmoid_(t1[0:n], t1[0:n], n)
            f.ts(logw[0:n], t1[0:n], -EXPM05, ALU.mult)
            f.tt(t2[0:n], ps_a[0:n, 0:CW], a0[0:n], ALU.add)
            sigmoid_(a_t[0:n], t2[0:n], n)
            f.copy(PKf[0:n, 1, :], ps_g[0:n, 0:CW], en="act")
            r_ = xs[0:n, 0:CW]
            k_ = xs[0:n, CW:2 * CW]
            v_ = xs[0:n, 2 * CW:3 * CW]
            f.copy(PKb[0:n, 0, :], v_, en="pool")
            f.tt(kk[0:n], k_, kkp[0:n], ALU.mult)
            f.tt(t1[0:n], kk[0:n], kk[0:n], ALU.mult)
            f.reduce(st8[0:n], v3(t1[0:n]))
            rsqrt_(st8[0:n], st8[0:n], eps_kk, 1.0, n)
            f.tt(v3(kk[0:n]), v3(kk[0:n]), bc3(st8[0:n], n), ALU.mult)
            f.ts(t1[0:n], a_t[0:n], -1.0, ALU.add)
            f.tt(t1[0:n], t1[0:n], kap[0:n], ALU.mult)
            f.ts(t1[0:n], t1[0:n], 1.0, ALU.add)
            f.tt(kmod[0:n], k_, t1[0:n], ALU.mult)
            f.tt(t1[0:n], r_, kmod[0:n], ALU.mult)
            f.tt(t1[0:n], t1[0:n], rkp[0:n], ALU.mult)
            f.reduce(st8b[0:n], v3(t1[0:n]))
            f.tt(v3(PKf[0:n, 0, :]), v3(v_), bc3(st8b[0:n], n), ALU.mult)

        def shift_mix(n, xs, c0, ncol, prev):
            f.tt(dtmp[0:n, 0:ncol], prev, xs[0:n, c0:c0 + ncol], ALU.subtract)
            f.tt(dtmp[0:n, 0:ncol], dtmp[0:n, 0:ncol], mu[0:n, c0:c0 + ncol], ALU.mult, en="pool")
            f.tt(xs[0:n, c0:c0 + ncol], dtmp[0:n, 0:ncol], xs[0:n, c0:c0 + ncol], ALU.add, en="pool")

        def group_out(n, y_ap, bonus_ap, g_ap, out_bf):
            f.reduce(st8[0:n], v3(y_ap))
            f.ts(st8[0:n], st8[0:n], -1.0 / 64, ALU.mult)
            f.tt(v3(yc_t[0:n]), v3(y_ap), bc3(st8[0:n], n), ALU.add)
            f.tt(y2[0:n], yc_t[0:n], yc_t[0:n], ALU.mult)
            f.reduce(st8b[0:n], v3(y2[0:n]))
            rsqrt_(st8b[0:n], st8b[0:n], eps_gn, 1.0 / 64, n)
            f.tt(v3(yc_t[0:n]), v3(yc_t[0:n]), bc3(st8b[0:n], n), ALU.mult)
            f.tt(yc_t[0:n], yc_t[0:n], lng[0:n], ALU.mult)
            f.tt(yc_t[0:n], yc_t[0:n], lnb[0:n], ALU.add)
            f.tt(yc_t[0:n], yc_t[0:n], bonus_ap, ALU.add)
            f.tt(out_bf, yc_t[0:n], g_ap, ALU.mult)

        f.dma(xs_bufs[0], pb_scr[0:128, :])
        for i in range(NTK):
            t0 = i * 128
            cb = cur_b[i % 2]
            pvb = cur_b[(i + 1) % 2]
            xs = xs_bufs[i % 2]
            if i + 1 < NTK:
                f.dma(xs_bufs[(i + 1) % 2], pb_scr[t0 + 128:t0 + 256, :])
            f.copy(cb[:, 0:1024], xs[:, 0:1024], en="pool")
            f.copy(cb[:, 1024:1792], xs[:, 1024:1792], en="act")
            for j, (c0, ncol) in enumerate(blocks):
                pp = banks[j % 2]
                f.mm(pp[:, 0:ncol], shiftA, cb[:, c0:c0 + ncol], start=True, stop=False)
                f.mm(pp[:, 0:ncol], shiftB, pvb[:, c0:c0 + ncol], start=False, stop=True)
                shift_mix(128, xs, c0, ncol, pp[:, 0:ncol])
            BB = int(os.environ.get("BISB", "99"))
            if BB < 2: continue
            prep(128, xs)
            if BB < 3: continue
            psL = banks[5]
            f.mm(psL[:, 0:CW], triblk, logw)
            f.act(eL, psL[:, 0:CW], AF.Exp)
            f.act(enL, psL[:, 0:CW], AF.Exp, scale=-1.0)
            f.tt(fm_b[1], xs[:, 0:CW], eL, ALU.mult)
            f.tt(t1, psL[:, 0:CW], logw, ALU.subtract)
            f.act(eL, t1, AF.Exp)
            f.tt(fm_b[0], kk, eL, ALU.mult)
            f.tt(t2, kk, a_t, ALU.mult, en="pool")
            f.tt(fm_b[2], t2, enL, ALU.mult)
            f.tt(fm_b[3], kmod, enL, ALU.mult)
            f.copy(PKb[:, 1, :], fm_b[2], en="pool")
            f.copy(PKb[:, 2, :], fm_b[3], en="pool")
            for which, (src, dst, slot) in enumerate([(fm_b[0], QRT, 0), (fm_b[1], QRT, 1), (fm_b[2], BKT, 0), (fm_b[3], BKT, 1)]):
                for h in range(NH):
                    f.transpose(ptb[0:64, h * 128:(h + 1) * 128], src[:, h * 64:(h + 1) * 64], ident_b)
                f.copy(dst[:, :, :, slot, :], ptb[0:64, 0:NH * 128].rearrange("p (q c t) -> p q c t", q=NH, c=2),
                       en="act" if which % 2 else "dve")
            psG = banks[6]
            for h in range(NH):
                f.mm(psG[0:64, h * 2:(h + 1) * 2], logw[:, h * 64:(h + 1) * 64], chind)
            f.act(GC, psG[0:64, 0:2 * NH].rearrange("p (q c) -> p q c", q=NH), AF.Exp)
            f.dma(PKb1, PKb[64:128])
            f.dma(PKf1, PKf[64:128])
            if BB < 4: continue

            for c in range(2):
                Vc = PKb[0:64, 0, :] if c == 0 else PKb1[:, 0, :]
                Bc = PKb[0:64, 1, :] if c == 0 else PKb1[:, 1, :]
                Kc = PKb[0:64, 2, :] if c == 0 else PKb1[:, 2, :]
                bon = PKf[0:64, 0, :] if c == 0 else PKf1[:, 0, :]
                gg = PKf[0:64, 1, :] if c == 0 else PKf1[:, 1, :]
                for h in range(NH):
                    qr = QRT[:, h, c, :, :].rearrange("p w t -> p (w t)")
                    bk, cb_ = h // 4, (h % 4) * 128
                    f.mm(banks[0 + bk][0:64, cb_:cb_ + 128], BKT[:, h, c, 0, :], qr)
                    f.mm(banks[2 + bk][0:64, cb_:cb_ + 128], BKT[:, h, c, 1, :], qr)
                    f.mm(banks[4][0:64, h * 64:(h + 1) * 64], QRT[:, h, c, 0, :], BKT[:, h, c, 0, :])
                for bk in range(2):
                    f.tt(G1[:, bk * 4:(bk + 1) * 4, :].rearrange("p h t -> p (h t)"), banks[0 + bk][0:64, :],
                         maskA[:, bk * 512:(bk + 1) * 512], ALU.mult, en="dve")
                    f.tt(G2[:, bk * 4:(bk + 1) * 4, :].rearrange("p h t -> p (h t)"), banks[2 + bk][0:64, :],
                         maskB[:, bk * 512:(bk + 1) * 512], ALU.mult, en="dve")
                if BB < 5: continue
                A_, AT_, P_ = XA[0], XT[0], Pm[0]
                f.copy(A_, G1[:, :, 0:64], en="pool")
                f.tt(AT_.rearrange("p h t -> p (h t)"), banks[4][0:64, 0:CW], maskC[:, 0:CW], ALU.mult)
                f.tt(P_.rearrange("p h t -> p (h t)"), A_.rearrange("p h t -> p (h t)"), eye8[:, 0:CW], ALU.add)
                pi = 0
                for k in range(1, 6):
                    An, ATn, Pn = XA[k % 2], XT[k % 2], Pm[(pi + 1) % 2]
                    psA, psAT, psP = banks[5], banks[6], banks[4]
                    for h in range(NH):
                        if k < 5:
                            f.mm(psA[0:64, h * 64:(h + 1) * 64], AT_[:, h, :], A_[:, h, :])
                        f.mm(psAT[0:64, h * 64:(h + 1) * 64], A_[:, h, :], AT_[:, h, :])
                    if k < 5:
                        f.copy(An.rearrange("p h t -> p (h t)"), psA[0:64, 0:CW], en="act")
                    f.copy(ATn.rearrange("p h t -> p (h t)"), psAT[0:64, 0:CW], en="dve")
                    for h in range(NH):
                        f.mm(psP[0:64, h * 64:(h + 1) * 64], ATn[:, h, :], P_[:, h, :])
                    f.tt(Pn.rearrange("p h t -> p (h t)"), psP[0:64, 0:CW], P_.rearrange("p h t -> p (h t)"), ALU.add)
                    A_, AT_, P_ = An, ATn, Pn
                    pi += 1
                TT = P_
                if BB < 6: continue
                psW, psU, psY, psS = banks[0], banks[1], banks[2], banks[3]
                for h in range(NH):
                    f.mm(psW[0:64, h * 64:(h + 1) * 64], QRT[:, h, c, 0, :], STb[:, h, :], start=True, stop=False)
                    f.mm(psW[0:64, h * 64:(h + 1) * 64], G2[:, h, 0:64], Vc[:, h * 64:(h + 1) * 64], start=False, stop=True)
                f.ts(Wn.rearrange("p h t -> p (h t)"), psW[0:64, 0:CW], -1.0, ALU.mult)
                for h in range(NH):
                    f.mm(psU[0:64, h * 64:(h + 1) * 64], TT[:, h, :], Wn[:, h, :])
                f.copy(U.rearrange("p h t -> p (h t)"), psU[0:64, 0:CW], en="act")
                for h in range(NH):
                    o_ = psY[0:64, h * 64:(h + 1) * 64]
                    f.mm(o_, QRT[:, h, c, 1, :], STb[:, h, :], start=True, stop=False)
                    f.mm(o_, G1[:, h, 64:128], U[:, h, :], start=False, stop=False)
                    f.mm(o_, G2[:, h, 64:128], Vc[:, h * 64:(h + 1) * 64], start=False, stop=True)
                for h in range(NH):
                    o_ = psS[0:64, h * 64:(h + 1) * 64]
                    f.mm(o_, Bc[:, h * 64:(h + 1) * 64], U[:, h, :], start=True, stop=False)
                    f.mm(o_, Kc[:, h * 64:(h + 1) * 64], Vc[:, h * 64:(h + 1) * 64], start=False, stop=True)
                f.tt(ST, ST, psS[0:64, 0:CW].rearrange("p (h d) -> p h d", h=NH), ALU.add)
                f.tt(ST, ST, GC[:, :, c].unsqueeze(2).broadcast_to([64, NH, 64]), ALU.mult)
                f.copy(STb, ST, en="pool")
                if BB < 7: continue
                group_out(64, psY[0:64, 0:CW], bon, gg, ybb)
                for k in range(4):
                    f.transpose(ptb[:, k * 64:(k + 1) * 64], ybb[:, k * 128:(k + 1) * 128], ident_b[0:64, 0:64])
                f.copy(ybT[:, :, c * 64:(c + 1) * 64], ptb[:, 0:256].rearrange("p (k t) -> p k t", k=4), en="act")
            if BB < 7: continue
            for half in range(2):
                ph = banks[5 + half]
                for mm_ in range(4):
                    m = half * 4 + mm_
                    for k in range(4):
                        f.mm(ph[:, mm_ * 128:(mm_ + 1) * 128], woB[:, k, m * 128:(m + 1) * 128], ybT[:, k, :],
                             start=(k == 0), stop=(k == 3))
                f.tt(hT[:, half * 4:(half + 1) * 4, t0:t0 + 128], hT[:, half * 4:(half + 1) * 4, t0:t0 + 128],
                     ph.rearrange("p (c t) -> p c t", c=4), ALU.add)
        for h in range(NH):
            f.transpose(banks[0][0:64, h * 64:(h + 1) * 64], ST[:, h, :], ident_f[0:64, 0:64])
        f.copy(y2, banks[0][0:64, 0:CW])
        f.dma(O["wkv_p"][l].rearrange("h i j -> i h j"), y2.rearrange("i (h j) -> i h j", h=NH))

        n = NS
        xs = xs_bufs[0]
        f.dma(xs[0:n, :], pb_scr[T:T + n, :])
        for (c0, ncol) in blocks:
            f.dma(ss[:, 0:ncol], I["sshift"][l][:, c0:c0 + ncol])
            shift_mix(n, xs, c0, ncol, ss[:, 0:ncol])
        prep(n, xs)
        f.act(t1[0:n], logw[0:n], AF.Exp)
        f.tt(t2[0:n], kk[0:n], a_t[0:n], ALU.mult)
        for q, src in enumerate([xs[0:n, 0:CW], t1[0:n], kmod[0:n], xs[0:n, 2 * CW:3 * CW], kk[0:n], t2[0:n]]):
            f.dma(scr_b[:, :, q, :], src.rearrange("s (h d) -> s h d", h=8))
        f.dma(scr_f, PKf[0:n])
        f.barrier()
      f.recs.clear()

    with contextlib.ExitStack() as ps_:
        def sb3(name, shape, dt=F32):
            return ps_.enter_context(nc.sbuf_tensor(name + "_s%d" % l, list(shape), dt)).ap()
        n = NS
        NP = NS * 8
        woBf = sb3("woBf", [128, 4, D], BF16)
        f.dma(woBf, W["w_out"][l][256:768, :].rearrange("(k p) n -> p k n", p=128), q="pool")
        lngf = sb3("lngf", [NS, 512]); bcload(lngf, W["b_ln_g"][l])
        lnbf = sb3("lnbf", [NS, 512]); bcload(lnbf, W["b_ln_b"][l])
        bg = sb3("bg", [NS, 2, 512])
        f.dma(bg, scr_f)
        pkp = sb3("pkp", [NP, 8, 64])
        f.dma(pkp[:, 0:6, :], scr_b.rearrange("s h q d -> (s h) q d")[:, 0:6, :])
        S = sb3("S", [NP, 64, 64])
        tmp = sb3("tmpS", [NP, 64, 64])
        f.dma(S.rearrange("p i j -> p (i j)"), I["swkv"][l])
        r_p, w_p, k_p, v_p, kk_p, b_p = (pkp[:, q, :] for q in range(6))

        def bj(ap):
            return ap.unsqueeze(1).broadcast_to([NP, 64, 64])

        def bi(ap):
            return ap.unsqueeze(2).broadcast_to([NP, 64, 64])

        sa = sb3("sa", [NP, 64])
        ysm = sb3("ysm", [NP, 64])
        f.tt(tmp, S, bj(kk_p), ALU.mult)
        f.reduce(sa, tmp)
        f.tt(S, S, bj(w_p), ALU.mult)
        f.tt(tmp, bi(sa), bj(b_p), ALU.mult, en="pool")
        f.tt(S, S, tmp, ALU.subtract)
        f.tt(tmp, bi(v_p), bj(k_p), ALU.mult, en="pool")
        f.tt(S, S, tmp, ALU.add)
        f.dma(O["wkv_s"][l], S.rearrange("p i j -> p (i j)"))
        f.tt(tmp, S, bj(r_p), ALU.mult)
        f.reduce(ysm, tmp)
        f.dma(scr_y, ysm)
        ytm = sb3("ytm", [NS, 512])
        f.dma(ytm, scr_y.rearrange("(s h) d -> s (h d)", h=8))
        st8 = sb3("st8", [NS, 8]); st8b = sb3("st8b", [NS, 8])
        yc_t = sb3("yc_t", [NS, 512]); y2 = sb3("y2", [NS, 512]); ybb = sb3("ybb", [NS, 512], BF16)
        v8 = lambda ap: ap.rearrange("p (h d) -> p h d", h=8)
        b8 = lambda ap: ap.unsqueeze(2).broadcast_to([n, 8, 64])
        f.reduce(st8, v8(ytm))
        f.ts(st8, st8, -1.0 / 64, ALU.mult)
        f.tt(v8(yc_t), v8(ytm), b8(st8), ALU.add)
        f.tt(y2, yc_t, yc_t, ALU.mult)
        f.reduce(st8b, v8(y2))
        rsqrt_(st8b, st8b, eps_gn, 1.0 / 64, n)
        f.tt(v8(yc_t), v8(yc_t), b8(st8b), ALU.mult)
        f.tt(yc_t, yc_t, lngf, ALU.mult)
        f.tt(yc_t, yc_t, lnbf, ALU.add)
        f.tt(yc_t, yc_t, bg[:, 0, :], ALU.add)
        f.tt(ybb, yc_t, bg[:, 1, :], ALU.mult)
        ybT_s = sb3("ybT_s", [128, 4, NS], BF16)
        for k in range(4):
            f.transpose(ptb[:, k * NS:(k + 1) * NS], ybb[:, k * 128:(k + 1) * 128], ident_b[0:n, 0:n])
        f.copy(ybT_s, ptb[:, 0:4 * NS].rearrange("p (k t) -> p k t", k=4), en="act")
        ph = banks[5]
        for m in range(8):
            for k in range(4):
                f.mm(ph[:, m * NS:(m + 1) * NS], woBf[:, k, m * 128:(m + 1) * 128], ybT_s[:, k, :], start=(k == 0), stop=(k == 3))
        f.tt(hT[:, :, T:T + NS], hT[:, :, T:T + NS], ph[:, 0:8 * NS].rearrange("p (c t) -> p c t", c=8), ALU.add)
        f.barrier()
    f.recs.clear()


from concourse.bass_utils import run_bass_kernel_spmd

_cache = {}

def run(inputs, L, T, NS_total, NPG, stages=("A", "B", "C", "F", "P"), trace=False):
    NC = 8
    NS = NS_total // NC
    NPHYS = inputs["cache_k"].shape[1]
    wsh = {n: list(inputs[n].shape) for n in WNAMES}
    key = (L, T, NS, NPG, NPHYS, tuple(stages))
    if key not in _cache:
        _cache[key] = build(L, T, NS, NPG, NPHYS, wsh, stages)[0]
    nc = _cache[key]
    cn = make_consts(NPG)
    f32 = lambda a: np.ascontiguousarray(a, dtype=np.float32)
    ck = f32(inputs["cache_k"]).reshape(L, NPHYS, 128 * 256)
    cv = f32(inputs["cache_v"]).reshape(L, NPHYS, 128 * 256)
    wts = {n: f32(inputs[n]) for n in WNAMES}
    cns = {"c_" + n: f32(v) for n, v in cn.items()}
    in_maps = []
    for c in range(NC):
        sl = slice(c * NS, (c + 1) * NS)
        m = {
            "xp": f32(inputs["x_prompt"][c]),
            "xs": f32(inputs["x_sample"][sl, 0]),
            "cache_k": ck, "cache_v": cv,
            "swkv": f32(inputs["state_wkv"][:, sl]).reshape(L, NS * 8, 4096),
            "sshift": f32(inputs["state_shift"][:, sl]),
            "ptab": np.ascontiguousarray(inputs["page_table"][sl]).astype(np.int32).reshape(NS * NPG, 1),
            "pp": f32(inputs["p_prompt"][:, c]),
            "psm": f32(inputs["p_sample"][:, sl, 0]),
        }
        m.update(wts)
        m.update(cns)
        in_maps.append(m)
    res = run_bass_kernel_spmd(nc, in_maps, core_ids=list(range(NC)), **({"trace": True} if trace else {}))
    R = res.results
    cat = lambda k, ax: np.concatenate([np.asarray(r[k]) for r in R], axis=ax)
    stk = lambda k, ax: np.stack([np.asarray(r[k]) for r in R], axis=ax)
    y_p = stk("y_p", 0)
    y_s = cat("y_s", 0)[:, None, :]
    k_p = stk("k_p", 1).reshape(L, NC, T, 4, 64)
    v_p = stk("v_p", 1).reshape(L, NC, T, 4, 64)
    wkv_p = stk("wkv_p", 1)
    shift_p = stk("shift_p", 1)
    k_s = cat("k_s", 1).reshape(L, NS_total, 1, 4, 64)
    v_s = cat("v_s", 1).reshape(L, NS_total, 1, 4, 64)
    wkv_s = cat("wkv_s", 1).reshape(L, NS_total, 8, 64, 64)
    shift_s = cat("shift_s", 1)
    av_s = cat("av_s", 1)[:, :, None, :]
    out = (y_p, y_s, k_p, v_p, wkv_p, shift_p, k_s, v_s, wkv_s, shift_s, av_s)
    return tuple(np.ascontiguousarray(o, dtype=np.float32) for o in out), res


def kernel(**inputs):
    inputs = {k: np.asarray(v) for k, v in inputs.items()}
    L = inputs["w_in"].shape[0]
    T = inputs["x_prompt"].shape[1]
    NS_total = inputs["x_sample"].shape[0]
    NPG = inputs["page_table"].shape[1]
    out, _ = run(inputs, L, T, NS_total, NPG)
    return out
```

```python
import numpy as np
import concourse.bass as bass
import concourse.mybir as mybir

F32 = mybir.dt.float32
BF16 = mybir.dt.bfloat16
I32 = mybir.dt.int32
AF = mybir.ActivationFunctionType
ALU = mybir.AluOpType
AX = mybir.AxisListType


class _Rec:
    __slots__ = ("lw", "rd")

    def __init__(self):
        self.lw = None
        self.rd = {}


class _Sem:
    def __init__(self, h, name):
        self.h = h
        self.name = name
        self.n = 0


class K:
    __slots__ = ("ap", "key")

    def __init__(self, ap, key):
        self.ap = ap
        self.key = key


def _ap(x):
    return x.ap if isinstance(x, K) else x


def _key(x):
    if isinstance(x, K):
        return x.key
    return x.tensor.name


class FW:
    def __init__(self, nc, n_dma_sems=16):
        self.nc = nc
        self.recs = {}
        self.engs = {}
        for nm, e in (("pe", nc.tensor), ("act", nc.scalar), ("dve", nc.vector),
                      ("pool", nc.gpsimd), ("sp", nc.sync)):
            s = _Sem(nc.alloc_semaphore("s_" + nm), nm)
            self.engs[nm] = (e, s)
        self.waited = {nm: {} for nm in self.engs}
        self.dsems = {}
        self.dnext = {}
        self.sems = {s.name: s for (_, s) in self.engs.values()}
        for q in ("sp", "pool", "act"):
            self.dsems[q] = [_Sem(nc.alloc_semaphore("d%s%d" % (q, i)), "d%s%d" % (q, i)) for i in range(n_dma_sems)]
            self.dnext[q] = 0
            for s in self.dsems[q]:
                self.sems[s.name] = s
        self.n_inst = 0
        self.psum_names = set()

    def rec(self, key):
        r = self.recs.get(key)
        if r is None:
            r = self.recs[key] = _Rec()
        return r

    def _need(self, en, writes, reads):
        need = {}

        def add(sv, same_ok):
            if sv is None:
                return
            sn, v = sv
            if sn == en and same_ok:
                return
            if need.get(sn, 0) < v:
                need[sn] = v

        pe = en == "pe"
        for x in reads:
            r = self.rec(_key(x))
            add(r.lw, pe)
            if _ap(x).tensor.name in self.psum_names:
                for sn, v in r.rd.items():
                    if sn != en:
                        add((sn, v), False)
        for x in writes:
            r = self.rec(_key(x))
            add(r.lw, pe)
            for sn, v in r.rd.items():
                add((sn, v), pe)
        return need

    def _emit_waits(self, en, need):
        e, _ = self.engs[en]
        w = self.waited[en]
        for sn, v in need.items():
            if w.get(sn, 0) >= v:
                continue
            e.wait_ge(self.sems[sn].h, v)
            w[sn] = v

    def op(self, en, fn, writes, reads, inc=True):
        need = self._need(en, writes, reads)
        self._emit_waits(en, need)
        e, s = self.engs[en]
        ins = fn(e)
        if inc:
            s.n += 1
            ins.then_inc(s.h, 1)
            cid = s.n
        else:
            cid = s.n + 1
        self.n_inst += 1
        for x in reads:
            self.rec(_key(x)).rd[en] = cid
        for x in writes:
            r = self.rec(_key(x))
            r.lw = (en, cid)
            r.rd = {}
        return ins

    def dma(self, out, in_, q="sp", fn=None, extra_reads=()):
        writes = [out]
        reads = [in_] + list(extra_reads)
        need = self._need("dma", writes, reads)
        ds = self.dsems[q][self.dnext[q]]
        self.dnext[q] = (self.dnext[q] + 1) % len(self.dsems[q])
        if ds.n > 0:
            need[ds.name] = max(need.get(ds.name, 0), ds.n)
        self._emit_waits(q, need)
        e, _ = self.engs[q]
        if fn is None:
            ins = e.dma_start(out=_ap(out), in_=_ap(in_))
        else:
            ins = fn(e)
        ds.n += 16
        ins.then_inc(ds.h, 16)
        self.n_inst += 1
        for x in reads:
            self.rec(_key(x)).rd[ds.name] = ds.n
        r = self.rec(_key(out))
        r.lw = (ds.name, ds.n)
        r.rd = {}
        return ins

    def barrier(self):
        vals = {s.name: s.n for s in self.sems.values() if s.n > 0}
        for en in self.engs:
            self._emit_waits(en, dict(vals))

    def finish(self):
        vals = {s.name: s.n for s in self.sems.values() if s.n > 0}
        self._emit_waits("sp", vals)

    def mm(self, out, lhsT, rhs, start=True, stop=True, **kw):
        return self.op("pe", lambda e: e.matmul(_ap(out), _ap(lhsT), _ap(rhs), start=start, stop=stop, **kw),
                       [out], [lhsT, rhs] + ([] if start else [out]), inc=bool(stop))

    def transpose(self, out, in_, ident):
        return self.op("pe", lambda e: e.transpose(_ap(out), _ap(in_), _ap(ident)), [out], [in_, ident])

    def act(self, out, in_, func, bias=None, scale=1.0, en="act", accum_out=None):
        reads = [in_]
        kw = {}
        if bias is not None:
            if not isinstance(bias, (int, float)):
                reads.append(bias)
                kw["bias"] = _ap(bias)
            else:
                kw["bias"] = float(bias)
        if not isinstance(scale, (int, float)):
            reads.append(scale)
            kw["scale"] = _ap(scale)
        else:
            kw["scale"] = float(scale)
        writes = [out]
        if accum_out is not None:
            writes.append(accum_out)
            kw["accum_out"] = _ap(accum_out)
        return self.op(en, lambda e: e.activation(_ap(out), _ap(in_), func, **kw), writes, reads)

    def tt(self, out, a, b, op, en="dve"):
        return self.op(en, lambda e: e.tensor_tensor(_ap(out), _ap(a), _ap(b), op), [out], [a, b])

    def ts(self, out, a, s1, op0, s2=None, op1=None, en="dve", accum_out=None):
        reads = [a]
        v1 = s1
        v2 = s2
        if not isinstance(s1, (int, float)):
            reads.append(s1)
            v1 = _ap(s1)
        if s2 is not None and not isinstance(s2, (int, float)):
            reads.append(s2)
            v2 = _ap(s2)
        kw = {}
        writes = [out]
        if accum_out is not None:
            writes.append(accum_out)
            kw["accum_out"] = _ap(accum_out)
        if op1 is None:
            return self.op(en, lambda e: e.tensor_scalar(_ap(out), _ap(a), v1, None, op0, **kw), writes, reads)
        return self.op(en, lambda e: e.tensor_scalar(_ap(out), _ap(a), v1, v2, op0, op1, **kw), writes, reads)

    def stt(self, out, a, s, b, op0, op1, en="dve"):
        reads = [a, b]
        sv = s
        if not isinstance(s, (int, float)):
            reads.append(s)
            sv = _ap(s)
        return self.op(en, lambda e: e.scalar_tensor_tensor(_ap(out), _ap(a), sv, _ap(b), op0, op1), [out], reads)

    def copy(self, out, in_, en="dve"):
        if en == "act":
            return self.op(en, lambda e: e.copy(_ap(out), _ap(in_)), [out], [in_])
        return self.op(en, lambda e: e.tensor_copy(_ap(out), _ap(in_)), [out], [in_])

    def reduce(self, out, in_, op=None, axis=None, en="dve"):
        op = op or ALU.add
        axis = axis or AX.X
        return self.op(en, lambda e: e.tensor_reduce(_ap(out), _ap(in_), axis, op), [out], [in_])

    def memset(self, out, val, en="pool"):
        return self.op(en, lambda e: e.memset(_ap(out), val), [out], [])


import math
import os
import numpy as np

D = 1024
HD = 64
DFF = 2816
SB_SCALE = HD ** -0.5
EXPM05 = math.exp(-0.5)


def make_consts(NPG):
    c = {}
    i = np.arange(128)
    c["ident"] = np.eye(128, dtype=np.float32)
    c["ones"] = np.ones((128, 128), np.float32)
    c["ntincl"] = -(i[:, None] >= i[None, :]).astype(np.float32)
    ms = (i[:, None] < i[None, :]).astype(np.float32)
    c["mstrict"] = np.tile(ms, (1, 4))
    c["triljt"] = (i[:, None] <= i[None, :]).astype(np.float32)
    s = np.arange(64)
    su = (s[:, None] < s[None, :]).astype(np.float32)
    iu = (s[:, None] <= s[None, :]).astype(np.float32)
    c["maskA"] = np.tile(np.concatenate([-su, iu], 1)[:, None, :], (1, 8, 1)).reshape(64, 1024)
    c["maskB"] = np.tile(np.concatenate([su, iu], 1)[:, None, :], (1, 8, 1)).reshape(64, 1024)
    sl = -(s[None, :] < s[:, None]).astype(np.float32)
    c["maskC"] = np.tile(sl[:, None, :], (1, 8, 1)).reshape(64, 512)
    c["eye8"] = np.tile(np.eye(64, dtype=np.float32)[:, None, :], (1, 8, 1)).reshape(64, 512)
    c["triblk"] = ((i[:, None] // 64 == i[None, :] // 64) & (i[:, None] <= i[None, :])).astype(np.float32)
    c["chind"] = (i[:, None] // 64 == np.arange(2)[None, :]).astype(np.float32)
    c["shiftA"] = (i[None, :] == i[:, None] + 1).astype(np.float32)
    sb = np.zeros((128, 128), np.float32)
    sb[127, 0] = 1.0
    c["shiftB"] = sb
    G = min(128, 16 * NPG)
    g = np.arange(G)
    c["pgsuf"] = ((g[:, None] // NPG == g[None, :] // NPG) & (g[:, None] > g[None, :])).astype(np.float32)
    SG = G // NPG
    c["seqind"] = (g[:, None] // NPG == np.arange(SG)[None, :]).astype(np.float32)
    return c


WNAMES = ["mix_norm", "w_in", "a_ln_g", "a_ln_b", "a_ws", "a_bs", "b_mu", "b_w0", "b_w2", "b_a0", "b_a2", "b_g2",
          "b_kk", "b_ka", "b_rk", "b_ln_g", "b_ln_b", "c_qn", "c_kn", "c_bias", "w_out", "ffn_norm", "w_gate",
          "w_up", "w_down", "ple_norm", "w_ple_gate", "w_ple"]


def build(L, T, NS, NPG, NPHYS, wshapes, stages=("A", "B", "C", "F", "P")):
    nc = bass.Bass("TRN2", target_bir_lowering=False)
    NTK = T // 128
    NT = T + NS
    G = min(128, NS * NPG)
    SG = G // NPG
    NG = NS // SG

    def din(name, shape, dt=F32):
        return nc.dram_tensor(name, list(shape), dt, kind="ExternalInput").ap()

    def dout(name, shape):
        return nc.dram_tensor(name, list(shape), F32, kind="ExternalOutput").ap()

    def dscr(name, shape, dt=F32):
        return nc.dram_tensor(name, list(shape), dt, kind="Internal").ap()

    I = {}
    I["xp"] = din("xp", [T, D])
    I["xs"] = din("xs", [NS, D])
    I["cache_k"] = din("cache_k", [L, NPHYS, 128 * 256])
    I["cache_v"] = din("cache_v", [L, NPHYS, 128 * 256])
    I["swkv"] = din("swkv", [L, NS * 8, 4096])
    I["sshift"] = din("sshift", [L, NS, 1792])
    I["ptab"] = din("ptab", [NS * NPG, 1], I32)
    I["pp"] = din("pp", [L, T, 256])
    I["psm"] = din("psm", [L, NS, 256])
    W = {n: din(n, wshapes[n]) for n in WNAMES}
    CN = make_consts(NPG)
    C = {n: din("c_" + n, CN[n].shape) for n in CN}

    O = {}
    O["y_p"] = dout("y_p", [T, D])
    O["y_s"] = dout("y_s", [NS, D])
    O["k_p"] = dout("k_p", [L, T, 256])
    O["v_p"] = dout("v_p", [L, T, 256])
    O["wkv_p"] = dout("wkv_p", [L, 8, 64, 64])
    O["shift_p"] = dout("shift_p", [L, 1792])
    O["k_s"] = dout("k_s", [L, NS, 256])
    O["v_s"] = dout("v_s", [L, NS, 256])
    O["wkv_s"] = dout("wkv_s", [L, NS * 8, 4096])
    O["shift_s"] = dout("shift_s", [L, NS, 1792])
    O["av_s"] = dout("av_s", [L, NS, 256])

    scr_q = dscr("scr_q", [NS, 256])
    scr_b = dscr("scr_b", [NS, 8, 8, 64])
    scr_y = dscr("scr_y", [NS * 8, 64])
    scr_f = dscr("scr_f", [NS, 2, 512])

    f = FW(nc)
    f.marks = []
    def mark(lbl):
        f.marks.append((lbl, f.sems['pe'].n))
    import contextlib
    es = contextlib.ExitStack()
    es.enter_context(nc.allow_non_contiguous_dma(reason="small parameter loads"))

    def sb(name, shape, dt=F32):
        return nc.alloc_sbuf_tensor(name, list(shape), dt).ap()

    banks = [nc.alloc_psum_tensor("bank%d" % i, [128, 512], F32).ap() for i in range(7)]
    ptb_ = nc.alloc_psum_tensor("ptb", [128, 1024], BF16).ap()

    def bankb(i):
        assert i == 7
        return ptb_

    f.psum_names = {b.tensor.name for b in banks} | {ptb_.tensor.name}

    ident_f = sb("ident_f", [128, 128]); f.dma(ident_f, C["ident"])
    ident_b = sb("ident_b", [128, 128], BF16); f.dma(ident_b, C["ident"], q="pool")
    ones_b = sb("ones_b", [128, 128], BF16); f.dma(ones_b, C["ones"], q="pool")
    negones_b = sb("negones_b", [128, 128], BF16)
    ntincl_b = sb("ntincl_b", [128, 128], BF16); f.dma(ntincl_b, C["ntincl"], q="pool")
    mstrict_b = sb("mstrict_b", [128, 512], BF16); f.dma(mstrict_b, C["mstrict"], q="pool")
    triljt = sb("triljt", [128, 128]); f.dma(triljt, C["triljt"])
    maskA = sb("maskA", [64, 1024]); f.dma(maskA, C["maskA"])
    maskB = sb("maskB", [64, 1024]); f.dma(maskB, C["maskB"])
    maskC = sb("maskC", [64, 512]); f.dma(maskC, C["maskC"])
    eye8 = sb("eye8", [64, 512]); f.dma(eye8, C["eye8"])
    triblk = sb("triblk", [128, 128]); f.dma(triblk, C["triblk"])
    chind = sb("chind", [128, 2]); f.dma(chind, C["chind"])
    shiftA = sb("shiftA", [128, 128], BF16); f.dma(shiftA, C["shiftA"], q="pool")
    shiftB = sb("shiftB", [128, 128], BF16); f.dma(shiftB, C["shiftB"], q="pool")
    pgsuf = sb("pgsuf", [G, G]); f.dma(pgsuf, C["pgsuf"])
    seqind = sb("seqind", [G, SG]); f.dma(seqind, C["seqind"])
    f.ts(negones_b, ones_b, -1.0, ALU.mult)
    onesf = sb("onesf", [128, 128]); f.dma(onesf, C["ones"])
    eps_rms = sb("eps_rms", [128, 1]); f.memset(eps_rms, 1e-6)
    eps_ln = sb("eps_ln", [128, 1]); f.memset(eps_ln, 1e-5)
    eps_gn = sb("eps_gn", [128, 1]); f.memset(eps_gn, 64e-5)
    eps_kk = sb("eps_kk", [128, 1]); f.memset(eps_kk, 1e-12)
    one_c = sb("one_c", [128, 1]); f.memset(one_c, 1.0)
    zero_c = sb("zero_c", [128, 1]); f.memset(zero_c, 0.0)

    hT = sb("hT", [128, 8, NT])
    class _Holder:
        pass
    hn = _Holder()
    hn.cm = None

    def hn_alloc(tag):
        hn.cm = nc.sbuf_tensor("hnT_" + tag, [128, 8, NT], BF16)
        hn.t = hn.cm.__enter__().ap()

    def hn_free():
        f.barrier()
        hn.cm.__exit__(None, None, None)
        hn.cm = None
        f.recs.clear()

    pb_scr = dscr("pb_scr", [NT, 1792])

    def rsqrt_(out, in_, eps_t, scale, np_):
        f.act(out, in_, AF.Ln, bias=eps_t[0:np_, :], scale=scale)
        f.act(out, out, AF.Exp, scale=-0.5)

    def sigmoid_(out, in_, np_, scale=1.0, tmp=None):
        f.act(out, in_, AF.Exp, scale=-scale)
        f.ts(out, out, 1.0, ALU.add)
        f.op("dve", lambda e: e.reciprocal(_ap(out), _ap(out)), [out], [out])

    tblocks = []
    c0 = 0
    while c0 < NT:
        n = min(512, NT - c0)
        tblocks.append((c0, n))
        c0 += n

    xt_cm = nc.sbuf_tensor("xt", [128, D], F32)
    xt = xt_cm.__enter__().ap()
    for i in range(NTK + 1):
        n = 128 if i < NTK else NS
        src = I["xp"][i * 128:(i + 1) * 128, :] if i < NTK else I["xs"]
        f.dma(xt[0:n, :], src)
        for half in range(2):
            pst = banks[half]
            for c in range(4):
                m = half * 4 + c
                f.transpose(pst[:, c * 128:c * 128 + n], xt[0:n, m * 128:(m + 1) * 128], ident_f[0:n, 0:n])
            f.copy(hT[:, half * 4:(half + 1) * 4, i * 128:i * 128 + n],
                   pst.rearrange("p (c t) -> p c t", c=4)[:, :, 0:n], en="act" if half else "dve")
    f.barrier()
    xt_cm.__exit__(None, None, None)
    f.recs.clear()

    def rmsnorm_all(gname, l):
        tag = "_%s_%d" % (gname, l)
        with nc.sbuf_tensor("gvec" + tag, [128, 8], F32) as gvec_h, nc.sbuf_tensor("sqb" + tag, [128, 8, 512], BF16) as sqb_h, \
                nc.sbuf_tensor("rstd_t" + tag, [128, 512], F32) as rstd_h:
            _rmsnorm_body(gname, l, gvec_h.ap(), sqb_h.ap(), rstd_h.ap())
            f.barrier()
        f.recs.clear()

    def _rmsnorm_body(gname, l, gvec, sqb, rstd_t):
        f.dma(gvec, W[gname][l].rearrange("(c p) -> p c", p=128))
        for (c0, n) in tblocks:
            f.act(sqb[:, :, 0:n], hT[:, :, c0:c0 + n], AF.Square)
            ps = banks[0]
            for k in range(8):
                f.mm(ps[:, 0:n], ones_b, sqb[:, k, 0:n], start=(k == 0), stop=(k == 7))
            rsqrt_(rstd_t[:, 0:n], ps[:, 0:n], eps_rms, 1.0 / D, 128)
            for k in range(8):
                f.stt(hn.t[:, k, c0:c0 + n], hT[:, k, c0:c0 + n], gvec[:, k:k + 1], rstd_t[:, 0:n],
                      ALU.mult, ALU.mult)

    def bcload(dst, row_ap, q="sp"):
        P = dst.shape[0]
        src = row_ap.unsqueeze(0).broadcast_to([P] + list(row_ap.shape))
        f.dma(dst, src, q=q)

    def proj_T(ps, ntok, col0, wt, wc0, wc1):
        for k in range(8):
            f.mm(ps[0:ntok, 0:wc1 - wc0], hn.t[:, k, col0:col0 + ntok], wt[:, k, wc0:wc1], start=(k == 0), stop=(k == 7))

    def add_h(m, c0, n, ps):
        f.tt(hT[:, m, c0:c0 + n], hT[:, m, c0:c0 + n], ps, ALU.add)

    for l in range(L):
        mark('L%d norm' % l)
        hn_alloc('m%d' % l)
        rmsnorm_all("mix_norm", l)
        mark('L%d pass1-prompt' % l)
        f.barrier()
        with contextlib.ExitStack() as p1:
            def sb1(name, shape, dt=F32):
                return p1.enter_context(nc.sbuf_tensor(name + "_%d" % l, list(shape), dt)).ap()
            p1a = contextlib.ExitStack()
            p1b = contextlib.ExitStack()

            def sb1a(name, shape, dt=F32):
                return p1a.enter_context(nc.sbuf_tensor(name + "_%d" % l, list(shape), dt)).ap()

            def sb1b(name, shape, dt=F32):
                return p1b.enter_context(nc.sbuf_tensor(name + "_%d" % l, list(shape), dt)).ap()
            wAC = sb1("wAC", [128, 8, 1280], BF16)
            win = W["w_in"][l].rearrange("(k p) n -> p k n", p=128)
            f.dma(wAC[:, :, 0:512], win[:, :, 0:512], q="pool")
            f.dma(wAC[:, :, 512:1280], win[:, :, 2304:3072], q="pool")
            woA = sb1("woA", [128, 2, D], BF16)
            f.dma(woA, W["w_out"][l][0:256, :].rearrange("(k p) n -> p k n", p=128), q="pool")
            lng = sb1("lng", [128, 256]); bcload(lng, W["a_ln_g"][l])
            lnb = sb1("lnb", [128, 256]); bcload(lnb, W["a_ln_b"][l])
            bs_t = sb1("bs_t", [128, 4]); f.dma(bs_t, W["a_bs"][l].rearrange("g i -> i g"))
            ws0 = sb1("ws0", [NS, 4]); f.dma(ws0, W["a_ws"][l][:, 0, 0:1].rearrange("g o -> o g").broadcast_to([NS, 4]))
            bs0 = sb1("bs0", [NS, 4]); f.dma(bs0, W["a_bs"][l][:, 0:1].rearrange("g o -> o g").broadcast_to([NS, 4]))
            gqk = sb1("gqk", [128, 512])
            f.dma(gqk[:, 0:256].rearrange("p (h d) -> p h d", h=4),
                  W["c_qn"][l].unsqueeze(0).unsqueeze(0).broadcast_to([128, 4, 64]))
            f.dma(gqk[:, 256:512].rearrange("p (h d) -> p h d", h=4),
                  W["c_kn"][l].unsqueeze(0).unsqueeze(0).broadcast_to([128, 4, 64]))
            f.ts(gqk[:, 0:256], gqk[:, 0:256], SB_SCALE, ALU.mult)
            cb1 = sb1("cb1", [1, 4]); f.dma(cb1, W["c_bias"][l].unsqueeze(0))
            biasrow = sb1("biasrow", [1, 4, 128], BF16)
            f.copy(biasrow, cb1.unsqueeze(2).broadcast_to([1, 4, 128]))
            cbG = sb1("cbG", [G, 4]); bcload(cbG, W["c_bias"][l])

            u_sb = sb1("u_sb", [128, 256])
            vc = sb1("vc", [128, 256])
            st1 = sb1("st1", [128, 8])
            st2 = sb1("st2", [128, 8])
            ya = sb1("ya", [128, 256])
            yab = sb1("yab", [128, 256], BF16)
            sq1 = sb1("sq1", [128, 512])
            qkn = sb1("qkn", [128, 512])
            qkb = sb1("qkb", [128, 512], BF16)
            vf = sb1("vf", [128, 256])
            woC = sb1a("woC", [64, 4, D], BF16)
            WmT = sb1a("WmT", [128, 4, 128], BF16)
            wsl = sb1a("wsl", [128, 4, 128])
            KT = sb1a("KT", [128, 2, T], BF16)
            Vall = sb1a("Vall", [128, NTK, 256], BF16)
            QT = sb1a("QT", [128, 2, 128], BF16)
            vnb = sb1a("vnb", [128, 256], BF16)
            yaT = sb1a("yaT", [128, 2, 128], BF16)
            e_t2 = [sb1a("e_t%d" % j, [128, 512], BF16) for j in range(2)]
            sp_all = sb1a("sp_all", [128, NTK, 512], BF16)
            att_t2 = [sb1a("att_t%d" % j, [128, 512], BF16) for j in range(2)]
            Cs2 = [sb1a("Cs%d" % j, [128, 512], BF16) for j in range(2)]
            ycT = sb1a("ycT", [64, 512], BF16)
            f.dma(woC, W["w_out"][l][768:1024, :].rearrange("(h p) n -> p h n", p=64), q="pool")
            f.dma(wsl, W["a_ws"][l].rearrange("g i j -> i g j"))
            for g in range(4):
                f.transpose(banks[0][:, g * 128:(g + 1) * 128], wsl[:, g, :], ident_f)
            f.tt(WmT, banks[0].rearrange("p (g i) -> p g i", g=4), triljt.unsqueeze(1).broadcast_to([128, 4, 128]), ALU.mult)

            def mixer_A(n, col0, sample):
                ps = banks[0]
                BIS3 = int(os.environ.get("BIS3", "99"))
                proj_T(ps, n, col0, wAC, 0, 512)
                if BIS3 < 1: return
                f.copy(u_sb[0:n], ps[0:n, 0:256], en="act")
                f.reduce(st1[0:n, 0:1], ps[0:n, 256:512])
                f.ts(st1[0:n, 0:1], st1[0:n, 0:1], -1.0 / 256, ALU.mult)
                if BIS3 < 2: return
                f.ts(vc[0:n], ps[0:n, 256:512], st1[0:n, 0:1], ALU.add)
                f.tt(ya[0:n], vc[0:n], vc[0:n], ALU.mult)
                f.reduce(st1[0:n, 1:2], ya[0:n])
                if BIS3 < 3: return
                rsqrt_(st1[0:n, 1:2], st1[0:n, 1:2], eps_ln, 1.0 / 256, n)
                f.ts(vc[0:n], vc[0:n], st1[0:n, 1:2], ALU.mult)
                if BIS3 < 4: return
                f.tt(vc[0:n], vc[0:n], lng[0:n], ALU.mult)
                f.tt(vc[0:n], vc[0:n], lnb[0:n], ALU.add)
                if BIS3 < 5: return
                if sample:
                    f.dma(O["av_s"][l], vc[0:n])
                    v3 = vc[0:n].rearrange("p (g d) -> p g d", g=4)
                    f.tt(ya[0:n].rearrange("p (g d) -> p g d", g=4), v3, ws0.unsqueeze(2).broadcast_to([NS, 4, 64]), ALU.mult)
                    f.tt(ya[0:n].rearrange("p (g d) -> p g d", g=4), ya[0:n].rearrange("p (g d) -> p g d", g=4),
                         bs0.unsqueeze(2).broadcast_to([NS, 4, 64]), ALU.add)
                    f.tt(yab[0:n], ya[0:n], u_sb[0:n], ALU.mult)
                else:
                    f.copy(vnb, vc, en="pool")
                    if BIS3 < 6: return
                    pm = banks[1]
                    for g in range(4):
                        f.mm(pm[:, g * 64:(g + 1) * 64], WmT[:, g, :], vnb[:, g * 64:(g + 1) * 64])
                    if BIS3 < 7: return
                    f.tt(ya.rearrange("p (g d) -> p g d", g=4), pm[:, 0:256].rearrange("p (g d) -> p g d", g=4),
                         bs_t.unsqueeze(2).broadcast_to([128, 4, 64]), ALU.add)
                    f.tt(yab, ya, u_sb, ALU.mult)

            def qkv(n, col0):
                pq = banks[2]
                pv = banks[3]
                proj_T(pq, n, col0, wAC, 512, 1024)
                proj_T(pv, n, col0, wAC, 1024, 1280)
                f.act(sq1[0:n], pq[0:n], AF.Square)
                f.reduce(st2[0:n], sq1[0:n].rearrange("p (h d) -> p h d", h=8))
                rsqrt_(st2[0:n], st2[0:n], eps_rms, 1.0 / 64, n)
                f.tt(qkn[0:n].rearrange("p (h d) -> p h d", h=8), pq[0:n].rearrange("p (h d) -> p h d", h=8),
                     st2[0:n].unsqueeze(2).broadcast_to([n, 8, 64]), ALU.mult)
                f.tt(qkn[0:n], qkn[0:n], gqk[0:n], ALU.mult, en="pool")
                f.copy(qkb[0:n], qkn[0:n], en="pool")
                f.copy(vf[0:n], pv[0:n, 0:256], en="act")

            BIS = int(os.environ.get("BIS", "9"))
            for i in range(NTK if (("A" in stages or "C" in stages) and BIS >= 2) else 0):
                t0 = i * 128
                col0 = t0
                BIS2 = int(os.environ.get("BIS2", "9"))
                if "A" in stages:
                    mixer_A(128, col0, False)
                    if BIS2 >= 2:
                        pt = bankb(7)
                        for k in range(2):
                            f.transpose(pt[:, k * 128:(k + 1) * 128], yab[:, k * 128:(k + 1) * 128], ident_b)
                        f.copy(yaT, pt[:, 0:256].rearrange("p (k t) -> p k t", k=2), en="act")
                    else:
                        f.memset(yaT, 0.0)
                if BIS2 < 3:
                    continue
                if "C" in stages:
                    qkv(128, col0)
                    f.dma(O["k_p"][l, t0:t0 + 128, :], qkn[:, 256:512])
                    f.dma(O["v_p"][l, t0:t0 + 128, :], vf)
                    f.copy(Vall[:, i, :], vf, en="pool")
                    pt = bankb(7)
                    for k in range(2):
                        f.transpose(pt[:, 256 + k * 128:256 + (k + 1) * 128], qkb[:, k * 128:(k + 1) * 128], ident_b)
                        f.transpose(pt[:, 512 + k * 128:512 + (k + 1) * 128], qkb[:, 256 + k * 128:256 + (k + 1) * 128], ident_b)
                    f.copy(QT, pt[:, 256:512].rearrange("p (k t) -> p k t", k=2), en="act")
                    f.copy(KT[:, :, t0:t0 + 128], pt[:, 512:768].rearrange("p (k t) -> p k t", k=2), en="dve")
                    pO = banks[6]

                    def qk_into(ps, kb, last):
                        for h in range(4):
                            r0 = (h % 2) * 64
                            f.mm(ps[:, h * 128:(h + 1) * 128], KT[r0:r0 + 64, h // 2, kb * 128:(kb + 1) * 128],
                                 QT[r0:r0 + 64, h // 2, :], start=(h == 0), stop=False, skip_group_check=True)
                            f.mm(ps[:, h * 128:(h + 1) * 128], ones_b[0:1, :], biasrow[0:1, h, :], start=False,
                                 stop=(last and h == 3), skip_group_check=True)

                    def spk(kb):
                        return K(sp_all[:, kb, :], "sp_all%d" % kb)

                    for n1, kb in enumerate(range(i, -1, -1)):
                        pS = banks[4 + n1 % 2]
                        et = e_t2[n1 % 2]
                        qk_into(pS, kb, True)
                        f.act(et, pS, AF.Exp)
                        f.act(spk(kb), et, AF.Ln, bias=one_c)
                        if kb == i:
                            f.tt(spk(kb), spk(kb), mstrict_b, ALU.mult)
                    pend = None
                    for n2, kb in enumerate(range(i, -1, -1)):
                        pE = banks[4 + n2 % 2]
                        at = att_t2[n2 % 2]
                        cs_cur, cs_nxt = Cs2[n2 % 2], Cs2[(n2 + 1) % 2]
                        qk_into(pE, kb, False)
                        f.mm(pE, ntincl_b, spk(kb), start=False, stop=(kb == i), skip_group_check=True)
                        if kb < i:
                            f.mm(pE, negones_b, cs_cur, start=False, stop=True, skip_group_check=True)
                        if pend is not None:
                            pend()
                        f.act(at, pE, AF.Exp)
                        if kb == i:
                            f.tt(at, at, mstrict_b, ALU.mult)
                        if kb > 0:
                            if kb == i:
                                f.copy(cs_nxt, spk(kb), en="pool")
                            else:
                                f.tt(cs_nxt, cs_cur, spk(kb), ALU.add, en="pool")

                        def av(kb=kb, at=at):
                            for h in range(4):
                                f.mm(pO[0:64, h * 128:(h + 1) * 128], Vall[:, kb, h * 64:(h + 1) * 64],
                                     at[:, h * 128:(h + 1) * 128], start=(kb == i and h == 0),
                                     stop=(h == 3), skip_group_check=True)
                        pend = av
                    pend()
                    f.copy(ycT, pO[0:64, :], en="act")
                for half in range(2):
                    ph = banks[half]
                    for mm_ in range(4):
                        m = half * 4 + mm_
                        ops = []
                        if "A" in stages:
                            ops += [(woA[:, k, m * 128:(m + 1) * 128], yaT[:, k, :]) for k in range(2)]
                        if "C" in stages:
                            ops += [(woC[:, h, m * 128:(m + 1) * 128], ycT[:, h * 128:(h + 1) * 128]) for h in range(4)]
                        for j, (a, b) in enumerate(ops):
                            f.mm(ph[:, mm_ * 128:(mm_ + 1) * 128], a, b, start=(j == 0), stop=(j == len(ops) - 1))
                    f.tt(hT[:, half * 4:(half + 1) * 4, t0:t0 + 128], hT[:, half * 4:(half + 1) * 4, t0:t0 + 128],
                         ph.rearrange("p (c t) -> p c t", c=4), ALU.add)

            f.barrier()
            p1a.close()
            f.recs.clear()
            mark('L%d pass1-sample' % l)
            col0 = T
            yaT_s = sb1b("yaT_s", [128, 2, NS], BF16)
            ycT_s = sb1b("ycT_s", [128, 2, NS], BF16)
            f.memset(yaT_s, 0.0)
            f.memset(ycT_s, 0.0)
            if "A" in stages and BIS >= 3:
                mixer_A(NS, col0, True)
                pt = bankb(7)
                for k in range(2):
                    f.transpose(pt[:, k * 128:k * 128 + NS], yab[0:NS, k * 128:(k + 1) * 128], ident_b[0:NS, 0:NS])
                f.copy(yaT_s, pt[:, 0:256].rearrange("p (k t) -> p k t", k=2)[:, :, 0:NS], en="act")
            if "C" in stages:
                qkv(NS, col0)
                f.dma(O["k_s"][l], qkn[0:NS, 256:512])
                f.dma(O["v_s"][l], vf[0:NS])
                f.dma(scr_q, qkn[0:NS, 0:256])
                PSL = 8
                NSL = 128 // PSL
                qrep = sb1b("qrep", [G, 256])
                idx = sb1b("idx", [G, 1], I32)
                idx8 = sb1b("idx8", [G, 1], I32)
                kv_t = [sb1b("kv%d" % j, [G, PSL * 256]) for j in range(2)]
                prod = sb1b("prod", [G, PSL * 256])
                s_all = sb1b("s_all", [G, 128, 4])
                e_d = sb1b("e_d", [G, 128, 4])
                sp_d = sb1b("sp_d", [G, 128, 4])
                cum_d = sb1b("cum_d", [G, 128, 4])
                tot_d = sb1b("tot_d", [G, 4])
                R_d = sb1b("R_d", [G, 4])
                att_d = sb1b("att_d", [G, 128, 4])
                yacc = sb1b("yacc", [G, 256])
                ypart = sb1b("ypart", [G, 256])
                for g in range(NG):
                    f.dma(idx, I["ptab"][g * G:(g + 1) * G, :])
                    f.ts(idx8, idx, NSL, ALU.mult)
                    f.dma(qrep, scr_q[g * SG:(g + 1) * SG, :].unsqueeze(1).broadcast_to([SG, NPG, 256]))
                    for sl in range(NSL):
                        kt = kv_t[sl % 2]
                        src = bass.AP(I["cache_k"].tensor, 0, [[PSL * 256, NPHYS * NSL], [1, PSL * 256]])
                        eo = l * NPHYS * 32768 + sl * PSL * 256
                        f.dma(kt, I["cache_k"], q="pool", extra_reads=[idx8],
                              fn=lambda e, kt=kt, src=src, eo=eo: e.indirect_dma_start(
                                  out=kt, out_offset=None, in_=src,
                                  in_offset=bass.IndirectOffsetOnAxis(ap=idx8[:, :], axis=0), element_offset=eo))
                        f.tt(prod.rearrange("p (s c) -> p s c", s=PSL), kt.rearrange("p (s c) -> p s c", s=PSL),
                             qrep.unsqueeze(1).broadcast_to([G, PSL, 256]), ALU.mult)
                        f.reduce(s_all[:, sl * PSL:(sl + 1) * PSL, :], prod.rearrange("p (s h d) -> p s h d", s=PSL, h=4))
                    f.tt(s_all, s_all, cbG.unsqueeze(1).broadcast_to([G, 128, 4]), ALU.add)
                    f.act(e_d, s_all, AF.Exp)
                    f.act(sp_d, e_d, AF.Ln, bias=one_c[0:G])
                    f.copy(cum_d[:, 127:128, :], sp_d[:, 127:128, :])
                    cur, nxt = sp_d, cum_d
                    sh = 1
                    bufs = [cum_d, e_d]
                    bi = 0
                    src_t = sp_d
                    while sh < 128:
                        dst_t = bufs[bi]
                        f.tt(dst_t[:, 0:128 - sh, :], src_t[:, 0:128 - sh, :], src_t[:, sh:128, :], ALU.add)
                        f.copy(dst_t[:, 128 - sh:128, :], src_t[:, 128 - sh:128, :], en="pool")
                        src_t = dst_t
                        bi ^= 1
                        sh *= 2
                    cumI = src_t
                    pR = banks[4]
                    f.copy(tot_d, cumI[:, 0, :])
                    f.mm(pR[0:G, 0:4], pgsuf, tot_d)
                    f.copy(R_d, pR[0:G, 0:4], en="act")
                    f.tt(att_d, s_all, cumI, ALU.subtract)
                    f.tt(att_d, att_d, R_d.unsqueeze(1).broadcast_to([G, 128, 4]), ALU.subtract)
                    f.act(att_d, att_d, AF.Exp)
                    for sl in range(NSL):
                        vt = kv_t[sl % 2]
                        src = bass.AP(I["cache_v"].tensor, 0, [[PSL * 256, NPHYS * NSL], [1, PSL * 256]])
                        eo = l * NPHYS * 32768 + sl * PSL * 256
                        f.dma(vt, I["cache_v"], q="pool", extra_reads=[idx8],
                              fn=lambda e, vt=vt, src=src, eo=eo: e.indirect_dma_start(
                                  out=vt, out_offset=None, in_=src,
                                  in_offset=bass.IndirectOffsetOnAxis(ap=idx8[:, :], axis=0), element_offset=eo))
                        f.tt(prod.rearrange("p (s h d) -> p s h d", s=PSL, h=4),
                             vt.rearrange("p (s h d) -> p s h d", s=PSL, h=4),
                             att_d[:, sl * PSL:(sl + 1) * PSL, :].unsqueeze(3).broadcast_to([G, PSL, 4, 64]), ALU.mult)
                        dst = yacc if sl == 0 else ypart
                        f.reduce(dst, prod.rearrange("p (s c) -> p c s", s=PSL))
                        if sl > 0:
                            f.tt(yacc, yacc, ypart, ALU.add, en="pool")
                    pY = banks[5]
                    for k in range(2):
                        f.mm(pY[:, k * SG:(k + 1) * SG], yacc[:, k * 128:(k + 1) * 128], seqind)
                    f.copy(ycT_s[:, :, g * SG:(g + 1) * SG], pY[:, 0:2 * SG].rearrange("p (k s) -> p k s", k=2), en="act")
            woC2 = sb1b("woC2", [128, 2, D], BF16)
            f.dma(woC2, W["w_out"][l][768:1024, :].rearrange("(k p) n -> p k n", p=128), q="pool")
            ph = banks[0]
            for m in range(8):
                ops = []
                if "A" in stages:
                    ops += [(woA[:, k, m * 128:(m + 1) * 128], yaT_s[:, k, :]) for k in range(2)]
                if "C" in stages:
                    ops += [(woC2[:, k, m * 128:(m + 1) * 128], ycT_s[:, k, :]) for k in range(2)]
                for j, (a, b) in enumerate(ops):
                    f.mm(ph[:, m * NS:(m + 1) * NS], a, b, start=(j == 0), stop=(j == len(ops) - 1))
            if "A" in stages or "C" in stages:
                f.tt(hT[:, :, T:T + NS], hT[:, :, T:T + NS], ph[:, 0:8 * NS].rearrange("p (c t) -> p c t", c=8), ALU.add)
            f.barrier()
            p1b.close()
        f.recs.clear()

        mark('L%d pass15' % l)
        if "B" in stages:
            with contextlib.ExitStack() as p15:
                wBf = p15.enter_context(nc.sbuf_tensor("wBf_%d" % l, [128, 8, 1792], BF16)).ap()
                stg = [p15.enter_context(nc.sbuf_tensor("stg%d_%d" % (j, l), [128, 1792], F32)).ap() for j in range(2)]
                f.dma(wBf, W["w_in"][l].rearrange("(k p) n -> p k n", p=128)[:, :, 512:2304], q="pool")
                cnt15 = 0
                for i in range(NTK + 1):
                    n = 128 if i < NTK else NS
                    st_ = stg[i % 2]
                    for (c0, ncol) in [(0, 512), (512, 512), (1024, 512), (1536, 256)]:
                        ps = banks[cnt15 % 4]
                        proj_T(ps, n, i * 128, wBf, c0, c0 + ncol)
                        f.copy(st_[0:n, c0:c0 + ncol], ps[0:n, 0:ncol], en="act" if cnt15 % 2 else "dve")
                        cnt15 += 1
                    f.dma(pb_scr[i * 128:i * 128 + n, :], st_[0:n, :])
                f.dma(O["shift_p"][l].unsqueeze(0), pb_scr[T - 1:T, :])
                f.dma(O["shift_s"][l], pb_scr[T:T + NS, :])
                f.barrier()
            f.recs.clear()
        hn_free()
        mark('L%d pass2' % l)
        if "B" in stages:
            with contextlib.ExitStack() as p2:
                def sb2(name, shape, dt=F32):
                    return p2.enter_context(nc.sbuf_tensor(name + "_%d" % l, list(shape), dt)).ap()
                _rwkv_pass(nc, f, l, L, T, NS, NTK, W, I, O, sb2, banks, bankb, hT, pb_scr, scr_b, scr_y, scr_f,
                           dict(ident_f=ident_f, ident_b=ident_b, maskA=maskA, maskB=maskB, maskC=maskC, eye8=eye8,
                                triblk=triblk, chind=chind, shiftA=shiftA, shiftB=shiftB, eps_gn=eps_gn, eps_kk=eps_kk,
                                one_c=one_c, onesf=onesf), rsqrt_, sigmoid_, bcload, proj_T)
                f.barrier()
            f.recs.clear()

        mark('L%d ffn' % l)
        if "F" in stages:
            hn_alloc('f%d' % l)
            rmsnorm_all("ffn_norm", l)
            with contextlib.ExitStack() as p3:
                def sb3(name, shape, dt=F32):
                    return p3.enter_context(nc.sbuf_tensor(name + "_%d" % l, list(shape), dt)).ap()
                wg = [sb3("wg%d" % j, [128, 8, 512], BF16) for j in range(2)]
                wu = [sb3("wu%d" % j, [128, 8, 512], BF16) for j in range(2)]
                wd = [sb3("wd%d" % j, [128, 4, D], BF16) for j in range(2)]
                silu_t = sb3("silu_t", [128, 512])
                actT = [sb3("actT%d" % j, [128, 4, 512], BF16) for j in range(2)]
                nblk = (DFF + 511) // 512
                gsrc = W["w_gate"][l].rearrange("(k p) n -> p k n", p=128)
                usrc = W["w_up"][l].rearrange("(k p) n -> p k n", p=128)
                steps = []
                for b in range(nblk):
                    for (c0, n) in tblocks:
                        steps.append((b, c0, n))
                loaded = set()

                def load_block(b):
                    if b in loaded or b >= nblk:
                        return
                    loaded.add(b)
                    c0f = b * 512
                    nf = min(512, DFF - c0f)
                    j = b % 2
                    f.dma(wg[j][:, :, 0:nf], gsrc[:, :, c0f:c0f + nf], q="pool")
                    f.dma(wu[j][:, :, 0:nf], usrc[:, :, c0f:c0f + nf], q="pool")
                    f.dma(wd[j][:, 0:nf // 128, :], W["w_down"][l][c0f:c0f + nf, :].rearrange("(c p) n -> p c n", p=128), q="pool")

                def gateup(si):
                    b, c0, n = steps[si]
                    load_block(b)
                    j = b % 2
                    ncf = min(512, DFF - b * 512) // 128
                    aT = actT[si % 2]
                    for c in range(ncf):
                        pg = banks[(2 * c) % 4]
                        pu = banks[(2 * c + 1) % 4]
                        for k in range(8):
                            f.mm(pg[:, 0:n], wg[j][:, k, c * 128:(c + 1) * 128], hn.t[:, k, c0:c0 + n],
                                 start=(k == 0), stop=(k == 7))
                        for k in range(8):
                            f.mm(pu[:, 0:n], wu[j][:, k, c * 128:(c + 1) * 128], hn.t[:, k, c0:c0 + n],
                                 start=(k == 0), stop=(k == 7))
                        f.act(silu_t[:, 0:n], pg[:, 0:n], AF.Silu)
                        f.tt(aT[:, c, 0:n], silu_t[:, 0:n], pu[:, 0:n], ALU.mult)

                def down(si):
                    b, c0, n = steps[si]
                    j = b % 2
                    ncf = min(512, DFF - b * 512) // 128
                    aT = actT[si % 2]
                    for m in range(8):
                        pd = banks[4 + m % 3]
                        for c in range(ncf):
                            f.mm(pd[:, 0:n], wd[j][:, c, m * 128:(m + 1) * 128], aT[:, c, 0:n],
                                 start=(c == 0), stop=(c == ncf - 1))
                        add_h(m, c0, n, pd[:, 0:n])

                for si in range(len(steps) + 1):
                    if si < len(steps):
                        gateup(si)
                    if si > 0:
                        down(si - 1)
                f.barrier()
            f.recs.clear()
            hn_free()

        mark('L%d ple' % l)
        if "P" in stages:
            hn_alloc('p%d' % l)
            rmsnorm_all("ple_norm", l)
            with contextlib.ExitStack() as p4:
                def sb4(name, shape, dt=F32):
                    return p4.enter_context(nc.sbuf_tensor(name + "_%d" % l, list(shape), dt)).ap()
                wpg = sb4("wpg", [128, 8, D], BF16)
                wpl = sb4("wpl", [128, 2, D], BF16)
                f.dma(wpg, W["w_ple_gate"][l].rearrange("(k p) n -> p k n", p=128), q="pool")
                f.dma(wpl, W["w_ple"][l].rearrange("(k p) n -> p k n", p=128), q="pool")
                peT = sb4("peT", [128, 2, NT], BF16)
                pet = [sb4("pet%d" % j, [128, 256]) for j in range(2)]
                for i in range(NTK + 1):
                    n = 128 if i < NTK else NS
                    src = I["pp"][l, i * 128:(i + 1) * 128, :] if i < NTK else I["psm"][l]
                    pe_ = pet[i % 2]
                    f.dma(pe_[0:n], src)
                    pt = banks[i % 2]
                    for k in range(2):
                        f.transpose(pt[:, k * 128:k * 128 + n], pe_[0:n, k * 128:(k + 1) * 128], ident_f[0:n, 0:n])
                    f.copy(peT[:, :, i * 128:i * 128 + n], pt[:, 0:256].rearrange("p (k t) -> p k t", k=2)[:, :, 0:n],
                           en="act" if i % 2 else "dve")
                gate_t = [sb4("gate_t%d" % j, [128, 512]) for j in range(2)]
                cnt = 0
                for (c0, n) in tblocks:
                    for m in range(8):
                        pa = banks[2 + (cnt % 2) * 2]
                        pb_ = banks[3 + (cnt % 2) * 2]
                        gt = gate_t[cnt % 2]
                        cnt += 1
                        for k in range(8):
                            f.mm(pa[:, 0:n], wpg[:, k, m * 128:(m + 1) * 128], hn.t[:, k, c0:c0 + n],
                                 start=(k == 0), stop=(k == 7))
                        for k in range(2):
                            f.mm(pb_[:, 0:n], wpl[:, k, m * 128:(m + 1) * 128], peT[:, k, c0:c0 + n],
                                 start=(k == 0), stop=(k == 1))
                        f.act(gt[:, 0:n], pa[:, 0:n], AF.Sigmoid)
                        f.tt(gt[:, 0:n], gt[:, 0:n], pb_[:, 0:n], ALU.mult)
                        f.tt(hT[:, m, c0:c0 + n], hT[:, m, c0:c0 + n], gt[:, 0:n], ALU.add, en="pool")
                f.barrier()
            f.recs.clear()
            hn_free()

    mark('final')
    yt = [sb("yt%d" % j, [128, D]) for j in range(2)]
    for i in range(NTK + 1):
        n = 128 if i < NTK else NS
        y_ = yt[i % 2]
        for half in range(2):
            pst = banks[(i % 2) * 2 + half]
            for c in range(4):
                m = half * 4 + c
                f.transpose(pst[0:n, c * 128:(c + 1) * 128], hT[:, m, i * 128:i * 128 + n], ident_f)
            f.copy(y_[0:n, half * 512:(half + 1) * 512], pst[0:n, :], en="act" if half else "dve")
        dst = O["y_p"][i * 128:(i + 1) * 128, :] if i < NTK else O["y_s"]
        f.dma(dst, y_[0:n])
    f.finish()
    es.close()
    return nc, f


def _rwkv_pass(nc, f, l, L, T, NS, NTK, W, I, O, sb2_outer, banks, bankb, hT, pb_scr, scr_b, scr_y, scr_f, CT, rsqrt_, sigmoid_,
               bcload, proj_T):
    import contextlib
    CH = BF16
    NH = 8
    CW = NH * 64
    ident_f, ident_b = CT["ident_f"], CT["ident_b"]
    maskA, maskB, maskC, eye8 = CT["maskA"], CT["maskB"], CT["maskC"], CT["eye8"]
    triblk, chind, shiftA, shiftB = CT["triblk"], CT["chind"], CT["shiftA"], CT["shiftB"]
    eps_gn, eps_kk, one_c = CT["eps_gn"], CT["eps_kk"], CT["one_c"]
    ptb = bankb(7)
    win = W["w_in"][l].rearrange("(k p) n -> p k n", p=128)

    def v3(ap, h=NH):
        return ap.rearrange("p (h d) -> p h d", h=h)

    def bc3(ap, n, h=NH):
        return ap.unsqueeze(2).broadcast_to([n, h, 64])

    for hg in range(1):
      with contextlib.ExitStack() as ph_:
        def sb2(name, shape, dt=F32):
            return ph_.enter_context(nc.sbuf_tensor(name + "_%d_%d" % (l, hg), list(shape), dt)).ap()
        blocks = [(0, 512), (512, 512), (1024, 512), (1536, 256)]
        mu = sb2("mu", [128, 1792])
        bcload(mu, W["b_mu"][l])
        woB = sb2("woB", [128, 4, D], BF16)
        f.dma(woB, W["w_out"][l][256:768, :].rearrange("(k p) n -> p k n", p=128), q="pool")
        cs = slice(0, CW)
        w2 = sb2("w2", [64, CW], BF16); f.dma(w2, W["b_w2"][l][:, cs], q="pool")
        a2 = sb2("a2", [64, CW], BF16); f.dma(a2, W["b_a2"][l][:, cs], q="pool")
        g2 = sb2("g2", [128, CW], BF16); f.dma(g2, W["b_g2"][l][:, cs], q="pool")
        w0 = sb2("w0", [128, CW]); bcload(w0, W["b_w0"][l][cs])
        a0 = sb2("a0", [128, CW]); bcload(a0, W["b_a0"][l][cs])
        kkp = sb2("kkp", [128, CW]); bcload(kkp, W["b_kk"][l][cs])
        kap = sb2("kap", [128, CW]); bcload(kap, W["b_ka"][l][cs])
        rkp = sb2("rkp", [128, CW]); bcload(rkp, W["b_rk"][l].rearrange("h d -> (h d)")[cs])
        lng = sb2("blng", [128, CW]); bcload(lng, W["b_ln_g"][l][cs])
        lnb = sb2("blnb", [128, CW]); bcload(lnb, W["b_ln_b"][l][cs])

        ST = sb2("ST", [64, NH, 64]); f.memset(ST, 0.0)
        STb = sb2("STb", [64, NH, 64], BF16); f.memset(STb, 0.0)
        xs_bufs = [sb2("xs%d" % j, [128, 1792]) for j in range(1)]
        cur_b = [sb2("cur_b%d" % j, [128, 1792], BF16) for j in range(2)]
        f.memset(cur_b[1], 0.0)
        lb = sb2("lb", [128, 128], BF16)
        lb2 = sb2("lb2", [128, 128], BF16)
        lT = sb2("lT", [128, 3, 128], BF16)
        logw = sb2("logw", [128, CW])
        a_t = sb2("a_t", [128, CW])
        PKf = sb2("PKf", [128, 2, CW])
        PKf1 = sb2("PKf1", [64, 2, CW])
        PKb = sb2("PKb", [128, 3, CW], BF16)
        PKb1 = sb2("PKb1", [64, 3, CW], BF16)
        kk = sb2("kk", [128, CW])
        kmod = sb2("kmod", [128, CW])
        t1 = sb2("t1", [128, CW])
        t2 = sb2("t2", [128, CW])
        st8 = sb2("st8", [128, NH])
        st8b = sb2("st8b", [128, NH])
        eL = sb2("eL", [128, CW])
        enL = sb2("enL", [128, CW])
        fm_b = [sb2("fm_b%d" % j, [128, CW], BF16) for j in range(4)]
        QRT = sb2("QRT", [64, NH, 2, 2, 64], BF16)
        BKT = sb2("BKT", [64, NH, 2, 2, 64], BF16)
        GC = sb2("GC", [64, NH, 2])
        G1s = [sb2("G1_%d" % c, [64, NH, 128], BF16) for c in range(2)]
        G2s = [sb2("G2_%d" % c, [64, NH, 128], BF16) for c in range(2)]
        XAs = [[sb2("XA%d_%d" % (j, c), [64, NH, 64], CH) for j in range(2)] for c in range(2)]
        XTs = [[sb2("XT%d_%d" % (j, c), [64, NH, 64], CH) for j in range(2)] for c in range(2)]
        Pms = [[sb2("Pm%d_%d" % (j, c), [64, NH, 64], CH) for j in range(2)] for c in range(2)]
        TTs = [None, None]
        Wn = sb2("Wn", [64, NH, 64], BF16)
        U = sb2("U", [64, NH, 64], BF16)
        yc_t = sb2("yc_t", [64, CW])
        y2 = sb2("y2", [64, CW])
        ybb = sb2("ybb", [64, CW], BF16)
        ybT = sb2("ybT", [128, 4, 128], BF16)
        ss = sb2("ss", [NS, CW])

        def prep(n, xs):
            f.act(t1[0:n, 0:64], xs[0:n, 3 * CW:3 * CW + 64], AF.Exp, scale=-2.0)
            f.ts(t1[0:n, 0:64], t1[0:n, 0:64], 1.0, ALU.add)
            f.op("dve", lambda e: e.reciprocal(t1[0:n, 0:64], t1[0:n, 0:64]), [t1], [t1])
            f.ts(lb[0:n, 0:64], t1[0:n, 0:64], 2.0, ALU.mult, -1.0, ALU.add)
            f.copy(lb[0:n, 64:128], xs[0:n, 3 * CW + 64:3 * CW + 128], en="pool")
            sigmoid_(t1[0:n, 128:256], xs[0:n, 3 * CW + 128:3 * CW + 256], n)
            f.copy(lb2[0:n], t1[0:n, 128:256], en="pool")
            f.transpose(ptb[0:64, 0:n], lb[0:n, 0:64], ident_b[0:n, 0:n])
            f.transpose(ptb[0:64, 128:128 + n], lb[0:n, 64:128], ident_b[0:n, 0:n])
            f.transpose(ptb[:, 256:256 + n], lb2[0:n, :], ident_b[0:n, 0:n])
            f.copy(lT[0:64, 0:2, 0:n], ptb[0:64, 0:256].rearrange("p (k t) -> p k t", k=2)[:, :, 0:n], en="act")
            f.copy(lT[:, 2, 0:n], ptb[:, 256:256 + n], en="act")
            ps_w, ps_a, ps_g = banks[2], banks[3], banks[4]
            f.mm(ps_w[0:n, 0:CW], lT[0:64, 0, 0:n], w2)
            f.mm(ps_a[0:n, 0:CW], lT[0:64, 1, 0:n], a2)
            f.mm(ps_g[0:n, 0:CW], lT[:, 2, 0:n], g2)
            f.tt(t1[0:n], ps_w[0:n, 0:CW], w0[0:n], ALU.add)
            sigmoid_(t1[0:n], t1[0:n], n)
            f.ts(logw[0:n], t1[0:n], -EXPM05, ALU.mult)
            f.tt(t2[0:n], ps_a[0:n, 0:CW], a0[0:n], ALU.add)
            sigmoid_(a_t[0:n], t2[0:n], n)
            f.copy(PKf[0:n, 1, :], ps_g[0:n, 0:CW], en="act")
            r_ = xs[0:n, 0:CW]
            k_ = xs[0:n, CW:2 * CW]
            v_ = xs[0:n, 2 * CW:3 * CW]
            f.copy(PKb[0:n, 0, :], v_, en="pool")
            f.tt(kk[0:n], k_, kkp[0:n], ALU.mult)
            f.tt(t1[0:n], kk[0:n], kk[0:n], ALU.mult)
            f.reduce(st8[0:n], v3(t1[0:n]))
            rsqrt_(st8[0:n], st8[0:n], eps_kk, 1.0, n)
            f.tt(v3(kk[0:n]), v3(kk[0:n]), bc3(st8[0:n], n), ALU.mult)
            f.ts(t1[0:n], a_t[0:n], -1.0, ALU.add)
            f.tt(t1[0:n], t1[0:n], kap[0:n], ALU.mult)
            f.ts(t1[0:n], t1[0:n], 1.0, ALU.add)
            f.tt(kmod[0:n], k_, t1[0:n], ALU.mult)
            f.tt(t1[0:n], r_, kmod[0:n], ALU.mult)
            f.tt(t1[0:n], t1[0:n], rkp[0:n], ALU.mult)
            f.reduce(st8b[0:n], v3(t1[0:n]))
            f.tt(v3(PKf[0:n, 0, :]), v3(v_), bc3(st8b[0:n], n), ALU.mult)

        def shift_mix(n, xs, c0, ncol, prev):
            f.tt(t2[0:n, 0:ncol], prev, xs[0:n, c0:c0 + ncol], ALU.subtract)
            f.tt(t2[0:n, 0:ncol], t2[0:n, 0:ncol], mu[0:n, c0:c0 + ncol], ALU.mult, en="pool")
            f.tt(xs[0:n, c0:c0 + ncol], t2[0:n, 0:ncol], xs[0:n, c0:c0 + ncol], ALU.add, en="pool")

        def group_out(n, y_ap, bonus_ap, g_ap, out_bf):
            f.reduce(st8[0:n], v3(y_ap))
            f.ts(st8[0:n], st8[0:n], -1.0 / 64, ALU.mult)
            f.tt(v3(yc_t[0:n]), v3(y_ap), bc3(st8[0:n], n), ALU.add)
            f.tt(y2[0:n], yc_t[0:n], yc_t[0:n], ALU.mult)
            f.reduce(st8b[0:n], v3(y2[0:n]))
            rsqrt_(st8b[0:n], st8b[0:n], eps_gn, 1.0 / 64, n)
            f.tt(v3(yc_t[0:n]), v3(yc_t[0:n]), bc3(st8b[0:n], n), ALU.mult)
            f.tt(yc_t[0:n], yc_t[0:n], lng[0:n], ALU.mult)
            f.tt(yc_t[0:n], yc_t[0:n], lnb[0:n], ALU.add)
            f.tt(yc_t[0:n], yc_t[0:n], bonus_ap, ALU.add)
            f.tt(out_bf, yc_t[0:n], g_ap, ALU.mult)

        f.dma(xs_bufs[0], pb_scr[0:128, :])
        for i in range(NTK):
            t0 = i * 128
            cb = cur_b[i % 2]
            pvb = cur_b[(i + 1) % 2]
            xs = xs_bufs[0]
            f.copy(cb[:, 0:1024], xs[:, 0:1024], en="pool")
            f.copy(cb[:, 1024:1792], xs[:, 1024:1792], en="act")
            for j, (c0, ncol) in enumerate(blocks):
                pp = banks[j % 2]
                f.mm(pp[:, 0:ncol], shiftA, cb[:, c0:c0 + ncol], start=True, stop=False)
                f.mm(pp[:, 0:ncol], shiftB, pvb[:, c0:c0 + ncol], start=False, stop=True)
                shift_mix(128, xs, c0, ncol, pp[:, 0:ncol])
            prep(128, xs)
            psL = banks[5]
            f.mm(psL[:, 0:CW], triblk, logw)
            f.act(eL, psL[:, 0:CW], AF.Exp)
            f.act(enL, psL[:, 0:CW], AF.Exp, scale=-1.0)
            f.tt(fm_b[1], xs[:, 0:CW], eL, ALU.mult)
            f.tt(t1, psL[:, 0:CW], logw, ALU.subtract)
            f.act(eL, t1, AF.Exp)
            f.tt(fm_b[0], kk, eL, ALU.mult)
            f.tt(t2, kk, a_t, ALU.mult, en="pool")
            f.tt(fm_b[2], t2, enL, ALU.mult)
            f.tt(fm_b[3], kmod, enL, ALU.mult)
            if i + 1 < NTK:
                f.dma(xs_bufs[0], pb_scr[t0 + 128:t0 + 256, :])
            f.copy(PKb[:, 1, :], fm_b[2], en="pool")
            f.copy(PKb[:, 2, :], fm_b[3], en="pool")
            for which, (src, dst, slot) in enumerate([(fm_b[0], QRT, 0), (fm_b[1], QRT, 1), (fm_b[2], BKT, 0), (fm_b[3], BKT, 1)]):
                for h in range(NH):
                    f.transpose(ptb[0:64, h * 128:(h + 1) * 128], src[:, h * 64:(h + 1) * 64], ident_b)
                f.copy(dst[:, :, :, slot, :], ptb[0:64, 0:NH * 128].rearrange("p (q c t) -> p q c t", q=NH, c=2),
                       en="act" if which % 2 else "dve")
            psG = banks[6]
            for h in range(NH):
                f.mm(psG[0:64, h * 2:(h + 1) * 2], logw[:, h * 64:(h + 1) * 64], chind)
            f.act(GC, psG[0:64, 0:2 * NH].rearrange("p (q c) -> p q c", q=NH), AF.Exp)
            f.dma(PKb1, PKb[64:128])
            f.dma(PKf1, PKf[64:128])

            def stage1(c):
                G1, G2 = G1s[c], G2s[c]
                for h in range(NH):
                    qr = QRT[:, h, c, :, :].rearrange("p w t -> p (w t)")
                    bk, cb_ = h // 4, (h % 4) * 128
                    f.mm(banks[0 + bk][0:64, cb_:cb_ + 128], BKT[:, h, c, 0, :], qr)
                    f.mm(banks[2 + bk][0:64, cb_:cb_ + 128], BKT[:, h, c, 1, :], qr)
                    f.mm(banks[4][0:64, h * 64:(h + 1) * 64], QRT[:, h, c, 0, :], BKT[:, h, c, 0, :])
                yield
                for bk in range(2):
                    f.tt(G1[:, bk * 4:(bk + 1) * 4, :].rearrange("p h t -> p (h t)"), banks[0 + bk][0:64, :],
                         maskA[:, bk * 512:(bk + 1) * 512], ALU.mult, en="dve")
                    f.tt(G2[:, bk * 4:(bk + 1) * 4, :].rearrange("p h t -> p (h t)"), banks[2 + bk][0:64, :],
                         maskB[:, bk * 512:(bk + 1) * 512], ALU.mult, en="dve")
                XA, XT, Pm = XAs[c], XTs[c], Pms[c]
                A_, AT_, P_ = XA[0], XT[0], Pm[0]
                f.copy(A_, G1[:, :, 0:64], en="pool")
                f.tt(AT_.rearrange("p h t -> p (h t)"), banks[4][0:64, 0:CW], maskC[:, 0:CW], ALU.mult)
                f.tt(P_.rearrange("p h t -> p (h t)"), A_.rearrange("p h t -> p (h t)"), eye8[:, 0:CW], ALU.add)
                yield
                psA, psAT, psP = (banks[5], banks[6], banks[4]) if c == 0 else (banks[0], banks[1], banks[2])
                pi = 0
                for k in range(1, 6):
                    An, ATn, Pn = XA[k % 2], XT[k % 2], Pm[(pi + 1) % 2]
                    for h in range(NH):
                        if k < 5:
                            f.mm(psA[0:64, h * 64:(h + 1) * 64], AT_[:, h, :], A_[:, h, :])
                        f.mm(psAT[0:64, h * 64:(h + 1) * 64], A_[:, h, :], AT_[:, h, :])
                    yield
                    if k < 5:
                        f.copy(An.rearrange("p h t -> p (h t)"), psA[0:64, 0:CW], en="act")
                    f.copy(ATn.rearrange("p h t -> p (h t)"), psAT[0:64, 0:CW], en="dve")
                    yield
                    for h in range(NH):
                        f.mm(psP[0:64, h * 64:(h + 1) * 64], ATn[:, h, :], P_[:, h, :])
                    yield
                    f.tt(Pn.rearrange("p h t -> p (h t)"), psP[0:64, 0:CW], P_.rearrange("p h t -> p (h t)"), ALU.add)
                    yield
                    A_, AT_, P_ = An, ATn, Pn
                    pi += 1
                TTs[c] = P_

            def stage2(c):
                Vc = PKb[0:64, 0, :] if c == 0 else PKb1[:, 0, :]
                Bc = PKb[0:64, 1, :] if c == 0 else PKb1[:, 1, :]
                Kc = PKb[0:64, 2, :] if c == 0 else PKb1[:, 2, :]
                bon = PKf[0:64, 0, :] if c == 0 else PKf1[:, 0, :]
                gg = PKf[0:64, 1, :] if c == 0 else PKf1[:, 1, :]
                G1, G2, TT = G1s[c], G2s[c], TTs[c]
                psW, psU, psY, psS = banks[3], banks[4], banks[5], banks[6]
                for h in range(NH):
                    f.mm(psW[0:64, h * 64:(h + 1) * 64], QRT[:, h, c, 0, :], STb[:, h, :], start=True, stop=False)
                    f.mm(psW[0:64, h * 64:(h + 1) * 64], G2[:, h, 0:64], Vc[:, h * 64:(h + 1) * 64], start=False, stop=True)
                f.ts(Wn.rearrange("p h t -> p (h t)"), psW[0:64, 0:CW], -1.0, ALU.mult)
                for h in range(NH):
                    f.mm(psU[0:64, h * 64:(h + 1) * 64], TT[:, h, :], Wn[:, h, :])
                f.copy(U.rearrange("p h t -> p (h t)"), psU[0:64, 0:CW], en="act")
                for h in range(NH):
                    o_ = psY[0:64, h * 64:(h + 1) * 64]
                    f.mm(o_, QRT[:, h, c, 1, :], STb[:, h, :], start=True, stop=False)
                    f.mm(o_, G1[:, h, 64:128], U[:, h, :], start=False, stop=False)
                    f.mm(o_, G2[:, h, 64:128], Vc[:, h * 64:(h + 1) * 64], start=False, stop=True)
                for h in range(NH):
                    o_ = psS[0:64, h * 64:(h + 1) * 64]
                    f.mm(o_, Bc[:, h * 64:(h + 1) * 64], U[:, h, :], start=True, stop=False)
                    f.mm(o_, Kc[:, h * 64:(h + 1) * 64], Vc[:, h * 64:(h + 1) * 64], start=False, stop=True)
                f.tt(ST, ST, psS[0:64, 0:CW].rearrange("p (h d) -> p h d", h=NH), ALU.add)
                f.tt(ST, ST, GC[:, :, c].unsqueeze(2).broadcast_to([64, NH, 64]), ALU.mult)
                f.copy(STb, ST, en="pool")
                group_out(64, psY[0:64, 0:CW], bon, gg, ybb)
                for k in range(4):
                    f.transpose(ptb[:, k * 64:(k + 1) * 64], ybb[:, k * 128:(k + 1) * 128], ident_b[0:64, 0:64])
                f.copy(ybT[:, :, c * 64:(c + 1) * 64], ptb[:, 0:256].rearrange("p (k t) -> p k t", k=4), en="act")

            gens = [stage1(0), stage1(1)]
            alive = [True, True]
            next(gens[0])
            next(gens[0])
            while any(alive):
                for gi in (1, 0):
                    g = gens[gi]
                    if alive[gi]:
                        try:
                            next(g)
                        except StopIteration:
                            alive[gi] = False
            stage2(0)
            stage2(1)
            for half in range(2):
                ph = banks[5 + half]
                for mm_ in range(4):
                    m = half * 4 + mm_
                    for k in range(4):
                        f.mm(ph[:, mm_ * 128:(mm_ + 1) * 128], woB[:, k, m * 128:(m + 1) * 128], ybT[:, k, :],
                             start=(k == 0), stop=(k == 3))
                f.tt(hT[:, half * 4:(half + 1) * 4, t0:t0 + 128], hT[:, half * 4:(half + 1) * 4, t0:t0 + 128],
                     ph.rearrange("p (c t) -> p c t", c=4), ALU.add)
        for h in range(NH):
            f.transpose(banks[0][0:64, h * 64:(h + 1) * 64], ST[:, h, :], ident_f[0:64, 0:64])
        f.copy(y2, banks[0][0:64, 0:CW])
        f.dma(O["wkv_p"][l].rearrange("h i j -> i h j"), y2.rearrange("i (h j) -> i h j", h=NH))

        n = NS
        xs = xs_bufs[0]
        f.dma(xs[0:n, :], pb_scr[T:T + n, :])
        for (c0, ncol) in blocks:
            f.dma(ss[:, 0:ncol], I["sshift"][l][:, c0:c0 + ncol])
            shift_mix(n, xs, c0, ncol, ss[:, 0:ncol])
        prep(n, xs)
        f.act(t1[0:n], logw[0:n], AF.Exp)
        f.tt(t2[0:n], kk[0:n], a_t[0:n], ALU.mult)
        for q, src in enumerate([xs[0:n, 0:CW], t1[0:n], kmod[0:n], xs[0:n, 2 * CW:3 * CW], kk[0:n], t2[0:n]]):
            f.dma(scr_b[:, :, q, :], src.rearrange("s (h d) -> s h d", h=8))
        f.dma(scr_f, PKf[0:n])
        f.barrier()
      f.recs.clear()

    with contextlib.ExitStack() as ps_:
        def sb3(name, shape, dt=F32):
            return ps_.enter_context(nc.sbuf_tensor(name + "_s%d" % l, list(shape), dt)).ap()
        n = NS
        NP = NS * 8
        woBf = sb3("woBf", [128, 4, D], BF16)
        f.dma(woBf, W["w_out"][l][256:768, :].rearrange("(k p) n -> p k n", p=128), q="pool")
        lngf = sb3("lngf", [NS, 512]); bcload(lngf, W["b_ln_g"][l])
        lnbf = sb3("lnbf", [NS, 512]); bcload(lnbf, W["b_ln_b"][l])
        bg = sb3("bg", [NS, 2, 512])
        f.dma(bg, scr_f)
        pkp = sb3("pkp", [NP, 8, 64])
        f.dma(pkp[:, 0:6, :], scr_b.rearrange("s h q d -> (s h) q d")[:, 0:6, :])
        S = sb3("S", [NP, 64, 64])
        tmp = sb3("tmpS", [NP, 64, 64])
        f.dma(S.rearrange("p i j -> p (i j)"), I["swkv"][l])
        r_p, w_p, k_p, v_p, kk_p, b_p = (pkp[:, q, :] for q in range(6))

        def bj(ap):
            return ap.unsqueeze(1).broadcast_to([NP, 64, 64])

        def bi(ap):
            return ap.unsqueeze(2).broadcast_to([NP, 64, 64])

        sa = sb3("sa", [NP, 64])
        ysm = sb3("ysm", [NP, 64])
        f.tt(tmp, S, bj(kk_p), ALU.mult)
        f.reduce(sa, tmp)
        f.tt(S, S, bj(w_p), ALU.mult)
        f.tt(tmp, bi(sa), bj(b_p), ALU.mult, en="pool")
        f.tt(S, S, tmp, ALU.subtract)
        f.tt(tmp, bi(v_p), bj(k_p), ALU.mult, en="pool")
        f.tt(S, S, tmp, ALU.add)
        f.dma(O["wkv_s"][l], S.rearrange("p i j -> p (i j)"))
        f.tt(tmp, S, bj(r_p), ALU.mult)
        f.reduce(ysm, tmp)
        f.dma(scr_y, ysm)
        ytm = sb3("ytm", [NS, 512])
        f.dma(ytm, scr_y.rearrange("(s h) d -> s (h d)", h=8))
        st8 = sb3("st8", [NS, 8]); st8b = sb3("st8b", [NS, 8])
        yc_t = sb3("yc_t", [NS, 512]); y2 = sb3("y2", [NS, 512]); ybb = sb3("ybb", [NS, 512], BF16)
        v8 = lambda ap: ap.rearrange("p (h d) -> p h d", h=8)
        b8 = lambda ap: ap.unsqueeze(2).broadcast_to([n, 8, 64])
        f.reduce(st8, v8(ytm))
        f.ts(st8, st8, -1.0 / 64, ALU.mult)
        f.tt(v8(yc_t), v8(ytm), b8(st8), ALU.add)
        f.tt(y2, yc_t, yc_t, ALU.mult)
        f.reduce(st8b, v8(y2))
        rsqrt_(st8b, st8b, eps_gn, 1.0 / 64, n)
        f.tt(v8(yc_t), v8(yc_t), b8(st8b), ALU.mult)
        f.tt(yc_t, yc_t, lngf, ALU.mult)
        f.tt(yc_t, yc_t, lnbf, ALU.add)
        f.tt(yc_t, yc_t, bg[:, 0, :], ALU.add)
        f.tt(ybb, yc_t, bg[:, 1, :], ALU.mult)
        ybT_s = sb3("ybT_s", [128, 4, NS], BF16)
        for k in range(4):
            f.transpose(ptb[:, k * NS:(k + 1) * NS], ybb[:, k * 128:(k + 1) * 128], ident_b[0:n, 0:n])
        f.copy(ybT_s, ptb[:, 0:4 * NS].rearrange("p (k t) -> p k t", k=4), en="act")
        ph = banks[5]
        for m in range(8):
            for k in range(4):
                f.mm(ph[:, m * NS:(m + 1) * NS], woBf[:, k, m * 128:(m + 1) * 128], ybT_s[:, k, :], start=(k == 0), stop=(k == 3))
        f.tt(hT[:, :, T:T + NS], hT[:, :, T:T + NS], ph[:, 0:8 * NS].rearrange("p (c t) -> p c t", c=8), ALU.add)
        f.barrier()
    f.recs.clear()


from concourse.bass_utils import run_bass_kernel_spmd

_cache = {}

def run(inputs, L, T, NS_total, NPG, stages=("A", "B", "C", "F", "P"), trace=False):
    NC = 8
    NS = NS_total // NC
    NPHYS = inputs["cache_k"].shape[1]
    wsh = {n: list(inputs[n].shape) for n in WNAMES}
    key = (L, T, NS, NPG, NPHYS, tuple(stages))
    if key not in _cache:
        _cache[key] = build(L, T, NS, NPG, NPHYS, wsh, stages)[0]
    nc = _cache[key]
    cn = make_consts(NPG)
    f32 = lambda a: np.ascontiguousarray(a, dtype=np.float32)
    ck = f32(inputs["cache_k"]).reshape(L, NPHYS, 128 * 256)
    cv = f32(inputs["cache_v"]).reshape(L, NPHYS, 128 * 256)
    wts = {n: f32(inputs[n]) for n in WNAMES}
    cns = {"c_" + n: f32(v) for n, v in cn.items()}
    in_maps = []
    for c in range(NC):
        sl = slice(c * NS, (c + 1) * NS)
        m = {
            "xp": f32(inputs["x_prompt"][c]),
            "xs": f32(inputs["x_sample"][sl, 0]),
            "cache_k": ck, "cache_v": cv,
            "swkv": f32(inputs["state_wkv"][:, sl]).reshape(L, NS * 8, 4096),
            "sshift": f32(inputs["state_shift"][:, sl]),
            "ptab": np.ascontiguousarray(inputs["page_table"][sl]).astype(np.int32).reshape(NS * NPG, 1),
            "pp": f32(inputs["p_prompt"][:, c]),
            "psm": f32(inputs["p_sample"][:, sl, 0]),
        }
        m.update(wts)
        m.update(cns)
        in_maps.append(m)
    res = run_bass_kernel_spmd(nc, in_maps, core_ids=list(range(NC)), **({"trace": True} if trace else {}))
    R = res.results
    cat = lambda k, ax: np.concatenate([np.asarray(r[k]) for r in R], axis=ax)
    stk = lambda k, ax: np.stack([np.asarray(r[k]) for r in R], axis=ax)
    y_p = stk("y_p", 0)
    y_s = cat("y_s", 0)[:, None, :]
    k_p = stk("k_p", 1).reshape(L, NC, T, 4, 64)
    v_p = stk("v_p", 1).reshape(L, NC, T, 4, 64)
    wkv_p = stk("wkv_p", 1)
    shift_p = stk("shift_p", 1)
    k_s = cat("k_s", 1).reshape(L, NS_total, 1, 4, 64)
    v_s = cat("v_s", 1).reshape(L, NS_total, 1, 4, 64)
    wkv_s = cat("wkv_s", 1).reshape(L, NS_total, 8, 64, 64)
    shift_s = cat("shift_s", 1)
    av_s = cat("av_s", 1)[:, :, None, :]
    out = (y_p, y_s, k_p, v_p, wkv_p, shift_p, k_s, v_s, wkv_s, shift_s, av_s)
    return tuple(np.ascontiguousarray(o, dtype=np.float32) for o in out), res


def kernel(**inputs):
    inputs = {k: np.asarray(v) for k, v in inputs.items()}
    L = inputs["w_in"].shape[0]
    T = inputs["x_prompt"].shape[1]
    NS_total = inputs["x_sample"].shape[0]
    NPG = inputs["page_table"].shape[1]
    out, _ = run(inputs, L, T, NS_total, NPG)
    return out
```

```python
import numpy as np
import concourse.bass as bass
import concourse.mybir as mybir

F32 = mybir.dt.float32
BF16 = mybir.dt.bfloat16
I32 = mybir.dt.int32
AF = mybir.ActivationFunctionType
ALU = mybir.AluOpType
AX = mybir.AxisListType


class _Rec:
    __slots__ = ("lw", "rd")

    def __init__(self):
        self.lw = None
        self.rd = {}


class _Sem:
    def __init__(self, h, name):
        self.h = h
        self.name = name
        self.n = 0


class K:
    __slots__ = ("ap", "key")

    def __init__(self, ap, key):
        self.ap = ap
        self.key = key


def _ap(x):
    return x.ap if isinstance(x, K) else x


def _key(x):
    if isinstance(x, K):
        return x.key
    return x.tensor.name


class FW:
    def __init__(self, nc, n_dma_sems=16):
        self.nc = nc
        self.recs = {}
        self.engs = {}
        for nm, e in (("pe", nc.tensor), ("act", nc.scalar), ("dve", nc.vector),
                      ("pool", nc.gpsimd), ("sp", nc.sync)):
            s = _Sem(nc.alloc_semaphore("s_" + nm), nm)
            self.engs[nm] = (e, s)
        self.waited = {nm: {} for nm in self.engs}
        self.dsems = {}
        self.dnext = {}
        self.sems = {s.name: s for (_, s) in self.engs.values()}
        for q in ("sp", "pool", "act"):
            self.dsems[q] = [_Sem(nc.alloc_semaphore("d%s%d" % (q, i)), "d%s%d" % (q, i)) for i in range(n_dma_sems)]
            self.dnext[q] = 0
            for s in self.dsems[q]:
                self.sems[s.name] = s
        self.n_inst = 0
        self.n_pe = 0
        self.psum_names = set()

    def rec(self, key):
        r = self.recs.get(key)
        if r is None:
            r = self.recs[key] = _Rec()
        return r

    def _need(self, en, writes, reads):
        need = {}

        def add(sv, same_ok):
            if sv is None:
                return
            sn, v = sv
            if sn == en and same_ok:
                return
            if need.get(sn, 0) < v:
                need[sn] = v

        pe = en == "pe"
        for x in reads:
            r = self.rec(_key(x))
            add(r.lw, pe)
            if _ap(x).tensor.name in self.psum_names:
                for sn, v in r.rd.items():
                    if sn != en:
                        add((sn, v), False)
        for x in writes:
            r = self.rec(_key(x))
            add(r.lw, pe)
            for sn, v in r.rd.items():
                add((sn, v), pe)
        return need

    def _emit_waits(self, en, need):
        e, _ = self.engs[en]
        w = self.waited[en]
        for sn, v in need.items():
            if w.get(sn, 0) >= v:
                continue
            e.wait_ge(self.sems[sn].h, v)
            w[sn] = v

    def op(self, en, fn, writes, reads, inc=True):
        need = self._need(en, writes, reads)
        self._emit_waits(en, need)
        e, s = self.engs[en]
        ins = fn(e)
        if inc:
            s.n += 1
            ins.then_inc(s.h, 1)
            cid = s.n
        else:
            cid = s.n + 1
        self.n_inst += 1
        if en == "pe":
            self.n_pe += 1
        for x in reads:
            self.rec(_key(x)).rd[en] = cid
        for x in writes:
            r = self.rec(_key(x))
            r.lw = (en, cid)
            r.rd = {}
        return ins

    def dma(self, out, in_, q="sp", fn=None, extra_reads=()):
        writes = [out]
        reads = [in_] + list(extra_reads)
        need = self._need("dma", writes, reads)
        ds = self.dsems[q][self.dnext[q]]
        self.dnext[q] = (self.dnext[q] + 1) % len(self.dsems[q])
        if ds.n > 0:
            need[ds.name] = max(need.get(ds.name, 0), ds.n)
        self._emit_waits(q, need)
        e, _ = self.engs[q]
        if fn is None:
            ins = e.dma_start(out=_ap(out), in_=_ap(in_))
        else:
            ins = fn(e)
        ds.n += 16
        ins.then_inc(ds.h, 16)
        self.n_inst += 1
        for x in reads:
            self.rec(_key(x)).rd[ds.name] = ds.n
        r = self.rec(_key(out))
        r.lw = (ds.name, ds.n)
        r.rd = {}
        return ins

    def barrier(self):
        vals = {s.name: s.n for s in self.sems.values() if s.n > 0}
        for en in self.engs:
            self._emit_waits(en, dict(vals))

    def finish(self):
        vals = {s.name: s.n for s in self.sems.values() if s.n > 0}
        self._emit_waits("sp", vals)

    def mm(self, out, lhsT, rhs, start=True, stop=True, **kw):
        return self.op("pe", lambda e: e.matmul(_ap(out), _ap(lhsT), _ap(rhs), start=start, stop=stop, **kw),
                       [out], [lhsT, rhs] + ([] if start else [out]), inc=bool(stop))

    def transpose(self, out, in_, ident):
        return self.op("pe", lambda e: e.transpose(_ap(out), _ap(in_), _ap(ident)), [out], [in_, ident])

    def act(self, out, in_, func, bias=None, scale=1.0, en="act", accum_out=None):
        reads = [in_]
        kw = {}
        if bias is not None:
            if not isinstance(bias, (int, float)):
                reads.append(bias)
                kw["bias"] = _ap(bias)
            else:
                kw["bias"] = float(bias)
        if not isinstance(scale, (int, float)):
            reads.append(scale)
            kw["scale"] = _ap(scale)
        else:
            kw["scale"] = float(scale)
        writes = [out]
        if accum_out is not None:
            writes.append(accum_out)
            kw["accum_out"] = _ap(accum_out)
        return self.op(en, lambda e: e.activation(_ap(out), _ap(in_), func, **kw), writes, reads)

    def tt(self, out, a, b, op, en="dve"):
        return self.op(en, lambda e: e.tensor_tensor(_ap(out), _ap(a), _ap(b), op), [out], [a, b])

    def ts(self, out, a, s1, op0, s2=None, op1=None, en="dve", accum_out=None):
        reads = [a]
        v1 = s1
        v2 = s2
        if not isinstance(s1, (int, float)):
            reads.append(s1)
            v1 = _ap(s1)
        if s2 is not None and not isinstance(s2, (int, float)):
            reads.append(s2)
            v2 = _ap(s2)
        kw = {}
        writes = [out]
        if accum_out is not None:
            writes.append(accum_out)
            kw["accum_out"] = _ap(accum_out)
        if op1 is None:
            return self.op(en, lambda e: e.tensor_scalar(_ap(out), _ap(a), v1, None, op0, **kw), writes, reads)
        return self.op(en, lambda e: e.tensor_scalar(_ap(out), _ap(a), v1, v2, op0, op1, **kw), writes, reads)

    def stt(self, out, a, s, b, op0, op1, en="dve"):
        reads = [a, b]
        sv = s
        if not isinstance(s, (int, float)):
            reads.append(s)
            sv = _ap(s)
        return self.op(en, lambda e: e.scalar_tensor_tensor(_ap(out), _ap(a), sv, _ap(b), op0, op1), [out], reads)

    def copy(self, out, in_, en="dve"):
        if en == "act":
            return self.op(en, lambda e: e.copy(_ap(out), _ap(in_)), [out], [in_])
        return self.op(en, lambda e: e.tensor_copy(_ap(out), _ap(in_)), [out], [in_])

    def reduce(self, out, in_, op=None, axis=None, en="dve"):
        op = op or ALU.add
        axis = axis or AX.X
        return self.op(en, lambda e: e.tensor_reduce(_ap(out), _ap(in_), axis, op), [out], [in_])

    def memset(self, out, val, en="pool"):
        return self.op(en, lambda e: e.memset(_ap(out), val), [out], [])


import math
import os
import numpy as np

D = 1024
HD = 64
DFF = 2816
SB_SCALE = HD ** -0.5
EXPM05 = math.exp(-0.5)


def make_consts(NPG):
    c = {}
    i = np.arange(128)
    c["ident"] = np.eye(128, dtype=np.float32)
    c["ones"] = np.ones((128, 128), np.float32)
    c["ntincl"] = -(i[:, None] >= i[None, :]).astype(np.float32)
    ms = (i[:, None] < i[None, :]).astype(np.float32)
    c["mstrict"] = np.tile(ms, (1, 4))
    c["triljt"] = (i[:, None] <= i[None, :]).astype(np.float32)
    s = np.arange(64)
    su = (s[:, None] < s[None, :]).astype(np.float32)
    iu = (s[:, None] <= s[None, :]).astype(np.float32)
    c["maskA"] = np.tile(np.concatenate([-su, iu], 1)[:, None, :], (1, 8, 1)).reshape(64, 1024)
    c["maskB"] = np.tile(np.concatenate([su, iu], 1)[:, None, :], (1, 8, 1)).reshape(64, 1024)
    sl = -(s[None, :] < s[:, None]).astype(np.float32)
    c["maskC"] = np.tile(sl[:, None, :], (1, 8, 1)).reshape(64, 512)
    c["eye8"] = np.tile(np.eye(64, dtype=np.float32)[:, None, :], (1, 8, 1)).reshape(64, 512)
    c["triblk"] = ((i[:, None] // 64 == i[None, :] // 64) & (i[:, None] <= i[None, :])).astype(np.float32)
    c["chind"] = (i[:, None] // 64 == np.arange(2)[None, :]).astype(np.float32)
    c["shiftA"] = (i[None, :] == i[:, None] + 1).astype(np.float32)
    sb = np.zeros((128, 128), np.float32)
    sb[127, 0] = 1.0
    c["shiftB"] = sb
    G = min(128, 16 * NPG)
    g = np.arange(G)
    c["pgsuf"] = ((g[:, None] // NPG == g[None, :] // NPG) & (g[:, None] > g[None, :])).astype(np.float32)
    SG = G // NPG
    c["seqind"] = (g[:, None] // NPG == np.arange(SG)[None, :]).astype(np.float32)
    return c


WNAMES = ["mix_norm", "w_in", "a_ln_g", "a_ln_b", "a_ws", "a_bs", "b_mu", "b_w0", "b_w2", "b_a0", "b_a2", "b_g2",
          "b_kk", "b_ka", "b_rk", "b_ln_g", "b_ln_b", "c_qn", "c_kn", "c_bias", "w_out", "ffn_norm", "w_gate",
          "w_up", "w_down", "ple_norm", "w_ple_gate", "w_ple"]


def build(L, T, NS, NPG, NPHYS, wshapes, stages=("A", "B", "C", "F", "P")):
    nc = bass.Bass("TRN2", target_bir_lowering=False)
    NTK = T // 128
    NT = T + NS
    G = min(128, NS * NPG)
    SG = G // NPG
    NG = NS // SG

    def din(name, shape, dt=F32):
        return nc.dram_tensor(name, list(shape), dt, kind="ExternalInput").ap()

    def dout(name, shape):
        return nc.dram_tensor(name, list(shape), F32, kind="ExternalOutput").ap()

    def dscr(name, shape, dt=F32):
        return nc.dram_tensor(name, list(shape), dt, kind="Internal").ap()

    I = {}
    I["xp"] = din("xp", [T, D])
    I["xs"] = din("xs", [NS, D])
    I["cache_k"] = din("cache_k", [L, NPHYS, 128 * 256])
    I["cache_v"] = din("cache_v", [L, NPHYS, 128 * 256])
    I["swkv"] = din("swkv", [L, NS * 8, 4096])
    I["sshift"] = din("sshift", [L, NS, 1792])
    I["ptab"] = din("ptab", [NS * NPG, 1], I32)
    I["pp"] = din("pp", [L, T, 256])
    I["psm"] = din("psm", [L, NS, 256])
    W = {n: din(n, wshapes[n]) for n in WNAMES}
    CN = make_consts(NPG)
    C = {n: din("c_" + n, CN[n].shape) for n in CN}

    O = {}
    O["y_p"] = dout("y_p", [T, D])
    O["y_s"] = dout("y_s", [NS, D])
    O["k_p"] = dout("k_p", [L, T, 256])
    O["v_p"] = dout("v_p", [L, T, 256])
    O["wkv_p"] = dout("wkv_p", [L, 8, 64, 64])
    O["shift_p"] = dout("shift_p", [L, 1792])
    O["k_s"] = dout("k_s", [L, NS, 256])
    O["v_s"] = dout("v_s", [L, NS, 256])
    O["wkv_s"] = dout("wkv_s", [L, NS * 8, 4096])
    O["shift_s"] = dout("shift_s", [L, NS, 1792])
    O["av_s"] = dout("av_s", [L, NS, 256])

    scr_q = dscr("scr_q", [NS, 256])
    scr_b = dscr("scr_b", [NS, 8, 8, 64])
    scr_y = dscr("scr_y", [NS * 8, 64])
    scr_f = dscr("scr_f", [NS, 2, 512])

    f = FW(nc)
    f.marks = []
    def mark(lbl):
        f.marks.append((lbl, f.n_pe))
    import contextlib
    es = contextlib.ExitStack()
    es.enter_context(nc.allow_non_contiguous_dma(reason="small parameter loads"))

    def sb(name, shape, dt=F32):
        return nc.alloc_sbuf_tensor(name, list(shape), dt).ap()

    banks = [nc.alloc_psum_tensor("bank%d" % i, [128, 512], F32).ap() for i in range(7)]
    ptb_ = nc.alloc_psum_tensor("ptb", [128, 1024], BF16).ap()

    def bankb(i):
        assert i == 7
        return ptb_

    f.psum_names = {b.tensor.name for b in banks} | {ptb_.tensor.name}

    ident_f = sb("ident_f", [128, 128]); f.dma(ident_f, C["ident"])
    ident_b = sb("ident_b", [128, 128], BF16); f.dma(ident_b, C["ident"], q="pool")
    ones_b = sb("ones_b", [128, 128], BF16); f.dma(ones_b, C["ones"], q="pool")
    negones_b = sb("negones_b", [128, 128], BF16)
    ntincl_b = sb("ntincl_b", [128, 128], BF16); f.dma(ntincl_b, C["ntincl"], q="pool")
    mstrict_b = sb("mstrict_b", [128, 512], BF16); f.dma(mstrict_b, C["mstrict"], q="pool")
    triljt = sb("triljt", [128, 128]); f.dma(triljt, C["triljt"])
    maskA = sb("maskA", [64, 1024]); f.dma(maskA, C["maskA"])
    maskB = sb("maskB", [64, 1024]); f.dma(maskB, C["maskB"])
    maskC = sb("maskC", [64, 512]); f.dma(maskC, C["maskC"])
    eye8 = sb("eye8", [64, 512]); f.dma(eye8, C["eye8"])
    triblk = sb("triblk", [128, 128]); f.dma(triblk, C["triblk"])
    chind = sb("chind", [128, 2]); f.dma(chind, C["chind"])
    shiftA = sb("shiftA", [128, 128], BF16); f.dma(shiftA, C["shiftA"], q="pool")
    shiftB = sb("shiftB", [128, 128], BF16); f.dma(shiftB, C["shiftB"], q="pool")
    pgsuf = sb("pgsuf", [G, G]); f.dma(pgsuf, C["pgsuf"])
    seqind = sb("seqind", [G, SG]); f.dma(seqind, C["seqind"])
    f.ts(negones_b, ones_b, -1.0, ALU.mult)
    onesf = sb("onesf", [128, 128]); f.dma(onesf, C["ones"])
    eps_rms = sb("eps_rms", [128, 1]); f.memset(eps_rms, 1e-6)
    eps_ln = sb("eps_ln", [128, 1]); f.memset(eps_ln, 1e-5)
    eps_gn = sb("eps_gn", [128, 1]); f.memset(eps_gn, 64e-5)
    eps_kk = sb("eps_kk", [128, 1]); f.memset(eps_kk, 1e-12)
    one_c = sb("one_c", [128, 1]); f.memset(one_c, 1.0)
    zero_c = sb("zero_c", [128, 1]); f.memset(zero_c, 0.0)

    hT = sb("hT", [128, 8, NT])
    class _Holder:
        pass
    hn = _Holder()
    hn.cm = None

    def hn_alloc(tag):
        hn.cm = nc.sbuf_tensor("hnT_" + tag, [128, 8, NT], BF16)
        hn.t = hn.cm.__enter__().ap()

    def hn_free():
        f.barrier()
        hn.cm.__exit__(None, None, None)
        hn.cm = None
        f.recs.clear()

    pb_scr = dscr("pb_scr", [NT, 1792])

    def rsqrt_(out, in_, eps_t, scale, np_):
        f.act(out, in_, AF.Ln, bias=eps_t[0:np_, :], scale=scale)
        f.act(out, out, AF.Exp, scale=-0.5)

    def sigmoid_(out, in_, np_, scale=1.0, tmp=None):
        f.act(out, in_, AF.Exp, scale=-scale)
        f.ts(out, out, 1.0, ALU.add)
        f.op("dve", lambda e: e.reciprocal(_ap(out), _ap(out)), [out], [out])

    tblocks = []
    c0 = 0
    while c0 < NT:
        n = min(512, NT - c0)
        tblocks.append((c0, n))
        c0 += n

    xt_cm = nc.sbuf_tensor("xt", [128, D], F32)
    xt = xt_cm.__enter__().ap()
    for i in range(NTK + 1):
        n = 128 if i < NTK else NS
        src = I["xp"][i * 128:(i + 1) * 128, :] if i < NTK else I["xs"]
        f.dma(xt[0:n, :], src)
        for half in range(2):
            pst = banks[half]
            for c in range(4):
                m = half * 4 + c
                f.transpose(pst[:, c * 128:c * 128 + n], xt[0:n, m * 128:(m + 1) * 128], ident_f[0:n, 0:n])
            f.copy(hT[:, half * 4:(half + 1) * 4, i * 128:i * 128 + n],
                   pst.rearrange("p (c t) -> p c t", c=4)[:, :, 0:n], en="act" if half else "dve")
    f.barrier()
    xt_cm.__exit__(None, None, None)
    f.recs.clear()

    def rmsnorm_all(gname, l):
        tag = "_%s_%d" % (gname, l)
        with nc.sbuf_tensor("gvec" + tag, [128, 8], F32) as gvec_h, nc.sbuf_tensor("sqb" + tag, [128, 8, 512], BF16) as sqb_h, \
                nc.sbuf_tensor("rstd_t" + tag, [128, 512], F32) as rstd_h:
            _rmsnorm_body(gname, l, gvec_h.ap(), sqb_h.ap(), rstd_h.ap())
            f.barrier()
        f.recs.clear()

    def _rmsnorm_body(gname, l, gvec, sqb, rstd_t):
        f.dma(gvec, W[gname][l].rearrange("(c p) -> p c", p=128))
        for (c0, n) in tblocks:
            f.act(sqb[:, :, 0:n], hT[:, :, c0:c0 + n], AF.Square)
            ps = banks[0]
            for k in range(8):
                f.mm(ps[:, 0:n], ones_b, sqb[:, k, 0:n], start=(k == 0), stop=(k == 7))
            rsqrt_(rstd_t[:, 0:n], ps[:, 0:n], eps_rms, 1.0 / D, 128)
            for k in range(8):
                f.stt(hn.t[:, k, c0:c0 + n], hT[:, k, c0:c0 + n], gvec[:, k:k + 1], rstd_t[:, 0:n],
                      ALU.mult, ALU.mult)

    def bcload(dst, row_ap, q="sp"):
        P = dst.shape[0]
        src = row_ap.unsqueeze(0).broadcast_to([P] + list(row_ap.shape))
        f.dma(dst, src, q=q)

    def proj_T(ps, ntok, col0, wt, wc0, wc1):
        for k in range(8):
            f.mm(ps[0:ntok, 0:wc1 - wc0], hn.t[:, k, col0:col0 + ntok], wt[:, k, wc0:wc1], start=(k == 0), stop=(k == 7))

    def add_h(m, c0, n, ps):
        f.tt(hT[:, m, c0:c0 + n], hT[:, m, c0:c0 + n], ps, ALU.add)

    for l in range(L):
        mark('L%d norm' % l)
        hn_alloc('m%d' % l)
        rmsnorm_all("mix_norm", l)
        mark('L%d pass1-prompt' % l)
        f.barrier()
        with contextlib.ExitStack() as p1:
            def sb1(name, shape, dt=F32):
                return p1.enter_context(nc.sbuf_tensor(name + "_%d" % l, list(shape), dt)).ap()
            p1a = contextlib.ExitStack()
            p1b = contextlib.ExitStack()

            def sb1a(name, shape, dt=F32):
                return p1a.enter_context(nc.sbuf_tensor(name + "_%d" % l, list(shape), dt)).ap()

            def sb1b(name, shape, dt=F32):
                return p1b.enter_context(nc.sbuf_tensor(name + "_%d" % l, list(shape), dt)).ap()
            wAC = sb1("wAC", [128, 8, 1280], BF16)
            win = W["w_in"][l].rearrange("(k p) n -> p k n", p=128)
            f.dma(wAC[:, :, 0:512], win[:, :, 0:512], q="pool")
            f.dma(wAC[:, :, 512:1280], win[:, :, 2304:3072], q="pool")
            woA = sb1("woA", [128, 2, D], BF16)
            f.dma(woA, W["w_out"][l][0:256, :].rearrange("(k p) n -> p k n", p=128), q="pool")
            lng = sb1("lng", [128, 256]); bcload(lng, W["a_ln_g"][l])
            lnb = sb1("lnb", [128, 256]); bcload(lnb, W["a_ln_b"][l])
            bs_t = sb1("bs_t", [128, 4]); f.dma(bs_t, W["a_bs"][l].rearrange("g i -> i g"))
            ws0 = sb1("ws0", [NS, 4]); f.dma(ws0, W["a_ws"][l][:, 0, 0:1].rearrange("g o -> o g").broadcast_to([NS, 4]))
            bs0 = sb1("bs0", [NS, 4]); f.dma(bs0, W["a_bs"][l][:, 0:1].rearrange("g o -> o g").broadcast_to([NS, 4]))
            gqk = sb1("gqk", [128, 512])
            f.dma(gqk[:, 0:256].rearrange("p (h d) -> p h d", h=4),
                  W["c_qn"][l].unsqueeze(0).unsqueeze(0).broadcast_to([128, 4, 64]))
            f.dma(gqk[:, 256:512].rearrange("p (h d) -> p h d", h=4),
                  W["c_kn"][l].unsqueeze(0).unsqueeze(0).broadcast_to([128, 4, 64]))
            f.ts(gqk[:, 0:256], gqk[:, 0:256], SB_SCALE, ALU.mult)
            cb1 = sb1("cb1", [1, 4]); f.dma(cb1, W["c_bias"][l].unsqueeze(0))
            biasrow = sb1("biasrow", [1, 4, 128], BF16)
            f.copy(biasrow, cb1.unsqueeze(2).broadcast_to([1, 4, 128]))
            cbG = sb1("cbG", [G, 4]); bcload(cbG, W["c_bias"][l])

            u_sb = sb1("u_sb", [128, 256])
            vc = sb1("vc", [128, 256])
            st1 = sb1("st1", [128, 8])
            st2 = sb1("st2", [128, 8])
            ya = sb1("ya", [128, 256])
            yab = sb1("yab", [128, 256], BF16)
            sq1 = sb1("sq1", [128, 512])
            qkn = sb1("qkn", [128, 512])
            qkb = sb1("qkb", [128, 512], BF16)
            vf = sb1("vf", [128, 256])
            woC = sb1a("woC", [64, 4, D], BF16)
            WmT = sb1a("WmT", [128, 4, 128], BF16)
            wsl = sb1a("wsl", [128, 4, 128])
            KT = sb1a("KT", [128, 2, T], BF16)
            Vall = sb1a("Vall", [128, NTK, 256], BF16)
            QT = sb1a("QT", [128, 2, 128], BF16)
            vnb = sb1a("vnb", [128, 256], BF16)
            yaT = sb1a("yaT", [128, 2, 128], BF16)
            e_t2 = [sb1a("e_t%d" % j, [128, 512], BF16) for j in range(2)]
            sp_all = sb1a("sp_all", [128, NTK, 512], BF16)
            att_t2 = [sb1a("att_t%d" % j, [128, 512], BF16) for j in range(2)]
            Cs2 = [sb1a("Cs%d" % j, [128, 512], BF16) for j in range(2)]
            ycT = sb1a("ycT", [64, 512], BF16)
            f.dma(woC, W["w_out"][l][768:1024, :].rearrange("(h p) n -> p h n", p=64), q="pool")
            f.dma(wsl, W["a_ws"][l].rearrange("g i j -> i g j"))
            for g in range(4):
                f.transpose(banks[0][:, g * 128:(g + 1) * 128], wsl[:, g, :], ident_f)
            f.tt(WmT, banks[0].rearrange("p (g i) -> p g i", g=4), triljt.unsqueeze(1).broadcast_to([128, 4, 128]), ALU.mult)

            def mixer_A(n, col0, sample):
                ps = banks[0]
                BIS3 = int(os.environ.get("BIS3", "99"))
                proj_T(ps, n, col0, wAC, 0, 512)
                if BIS3 < 1: return
                f.copy(u_sb[0:n], ps[0:n, 0:256], en="act")
                f.reduce(st1[0:n, 0:1], ps[0:n, 256:512])
                f.ts(st1[0:n, 0:1], st1[0:n, 0:1], -1.0 / 256, ALU.mult)
                if BIS3 < 2: return
                f.ts(vc[0:n], ps[0:n, 256:512], st1[0:n, 0:1], ALU.add)
                f.tt(ya[0:n], vc[0:n], vc[0:n], ALU.mult)
                f.reduce(st1[0:n, 1:2], ya[0:n])
                if BIS3 < 3: return
                rsqrt_(st1[0:n, 1:2], st1[0:n, 1:2], eps_ln, 1.0 / 256, n)
                f.ts(vc[0:n], vc[0:n], st1[0:n, 1:2], ALU.mult)
                if BIS3 < 4: return
                f.tt(vc[0:n], vc[0:n], lng[0:n], ALU.mult)
                f.tt(vc[0:n], vc[0:n], lnb[0:n], ALU.add)
                if BIS3 < 5: return
                if sample:
                    f.dma(O["av_s"][l], vc[0:n])
                    v3 = vc[0:n].rearrange("p (g d) -> p g d", g=4)
                    f.tt(ya[0:n].rearrange("p (g d) -> p g d", g=4), v3, ws0.unsqueeze(2).broadcast_to([NS, 4, 64]), ALU.mult)
                    f.tt(ya[0:n].rearrange("p (g d) -> p g d", g=4), ya[0:n].rearrange("p (g d) -> p g d", g=4),
                         bs0.unsqueeze(2).broadcast_to([NS, 4, 64]), ALU.add)
                    f.tt(yab[0:n], ya[0:n], u_sb[0:n], ALU.mult)
                else:
                    f.copy(vnb, vc, en="pool")
                    if BIS3 < 6: return
                    pm = banks[1]
                    for g in range(4):
                        f.mm(pm[:, g * 64:(g + 1) * 64], WmT[:, g, :], vnb[:, g * 64:(g + 1) * 64])
                    if BIS3 < 7: return
                    f.tt(ya.rearrange("p (g d) -> p g d", g=4), pm[:, 0:256].rearrange("p (g d) -> p g d", g=4),
                         bs_t.unsqueeze(2).broadcast_to([128, 4, 64]), ALU.add)
                    f.tt(yab, ya, u_sb, ALU.mult)

            def qkv(n, col0):
                pq = banks[2]
                pv = banks[3]
                proj_T(pq, n, col0, wAC, 512, 1024)
                proj_T(pv, n, col0, wAC, 1024, 1280)
                f.act(sq1[0:n], pq[0:n], AF.Square)
                f.reduce(st2[0:n], sq1[0:n].rearrange("p (h d) -> p h d", h=8))
                rsqrt_(st2[0:n], st2[0:n], eps_rms, 1.0 / 64, n)
                f.tt(qkn[0:n].rearrange("p (h d) -> p h d", h=8), pq[0:n].rearrange("p (h d) -> p h d", h=8),
                     st2[0:n].unsqueeze(2).broadcast_to([n, 8, 64]), ALU.mult)
                f.tt(qkn[0:n], qkn[0:n], gqk[0:n], ALU.mult, en="pool")
                f.copy(qkb[0:n], qkn[0:n], en="pool")
                f.copy(vf[0:n], pv[0:n, 0:256], en="act")

            BIS = int(os.environ.get("BIS", "9"))
            for i in range(NTK if (("A" in stages or "C" in stages) and BIS >= 2) else 0):
                t0 = i * 128
                col0 = t0
                BIS2 = int(os.environ.get("BIS2", "9"))
                if "A" in stages:
                    mixer_A(128, col0, False)
                    if BIS2 >= 2:
                        pt = bankb(7)
                        for k in range(2):
                            f.transpose(pt[:, k * 128:(k + 1) * 128], yab[:, k * 128:(k + 1) * 128], ident_b)
                        f.copy(yaT, pt[:, 0:256].rearrange("p (k t) -> p k t", k=2), en="act")
                    else:
                        f.memset(yaT, 0.0)
                if BIS2 < 3:
                    continue
                if "C" in stages:
                    qkv(128, col0)
                    f.dma(O["k_p"][l, t0:t0 + 128, :], qkn[:, 256:512])
                    f.dma(O["v_p"][l, t0:t0 + 128, :], vf)
                    f.copy(Vall[:, i, :], vf, en="pool")
                    pt = bankb(7)
                    for k in range(2):
                        f.transpose(pt[:, 256 + k * 128:256 + (k + 1) * 128], qkb[:, k * 128:(k + 1) * 128], ident_b)
                        f.transpose(pt[:, 512 + k * 128:512 + (k + 1) * 128], qkb[:, 256 + k * 128:256 + (k + 1) * 128], ident_b)
                    f.copy(QT, pt[:, 256:512].rearrange("p (k t) -> p k t", k=2), en="act")
                    f.copy(KT[:, :, t0:t0 + 128], pt[:, 512:768].rearrange("p (k t) -> p k t", k=2), en="dve")
                    pO = banks[6]

                    def qk_into(ps, kb, last):
                        for h in range(4):
                            r0 = (h % 2) * 64
                            f.mm(ps[:, h * 128:(h + 1) * 128], KT[r0:r0 + 64, h // 2, kb * 128:(kb + 1) * 128],
                                 QT[r0:r0 + 64, h // 2, :], start=(h == 0), stop=False, skip_group_check=True)
                            f.mm(ps[:, h * 128:(h + 1) * 128], ones_b[0:1, :], biasrow[0:1, h, :], start=False,
                                 stop=(last and h == 3), skip_group_check=True)

                    def spk(kb):
                        return K(sp_all[:, kb, :], "sp_all%d" % kb)

                    for n1, kb in enumerate(range(i, -1, -1)):
                        pS = banks[4 + n1 % 2]
                        et = e_t2[n1 % 2]
                        qk_into(pS, kb, True)
                        f.act(et, pS, AF.Exp)
                        f.act(spk(kb), et, AF.Ln, bias=one_c)
                        if kb == i:
                            f.tt(spk(kb), spk(kb), mstrict_b, ALU.mult)
                    pend = None
                    for n2, kb in enumerate(range(i, -1, -1)):
                        pE = banks[4 + n2 % 2]
                        at = att_t2[n2 % 2]
                        cs_cur, cs_nxt = Cs2[n2 % 2], Cs2[(n2 + 1) % 2]
                        qk_into(pE, kb, False)
                        f.mm(pE, ntincl_b, spk(kb), start=False, stop=(kb == i), skip_group_check=True)
                        if kb < i:
                            f.mm(pE, negones_b, cs_cur, start=False, stop=True, skip_group_check=True)
                        if pend is not None:
                            pend()
                        f.act(at, pE, AF.Exp)
                        if kb == i:
                            f.tt(at, at, mstrict_b, ALU.mult)
                        if kb > 0:
                            if kb == i:
                                f.copy(cs_nxt, spk(kb), en="pool")
                            else:
                                f.tt(cs_nxt, cs_cur, spk(kb), ALU.add, en="pool")

                        def av(kb=kb, at=at):
                            for h in range(4):
                                f.mm(pO[0:64, h * 128:(h + 1) * 128], Vall[:, kb, h * 64:(h + 1) * 64],
                                     at[:, h * 128:(h + 1) * 128], start=(kb == i and h == 0),
                                     stop=(h == 3), skip_group_check=True)
                        pend = av
                    pend()
                    f.copy(ycT, pO[0:64, :], en="act")
                for half in range(2):
                    ph = banks[half]
                    for mm_ in range(4):
                        m = half * 4 + mm_
                        ops = []
                        if "A" in stages:
                            ops += [(woA[:, k, m * 128:(m + 1) * 128], yaT[:, k, :]) for k in range(2)]
                        if "C" in stages:
                            ops += [(woC[:, h, m * 128:(m + 1) * 128], ycT[:, h * 128:(h + 1) * 128]) for h in range(4)]
                        for j, (a, b) in enumerate(ops):
                            f.mm(ph[:, mm_ * 128:(mm_ + 1) * 128], a, b, start=(j == 0), stop=(j == len(ops) - 1))
                    f.tt(hT[:, half * 4:(half + 1) * 4, t0:t0 + 128], hT[:, half * 4:(half + 1) * 4, t0:t0 + 128],
                         ph.rearrange("p (c t) -> p c t", c=4), ALU.add)

            f.barrier()
            p1a.close()
            f.recs.clear()
            mark('L%d pass1-sample' % l)
            col0 = T
            yaT_s = sb1b("yaT_s", [128, 2, NS], BF16)
            ycT_s = sb1b("ycT_s", [128, 2, NS], BF16)
            f.memset(yaT_s, 0.0)
            f.memset(ycT_s, 0.0)
            if "A" in stages and BIS >= 3:
                mixer_A(NS, col0, True)
                pt = bankb(7)
                for k in range(2):
                    f.transpose(pt[:, k * 128:k * 128 + NS], yab[0:NS, k * 128:(k + 1) * 128], ident_b[0:NS, 0:NS])
                f.copy(yaT_s, pt[:, 0:256].rearrange("p (k t) -> p k t", k=2)[:, :, 0:NS], en="act")
            if "C" in stages:
                qkv(NS, col0)
                f.dma(O["k_s"][l], qkn[0:NS, 256:512])
                f.dma(O["v_s"][l], vf[0:NS])
                f.dma(scr_q, qkn[0:NS, 0:256])
                PSL = 8
                NSL = 128 // PSL
                qrep = sb1b("qrep", [G, 256])
                idx = sb1b("idx", [G, 1], I32)
                idx8 = sb1b("idx8", [G, 1], I32)
                kv_t = [sb1b("kv%d" % j, [G, PSL * 256]) for j in range(3)]
                prod = sb1b("prod", [G, PSL * 256])
                s_all = sb1b("s_all", [G, 128, 4])
                e_d = sb1b("e_d", [G, 128, 4])
                sp_d = sb1b("sp_d", [G, 128, 4])
                cum_d = sb1b("cum_d", [G, 128, 4])
                tot_d = sb1b("tot_d", [G, 4])
                R_d = sb1b("R_d", [G, 4])
                att_d = sb1b("att_d", [G, 128, 4])
                yacc = sb1b("yacc", [G, 256])
                ypart = sb1b("ypart", [G, 256])
                for g in range(NG):
                    f.dma(idx, I["ptab"][g * G:(g + 1) * G, :])
                    f.ts(idx8, idx, NSL, ALU.mult)
                    f.dma(qrep, scr_q[g * SG:(g + 1) * SG, :].unsqueeze(1).broadcast_to([SG, NPG, 256]))
                    for sl in range(NSL):
                        kt = kv_t[sl % 3]
                        src = bass.AP(I["cache_k"].tensor, 0, [[PSL * 256, NPHYS * NSL], [1, PSL * 256]])
                        eo = l * NPHYS * 32768 + sl * PSL * 256
                        f.dma(kt, I["cache_k"], q="pool", extra_reads=[idx8],
                              fn=lambda e, kt=kt, src=src, eo=eo: e.indirect_dma_start(
                                  out=kt, out_offset=None, in_=src,
                                  in_offset=bass.IndirectOffsetOnAxis(ap=idx8[:, :], axis=0), element_offset=eo))
                        f.tt(prod.rearrange("p (s c) -> p s c", s=PSL), kt.rearrange("p (s c) -> p s c", s=PSL),
                             qrep.unsqueeze(1).broadcast_to([G, PSL, 256]), ALU.mult)
                        f.reduce(s_all[:, sl * PSL:(sl + 1) * PSL, :], prod.rearrange("p (s h d) -> p s h d", s=PSL, h=4))
                    f.tt(s_all, s_all, cbG.unsqueeze(1).broadcast_to([G, 128, 4]), ALU.add)
                    f.act(e_d, s_all, AF.Exp)
                    f.act(sp_d, e_d, AF.Ln, bias=one_c[0:G])
                    f.copy(cum_d[:, 127:128, :], sp_d[:, 127:128, :])
                    cur, nxt = sp_d, cum_d
                    sh = 1
                    bufs = [cum_d, e_d]
                    bi = 0
                    src_t = sp_d
                    while sh < 128:
                        dst_t = bufs[bi]
                        f.tt(dst_t[:, 0:128 - sh, :], src_t[:, 0:128 - sh, :], src_t[:, sh:128, :], ALU.add)
                        f.copy(dst_t[:, 128 - sh:128, :], src_t[:, 128 - sh:128, :], en="pool")
                        src_t = dst_t
                        bi ^= 1
                        sh *= 2
                    cumI = src_t
                    pR = banks[4]
                    f.copy(tot_d, cumI[:, 0, :])
                    f.mm(pR[0:G, 0:4], pgsuf, tot_d)
                    f.copy(R_d, pR[0:G, 0:4], en="act")
                    f.tt(att_d, s_all, cumI, ALU.subtract)
                    f.tt(att_d, att_d, R_d.unsqueeze(1).broadcast_to([G, 128, 4]), ALU.subtract)
                    f.act(att_d, att_d, AF.Exp)
                    for sl in range(NSL):
                        vt = kv_t[sl % 3]
                        src = bass.AP(I["cache_v"].tensor, 0, [[PSL * 256, NPHYS * NSL], [1, PSL * 256]])
                        eo = l * NPHYS * 32768 + sl * PSL * 256
                        f.dma(vt, I["cache_v"], q="pool", extra_reads=[idx8],
                              fn=lambda e, vt=vt, src=src, eo=eo: e.indirect_dma_start(
                                  out=vt, out_offset=None, in_=src,
                                  in_offset=bass.IndirectOffsetOnAxis(ap=idx8[:, :], axis=0), element_offset=eo))
                        f.tt(prod.rearrange("p (s h d) -> p s h d", s=PSL, h=4),
                             vt.rearrange("p (s h d) -> p s h d", s=PSL, h=4),
                             att_d[:, sl * PSL:(sl + 1) * PSL, :].unsqueeze(3).broadcast_to([G, PSL, 4, 64]), ALU.mult)
                        dst = yacc if sl == 0 else ypart
                        f.reduce(dst, prod.rearrange("p (s c) -> p c s", s=PSL))
                        if sl > 0:
                            f.tt(yacc, yacc, ypart, ALU.add, en="pool")
                    pY = banks[5]
                    for k in range(2):
                        f.mm(pY[:, k * SG:(k + 1) * SG], yacc[:, k * 128:(k + 1) * 128], seqind)
                    f.copy(ycT_s[:, :, g * SG:(g + 1) * SG], pY[:, 0:2 * SG].rearrange("p (k s) -> p k s", k=2), en="act")
            woC2 = sb1b("woC2", [128, 2, D], BF16)
            f.dma(woC2, W["w_out"][l][768:1024, :].rearrange("(k p) n -> p k n", p=128), q="pool")
            ph = banks[0]
            for m in range(8):
                ops = []
                if "A" in stages:
                    ops += [(woA[:, k, m * 128:(m + 1) * 128], yaT_s[:, k, :]) for k in range(2)]
                if "C" in stages:
                    ops += [(woC2[:, k, m * 128:(m + 1) * 128], ycT_s[:, k, :]) for k in range(2)]
                for j, (a, b) in enumerate(ops):
                    f.mm(ph[:, m * NS:(m + 1) * NS], a, b, start=(j == 0), stop=(j == len(ops) - 1))
            if "A" in stages or "C" in stages:
                f.tt(hT[:, :, T:T + NS], hT[:, :, T:T + NS], ph[:, 0:8 * NS].rearrange("p (c t) -> p c t", c=8), ALU.add)
            f.barrier()
            p1b.close()
        f.recs.clear()

        mark('L%d pass15' % l)
        if "B" in stages:
            with contextlib.ExitStack() as p15:
                wBf = p15.enter_context(nc.sbuf_tensor("wBf_%d" % l, [128, 8, 1792], BF16)).ap()
                stg = [p15.enter_context(nc.sbuf_tensor("stg%d_%d" % (j, l), [128, 1792], F32)).ap() for j in range(2)]
                f.dma(wBf, W["w_in"][l].rearrange("(k p) n -> p k n", p=128)[:, :, 512:2304], q="pool")
                cnt15 = 0
                for i in range(NTK + 1):
                    n = 128 if i < NTK else NS
                    st_ = stg[i % 2]
                    for (c0, ncol) in [(0, 512), (512, 512), (1024, 512), (1536, 256)]:
                        ps = banks[cnt15 % 4]
                        proj_T(ps, n, i * 128, wBf, c0, c0 + ncol)
                        f.copy(st_[0:n, c0:c0 + ncol], ps[0:n, 0:ncol], en="act" if cnt15 % 2 else "dve")
                        cnt15 += 1
                    f.dma(pb_scr[i * 128:i * 128 + n, :], st_[0:n, :])
                f.dma(O["shift_p"][l].unsqueeze(0), pb_scr[T - 1:T, :])
                f.dma(O["shift_s"][l], pb_scr[T:T + NS, :])
                f.barrier()
            f.recs.clear()
        hn_free()
        mark('L%d pass2' % l)
        if "B" in stages:
            with contextlib.ExitStack() as p2:
                def sb2(name, shape, dt=F32):
                    return p2.enter_context(nc.sbuf_tensor(name + "_%d" % l, list(shape), dt)).ap()
                _rwkv_pass(nc, f, l, L, T, NS, NTK, W, I, O, sb2, banks, bankb, hT, pb_scr, scr_b, scr_y, scr_f,
                           dict(ident_f=ident_f, ident_b=ident_b, maskA=maskA, maskB=maskB, maskC=maskC, eye8=eye8,
                                triblk=triblk, chind=chind, shiftA=shiftA, shiftB=shiftB, eps_gn=eps_gn, eps_kk=eps_kk,
                                one_c=one_c, onesf=onesf), rsqrt_, sigmoid_, bcload, proj_T)
                f.barrier()
            f.recs.clear()

        mark('L%d ffn' % l)
        if "F" in stages:
            hn_alloc('f%d' % l)
            rmsnorm_all("ffn_norm", l)
            with contextlib.ExitStack() as p3:
                def sb3(name, shape, dt=F32):
                    return p3.enter_context(nc.sbuf_tensor(name + "_%d" % l, list(shape), dt)).ap()
                wg = [sb3("wg%d" % j, [128, 8, 512], BF16) for j in range(2)]
                wu = [sb3("wu%d" % j, [128, 8, 512], BF16) for j in range(2)]
                wd = [sb3("wd%d" % j, [128, 4, D], BF16) for j in range(2)]
                silu_t = sb3("silu_t", [128, 512])
                actT = [sb3("actT%d" % j, [128, 4, 512], BF16) for j in range(2)]
                nblk = (DFF + 511) // 512
                gsrc = W["w_gate"][l].rearrange("(k p) n -> p k n", p=128)
                usrc = W["w_up"][l].rearrange("(k p) n -> p k n", p=128)
                steps = []
                for b in range(nblk):
                    for (c0, n) in tblocks:
                        steps.append((b, c0, n))
                loaded = set()

                def load_block(b):
                    if b in loaded or b >= nblk:
                        return
                    loaded.add(b)
                    c0f = b * 512
                    nf = min(512, DFF - c0f)
                    j = b % 2
                    f.dma(wg[j][:, :, 0:nf], gsrc[:, :, c0f:c0f + nf], q="pool")
                    f.dma(wu[j][:, :, 0:nf], usrc[:, :, c0f:c0f + nf], q="pool")
                    f.dma(wd[j][:, 0:nf // 128, :], W["w_down"][l][c0f:c0f + nf, :].rearrange("(c p) n -> p c n", p=128), q="pool")

                def gateup(si):
                    b, c0, n = steps[si]
                    load_block(b)
                    j = b % 2
                    ncf = min(512, DFF - b * 512) // 128
                    aT = actT[si % 2]
                    for c in range(ncf):
                        pg = banks[(2 * c) % 4]
                        pu = banks[(2 * c + 1) % 4]
                        for k in range(8):
                            f.mm(pg[:, 0:n], wg[j][:, k, c * 128:(c + 1) * 128], hn.t[:, k, c0:c0 + n],
                                 start=(k == 0), stop=(k == 7))
                        for k in range(8):
                            f.mm(pu[:, 0:n], wu[j][:, k, c * 128:(c + 1) * 128], hn.t[:, k, c0:c0 + n],
                                 start=(k == 0), stop=(k == 7))
                        f.act(silu_t[:, 0:n], pg[:, 0:n], AF.Silu)
                        f.tt(aT[:, c, 0:n], silu_t[:, 0:n], pu[:, 0:n], ALU.mult)

                def down(si):
                    b, c0, n = steps[si]
                    j = b % 2
                    ncf = min(512, DFF - b * 512) // 128
                    aT = actT[si % 2]
                    for m in range(8):
                        pd = banks[4 + m % 3]
                        for c in range(ncf):
                            f.mm(pd[:, 0:n], wd[j][:, c, m * 128:(m + 1) * 128], aT[:, c, 0:n],
                                 start=(c == 0), stop=(c == ncf - 1))
                        add_h(m, c0, n, pd[:, 0:n])

                for si in range(len(steps) + 1):
                    if si < len(steps):
                        gateup(si)
                    if si > 0:
                        down(si - 1)
                f.barrier()
            f.recs.clear()
            hn_free()

        mark('L%d ple' % l)
        if "P" in stages:
            hn_alloc('p%d' % l)
            rmsnorm_all("ple_norm", l)
            with contextlib.ExitStack() as p4:
                def sb4(name, shape, dt=F32):
                    return p4.enter_context(nc.sbuf_tensor(name + "_%d" % l, list(shape), dt)).ap()
                wpg = sb4("wpg", [128, 8, D], BF16)
                wpl = sb4("wpl", [128, 2, D], BF16)
                f.dma(wpg, W["w_ple_gate"][l].rearrange("(k p) n -> p k n", p=128), q="pool")
                f.dma(wpl, W["w_ple"][l].rearrange("(k p) n -> p k n", p=128), q="pool")
                peT = sb4("peT", [128, 2, NT], BF16)
                pet = [sb4("pet%d" % j, [128, 256]) for j in range(2)]
                for i in range(NTK + 1):
                    n = 128 if i < NTK else NS
                    src = I["pp"][l, i * 128:(i + 1) * 128, :] if i < NTK else I["psm"][l]
                    pe_ = pet[i % 2]
                    f.dma(pe_[0:n], src)
                    pt = banks[i % 2]
                    for k in range(2):
                        f.transpose(pt[:, k * 128:k * 128 + n], pe_[0:n, k * 128:(k + 1) * 128], ident_f[0:n, 0:n])
                    f.copy(peT[:, :, i * 128:i * 128 + n], pt[:, 0:256].rearrange("p (k t) -> p k t", k=2)[:, :, 0:n],
                           en="act" if i % 2 else "dve")
                gate_t = [sb4("gate_t%d" % j, [128, 512]) for j in range(2)]
                cnt = 0
                for (c0, n) in tblocks:
                    for m in range(8):
                        pa = banks[2 + (cnt % 2) * 2]
                        pb_ = banks[3 + (cnt % 2) * 2]
                        gt = gate_t[cnt % 2]
                        cnt += 1
                        for k in range(8):
                            f.mm(pa[:, 0:n], wpg[:, k, m * 128:(m + 1) * 128], hn.t[:, k, c0:c0 + n],
                                 start=(k == 0), stop=(k == 7))
                        for k in range(2):
                            f.mm(pb_[:, 0:n], wpl[:, k, m * 128:(m + 1) * 128], peT[:, k, c0:c0 + n],
                                 start=(k == 0), stop=(k == 1))
                        f.act(gt[:, 0:n], pa[:, 0:n], AF.Sigmoid)
                        f.tt(gt[:, 0:n], gt[:, 0:n], pb_[:, 0:n], ALU.mult)
                        f.tt(hT[:, m, c0:c0 + n], hT[:, m, c0:c0 + n], gt[:, 0:n], ALU.add, en="pool")
                f.barrier()
            f.recs.clear()
            hn_free()

    mark('final')
    yt = [sb("yt%d" % j, [128, D]) for j in range(2)]
    for i in range(NTK + 1):
        n = 128 if i < NTK else NS
        y_ = yt[i % 2]
        for half in range(2):
            pst = banks[(i % 2) * 2 + half]
            for c in range(4):
                m = half * 4 + c
                f.transpose(pst[0:n, c * 128:(c + 1) * 128], hT[:, m, i * 128:i * 128 + n], ident_f)
            f.copy(y_[0:n, half * 512:(half + 1) * 512], pst[0:n, :], en="act" if half else "dve")
        dst = O["y_p"][i * 128:(i + 1) * 128, :] if i < NTK else O["y_s"]
        f.dma(dst, y_[0:n])
    f.finish()
    es.close()
    return nc, f


def _rwkv_pass(nc, f, l, L, T, NS, NTK, W, I, O, sb2_outer, banks, bankb, hT, pb_scr, scr_b, scr_y, scr_f, CT, rsqrt_, sigmoid_,
               bcload, proj_T):
    import contextlib
    CH = BF16
    NH = 8
    CW = NH * 64
    ident_f, ident_b = CT["ident_f"], CT["ident_b"]
    maskA, maskB, maskC, eye8 = CT["maskA"], CT["maskB"], CT["maskC"], CT["eye8"]
    triblk, chind, shiftA, shiftB = CT["triblk"], CT["chind"], CT["shiftA"], CT["shiftB"]
    eps_gn, eps_kk, one_c = CT["eps_gn"], CT["eps_kk"], CT["one_c"]
    ptb = bankb(7)
    win = W["w_in"][l].rearrange("(k p) n -> p k n", p=128)

    def v3(ap, h=NH):
        return ap.rearrange("p (h d) -> p h d", h=h)

    def bc3(ap, n, h=NH):
        return ap.unsqueeze(2).broadcast_to([n, h, 64])

    for hg in range(1):
      with contextlib.ExitStack() as ph_:
        def sb2(name, shape, dt=F32):
            return ph_.enter_context(nc.sbuf_tensor(name + "_%d_%d" % (l, hg), list(shape), dt)).ap()
        blocks = [(0, 512), (512, 512), (1024, 512), (1536, 256)]
        mu = sb2("mu", [128, 1792])
        bcload(mu, W["b_mu"][l])
        woB = sb2("woB", [128, 4, D], BF16)
        f.dma(woB, W["w_out"][l][256:768, :].rearrange("(k p) n -> p k n", p=128), q="pool")
        cs = slice(0, CW)
        w2 = sb2("w2", [64, CW], BF16); f.dma(w2, W["b_w2"][l][:, cs], q="pool")
        a2 = sb2("a2", [64, CW], BF16); f.dma(a2, W["b_a2"][l][:, cs], q="pool")
        g2 = sb2("g2", [128, CW], BF16); f.dma(g2, W["b_g2"][l][:, cs], q="pool")
        w0 = sb2("w0", [128, CW]); bcload(w0, W["b_w0"][l][cs])
        a0 = sb2("a0", [128, CW]); bcload(a0, W["b_a0"][l][cs])
        kkp = sb2("kkp", [128, CW]); bcload(kkp, W["b_kk"][l][cs])
        kap = sb2("kap", [128, CW]); bcload(kap, W["b_ka"][l][cs])
        rkp = sb2("rkp", [128, CW]); bcload(rkp, W["b_rk"][l].rearrange("h d -> (h d)")[cs])
        lng = sb2("blng", [128, CW]); bcload(lng, W["b_ln_g"][l][cs])
        lnb = sb2("blnb", [128, CW]); bcload(lnb, W["b_ln_b"][l][cs])

        ST = sb2("ST", [64, NH, 64]); f.memset(ST, 0.0)
        STb = sb2("STb", [64, NH, 64], BF16); f.memset(STb, 0.0)
        xs_bufs = [sb2("xs%d" % j, [128, 1792]) for j in range(1)]
        cur_b = [sb2("cur_b%d" % j, [128, 1792], BF16) for j in range(2)]
        f.memset(cur_b[1], 0.0)
        lb = sb2("lb", [128, 128], BF16)
        lb2 = sb2("lb2", [128, 128], BF16)
        lT = sb2("lT", [128, 3, 128], BF16)
        logw = sb2("logw", [128, CW])
        a_t = sb2("a_t", [128, CW])
        PKf = sb2("PKf", [128, 2, CW])
        PKf1 = sb2("PKf1", [64, 2, CW])
        PKb = sb2("PKb", [128, 3, CW], BF16)
        PKb1 = sb2("PKb1", [64, 3, CW], BF16)
        kk = sb2("kk", [128, CW])
        kmod = sb2("kmod", [128, CW])
        t1 = sb2("t1", [128, CW])
        t2 = sb2("t2", [128, CW])
        st8 = sb2("st8", [128, NH])
        st8b = sb2("st8b", [128, NH])
        eL = sb2("eL", [128, CW])
        enL = sb2("enL", [128, CW])
        fm_b = [sb2("fm_b%d" % j, [128, CW], BF16) for j in range(4)]
        QRT = sb2("QRT", [64, NH, 2, 2, 64], BF16)
        BKT = sb2("BKT", [64, NH, 2, 2, 64], BF16)
        GC = sb2("GC", [64, NH, 2])
        G1s = [sb2("G1_%d" % c, [64, NH, 128], BF16) for c in range(2)]
        G2s = [sb2("G2_%d" % c, [64, NH, 128], BF16) for c in range(2)]
        XAs = [[sb2("XA%d_%d" % (j, c), [64, NH, 64], CH) for j in range(2)] for c in range(2)]
        XTs = [[sb2("XT%d_%d" % (j, c), [64, NH, 64], CH) for j in range(2)] for c in range(2)]
        Pms = [[sb2("Pm%d_%d" % (j, c), [64, NH, 64], CH) for j in range(2)] for c in range(2)]
        TTs = [None, None]
        Wn = sb2("Wn", [64, NH, 64], BF16)
        U = sb2("U", [64, NH, 64], BF16)
        yc_t = sb2("yc_t", [64, CW])
        y2 = sb2("y2", [64, CW])
        ybb = sb2("ybb", [64, CW], BF16)
        ybT = sb2("ybT", [128, 4, 128], BF16)
        ss = sb2("ss", [NS, CW])

        def prep(n, xs):
            f.act(t1[0:n, 0:64], xs[0:n, 3 * CW:3 * CW + 64], AF.Exp, scale=-2.0)
            f.ts(t1[0:n, 0:64], t1[0:n, 0:64], 1.0, ALU.add)
            f.op("dve", lambda e: e.reciprocal(t1[0:n, 0:64], t1[0:n, 0:64]), [t1], [t1])
            f.ts(lb[0:n, 0:64], t1[0:n, 0:64], 2.0, ALU.mult, -1.0, ALU.add)
            f.copy(lb[0:n, 64:128], xs[0:n, 3 * CW + 64:3 * CW + 128], en="pool")
            sigmoid_(t1[0:n, 128:256], xs[0:n, 3 * CW + 128:3 * CW + 256], n)
            f.copy(lb2[0:n], t1[0:n, 128:256], en="pool")
            f.transpose(ptb[0:64, 0:n], lb[0:n, 0:64], ident_b[0:n, 0:n])
            f.transpose(ptb[0:64, 128:128 + n], lb[0:n, 64:128], ident_b[0:n, 0:n])
            f.transpose(ptb[:, 256:256 + n], lb2[0:n, :], ident_b[0:n, 0:n])
            f.copy(lT[0:64, 0:2, 0:n], ptb[0:64, 0:256].rearrange("p (k t) -> p k t", k=2)[:, :, 0:n], en="act")
            f.copy(lT[:, 2, 0:n], ptb[:, 256:256 + n], en="act")
            ps_w, ps_a, ps_g = banks[2], banks[3], banks[4]
            f.mm(ps_w[0:n, 0:CW], lT[0:64, 0, 0:n], w2)
            f.mm(ps_a[0:n, 0:CW], lT[0:64, 1, 0:n], a2)
            f.mm(ps_g[0:n, 0:CW], lT[:, 2, 0:n], g2)
            f.tt(t1[0:n], ps_w[0:n, 0:CW], w0[0:n], ALU.add)
            sigmoid_(t1[0:n], t1[0:n], n)
            f.ts(logw[0:n], t1[0:n], -EXPM05, ALU.mult)
            f.tt(t2[0:n], ps_a[0:n, 0:CW], a0[0:n], ALU.add)
            sigmoid_(a_t[0:n], t2[0:n], n)
            f.copy(PKf[0:n, 1, :], ps_g[0:n, 0:CW], en="act")
            r_ = xs[0:n, 0:CW]
            k_ = xs[0:n, CW:2 * CW]
            v_ = xs[0:n, 2 * CW:3 * CW]
            f.copy(PKb[0:n, 0, :], v_, en="pool")
            f.tt(kk[0:n], k_, kkp[0:n], ALU.mult)
            f.tt(t1[0:n], kk[0:n], kk[0:n], ALU.mult)
            f.reduce(st8[0:n], v3(t1[0:n]))
            rsqrt_(st8[0:n], st8[0:n], eps_kk, 1.0, n)
            f.tt(v3(kk[0:n]), v3(kk[0:n]), bc3(st8[0:n], n), ALU.mult)
            f.ts(t1[0:n], a_t[0:n], -1.0, ALU.add)
            f.tt(t1[0:n], t1[0:n], kap[0:n], ALU.mult)
            f.ts(t1[0:n], t1[0:n], 1.0, ALU.add)
            f.tt(kmod[0:n], k_, t1[0:n], ALU.mult)
            f.tt(t1[0:n], r_, kmod[0:n], ALU.mult)
            f.tt(t1[0:n], t1[0:n], rkp[0:n], ALU.mult)
            f.reduce(st8b[0:n], v3(t1[0:n]))
            f.tt(v3(PKf[0:n, 0, :]), v3(v_), bc3(st8b[0:n], n), ALU.mult)

        def shift_mix(n, xs, c0, ncol, prev):
            f.tt(t2[0:n, 0:ncol], prev, xs[0:n, c0:c0 + ncol], ALU.subtract)
            f.tt(t2[0:n, 0:ncol], t2[0:n, 0:ncol], mu[0:n, c0:c0 + ncol], ALU.mult, en="pool")
            f.tt(xs[0:n, c0:c0 + ncol], t2[0:n, 0:ncol], xs[0:n, c0:c0 + ncol], ALU.add, en="pool")

        def group_out(n, y_ap, bonus_ap, g_ap, out_bf):
            f.reduce(st8[0:n], v3(y_ap))
            f.ts(st8[0:n], st8[0:n], -1.0 / 64, ALU.mult)
            f.tt(v3(yc_t[0:n]), v3(y_ap), bc3(st8[0:n], n), ALU.add)
            f.tt(y2[0:n], yc_t[0:n], yc_t[0:n], ALU.mult)
            f.reduce(st8b[0:n], v3(y2[0:n]))
            rsqrt_(st8b[0:n], st8b[0:n], eps_gn, 1.0 / 64, n)
            f.tt(v3(yc_t[0:n]), v3(yc_t[0:n]), bc3(st8b[0:n], n), ALU.mult)
            f.tt(yc_t[0:n], yc_t[0:n], lng[0:n], ALU.mult)
            f.tt(yc_t[0:n], yc_t[0:n], lnb[0:n], ALU.add)
            f.tt(yc_t[0:n], yc_t[0:n], bonus_ap, ALU.add)
            f.tt(out_bf, yc_t[0:n], g_ap, ALU.mult)

        f.dma(xs_bufs[0], pb_scr[0:128, :])
        for i in range(NTK):
            t0 = i * 128
            cb = cur_b[i % 2]
            pvb = cur_b[(i + 1) % 2]
            xs = xs_bufs[0]
            f.copy(cb[:, 0:1024], xs[:, 0:1024], en="pool")
            f.copy(cb[:, 1024:1792], xs[:, 1024:1792], en="act")
            for j, (c0, ncol) in enumerate(blocks):
                pp = banks[j % 2]
                f.mm(pp[:, 0:ncol], shiftA, cb[:, c0:c0 + ncol], start=True, stop=False)
                f.mm(pp[:, 0:ncol], shiftB, pvb[:, c0:c0 + ncol], start=False, stop=True)
                shift_mix(128, xs, c0, ncol, pp[:, 0:ncol])
            prep(128, xs)
            psL = banks[5]
            f.mm(psL[:, 0:CW], triblk, logw)
            f.act(eL, psL[:, 0:CW], AF.Exp)
            f.act(enL, psL[:, 0:CW], AF.Exp, scale=-1.0)
            f.tt(fm_b[1], xs[:, 0:CW], eL, ALU.mult)
            f.tt(t1, psL[:, 0:CW], logw, ALU.subtract)
            f.act(eL, t1, AF.Exp)
            f.tt(fm_b[0], kk, eL, ALU.mult)
            f.tt(t2, kk, a_t, ALU.mult, en="pool")
            f.tt(fm_b[2], t2, enL, ALU.mult)
            f.tt(fm_b[3], kmod, enL, ALU.mult)
            if i + 1 < NTK:
                f.dma(xs_bufs[0], pb_scr[t0 + 128:t0 + 256, :])
            f.copy(PKb[:, 1, :], fm_b[2], en="pool")
            f.copy(PKb[:, 2, :], fm_b[3], en="pool")
            for which, (src, dst, slot) in enumerate([(fm_b[0], QRT, 0), (fm_b[1], QRT, 1), (fm_b[2], BKT, 0), (fm_b[3], BKT, 1)]):
                for h in range(NH):
                    f.transpose(ptb[0:64, h * 128:(h + 1) * 128], src[:, h * 64:(h + 1) * 64], ident_b)
                f.copy(dst[:, :, :, slot, :], ptb[0:64, 0:NH * 128].rearrange("p (q c t) -> p q c t", q=NH, c=2),
                       en="act" if which % 2 else "dve")
            psG = banks[6]
            for h in range(NH):
                f.mm(psG[0:64, h * 2:(h + 1) * 2], logw[:, h * 64:(h + 1) * 64], chind)
            f.act(GC, psG[0:64, 0:2 * NH].rearrange("p (q c) -> p q c", q=NH), AF.Exp)
            f.dma(PKb1, PKb[64:128])
            f.dma(PKf1, PKf[64:128])

            def stage1(c):
                G1, G2 = G1s[c], G2s[c]
                for h in range(NH):
                    qr = QRT[:, h, c, :, :].rearrange("p w t -> p (w t)")
                    bk, cb_ = h // 4, (h % 4) * 128
                    f.mm(banks[0 + bk][0:64, cb_:cb_ + 128], BKT[:, h, c, 0, :], qr)
                    f.mm(banks[2 + bk][0:64, cb_:cb_ + 128], BKT[:, h, c, 1, :], qr)
                    f.mm(banks[4][0:64, h * 64:(h + 1) * 64], QRT[:, h, c, 0, :], BKT[:, h, c, 0, :])
                yield
                for bk in range(2):
                    f.tt(G1[:, bk * 4:(bk + 1) * 4, :].rearrange("p h t -> p (h t)"), banks[0 + bk][0:64, :],
                         maskA[:, bk * 512:(bk + 1) * 512], ALU.mult, en="dve")
                    f.tt(G2[:, bk * 4:(bk + 1) * 4, :].rearrange("p h t -> p (h t)"), banks[2 + bk][0:64, :],
                         maskB[:, bk * 512:(bk + 1) * 512], ALU.mult, en="dve")
                XA, XT, Pm = XAs[c], XTs[c], Pms[c]
                A_, AT_, P_ = XA[0], XT[0], Pm[0]
                f.copy(A_, G1[:, :, 0:64], en="pool")
                f.tt(AT_.rearrange("p h t -> p (h t)"), banks[4][0:64, 0:CW], maskC[:, 0:CW], ALU.mult)
                f.tt(P_.rearrange("p h t -> p (h t)"), A_.rearrange("p h t -> p (h t)"), eye8[:, 0:CW], ALU.add)
                yield
                psA, psAT, psP = (banks[5], banks[6], banks[4]) if c == 0 else (banks[0], banks[1], banks[2])
                pi = 0
                for k in range(1, 6):
                    An, ATn, Pn = XA[k % 2], XT[k % 2], Pm[(pi + 1) % 2]
                    for h in range(NH):
                        if k < 5:
                            f.mm(psA[0:64, h * 64:(h + 1) * 64], AT_[:, h, :], A_[:, h, :])
                        f.mm(psAT[0:64, h * 64:(h + 1) * 64], A_[:, h, :], AT_[:, h, :])
                    yield
                    if k < 5:
                        f.copy(An.rearrange("p h t -> p (h t)"), psA[0:64, 0:CW], en="act")
                    f.copy(ATn.rearrange("p h t -> p (h t)"), psAT[0:64, 0:CW], en="dve")
                    yield
                    for h in range(NH):
                        f.mm(psP[0:64, h * 64:(h + 1) * 64], ATn[:, h, :], P_[:, h, :])
                    yield
                    f.tt(Pn.rearrange("p h t -> p (h t)"), psP[0:64, 0:CW], P_.rearrange("p h t -> p (h t)"), ALU.add)
                    yield
                    A_, AT_, P_ = An, ATn, Pn
                    pi += 1
                TTs[c] = P_

            def stage2(c):
                Vc = PKb[0:64, 0, :] if c == 0 else PKb1[:, 0, :]
                Bc = PKb[0:64, 1, :] if c == 0 else PKb1[:, 1, :]
                Kc = PKb[0:64, 2, :] if c == 0 else PKb1[:, 2, :]
                bon = PKf[0:64, 0, :] if c == 0 else PKf1[:, 0, :]
                gg = PKf[0:64, 1, :] if c == 0 else PKf1[:, 1, :]
                G1, G2, TT = G1s[c], G2s[c], TTs[c]
                psW, psU, psY, psS = banks[3], banks[4], banks[5], banks[6]
                for h in range(NH):
                    f.mm(psW[0:64, h * 64:(h + 1) * 64], QRT[:, h, c, 0, :], STb[:, h, :], start=True, stop=False)
                    f.mm(psW[0:64, h * 64:(h + 1) * 64], G2[:, h, 0:64], Vc[:, h * 64:(h + 1) * 64], start=False, stop=True)
                f.ts(Wn.rearrange("p h t -> p (h t)"), psW[0:64, 0:CW], -1.0, ALU.mult)
                for h in range(NH):
                    f.mm(psU[0:64, h * 64:(h + 1) * 64], TT[:, h, :], Wn[:, h, :])
                f.copy(U.rearrange("p h t -> p (h t)"), psU[0:64, 0:CW], en="act")
                for h in range(NH):
                    o_ = psY[0:64, h * 64:(h + 1) * 64]
                    f.mm(o_, QRT[:, h, c, 1, :], STb[:, h, :], start=True, stop=False)
                    f.mm(o_, G1[:, h, 64:128], U[:, h, :], start=False, stop=False)
                    f.mm(o_, G2[:, h, 64:128], Vc[:, h * 64:(h + 1) * 64], start=False, stop=True)
                for h in range(NH):
                    o_ = psS[0:64, h * 64:(h + 1) * 64]
                    f.mm(o_, Bc[:, h * 64:(h + 1) * 64], U[:, h, :], start=True, stop=False)
                    f.mm(o_, Kc[:, h * 64:(h + 1) * 64], Vc[:, h * 64:(h + 1) * 64], start=False, stop=True)
                f.tt(ST, ST, psS[0:64, 0:CW].rearrange("p (h d) -> p h d", h=NH), ALU.add)
                f.tt(ST, ST, GC[:, :, c].unsqueeze(2).broadcast_to([64, NH, 64]), ALU.mult)
                f.copy(STb, ST, en="pool")
                group_out(64, psY[0:64, 0:CW], bon, gg, ybb)
                for k in range(4):
                    f.transpose(ptb[:, k * 64:(k + 1) * 64], ybb[:, k * 128:(k + 1) * 128], ident_b[0:64, 0:64])
                f.copy(ybT[:, :, c * 64:(c + 1) * 64], ptb[:, 0:256].rearrange("p (k t) -> p k t", k=4), en="act")

            gens = [stage1(0), stage1(1)]
            alive = [True, True]
            next(gens[0])
            next(gens[0])
            while any(alive):
                for gi in (1, 0):
                    g = gens[gi]
                    if alive[gi]:
                        try:
                            next(g)
                        except StopIteration:
                            alive[gi] = False
            stage2(0)
            stage2(1)
            for half in range(2):
                ph = banks[5 + half]
                for mm_ in range(4):
                    m = half * 4 + mm_
                    for k in range(4):
                        f.mm(ph[:, mm_ * 128:(mm_ + 1) * 128], woB[:, k, m * 128:(m + 1) * 128], ybT[:, k, :],
                             start=(k == 0), stop=(k == 3))
                f.tt(hT[:, half * 4:(half + 1) * 4, t0:t0 + 128], hT[:, half * 4:(half + 1) * 4, t0:t0 + 128],
                     ph.rearrange("p (c t) -> p c t", c=4), ALU.add)
        for h in range(NH):
            f.transpose(banks[0][0:64, h * 64:(h + 1) * 64], ST[:, h, :], ident_f[0:64, 0:64])
        f.copy(y2, banks[0][0:64, 0:CW])
        f.dma(O["wkv_p"][l].rearrange("h i j -> i h j"), y2.rearrange("i (h j) -> i h j", h=NH))

        n = NS
        xs = xs_bufs[0]
        f.dma(xs[0:n, :], pb_scr[T:T + n, :])
        for (c0, ncol) in blocks:
            f.dma(ss[:, 0:ncol], I["sshift"][l][:, c0:c0 + ncol])
            shift_mix(n, xs, c0, ncol, ss[:, 0:ncol])
        prep(n, xs)
        f.act(t1[0:n], logw[0:n], AF.Exp)
        f.tt(t2[0:n], kk[0:n], a_t[0:n], ALU.mult)
        for q, src in enumerate([xs[0:n, 0:CW], t1[0:n], kmod[0:n], xs[0:n, 2 * CW:3 * CW], kk[0:n], t2[0:n]]):
            f.dma(scr_b[:, :, q, :], src.rearrange("s (h d) -> s h d", h=8))
        f.dma(scr_f, PKf[0:n])
        f.barrier()
      f.recs.clear()

    with contextlib.ExitStack() as ps_:
        def sb3(name, shape, dt=F32):
            return ps_.enter_context(nc.sbuf_tensor(name + "_s%d" % l, list(shape), dt)).ap()
        n = NS
        NP = NS * 8
        woBf = sb3("woBf", [128, 4, D], BF16)
        f.dma(woBf, W["w_out"][l][256:768, :].rearrange("(k p) n -> p k n", p=128), q="pool")
        lngf = sb3("lngf", [NS, 512]); bcload(lngf, W["b_ln_g"][l])
        lnbf = sb3("lnbf", [NS, 512]); bcload(lnbf, W["b_ln_b"][l])
        bg = sb3("bg", [NS, 2, 512])
        f.dma(bg, scr_f)
        pkp = sb3("pkp", [NP, 8, 64])
        f.dma(pkp[:, 0:6, :], scr_b.rearrange("s h q d -> (s h) q d")[:, 0:6, :])
        S = sb3("S", [NP, 64, 64])
        tmp = sb3("tmpS", [NP, 64, 64])
        f.dma(S.rearrange("p i j -> p (i j)"), I["swkv"][l])
        r_p, w_p, k_p, v_p, kk_p, b_p = (pkp[:, q, :] for q in range(6))

        def bj(ap):
            return ap.unsqueeze(1).broadcast_to([NP, 64, 64])

        def bi(ap):
            return ap.unsqueeze(2).broadcast_to([NP, 64, 64])

        sa = sb3("sa", [NP, 64])
        ysm = sb3("ysm", [NP, 64])
        f.tt(tmp, S, bj(kk_p), ALU.mult)
        f.reduce(sa, tmp)
        f.tt(S, S, bj(w_p), ALU.mult)
        f.tt(tmp, bi(sa), bj(b_p), ALU.mult, en="pool")
        f.tt(S, S, tmp, ALU.subtract)
        f.tt(tmp, bi(v_p), bj(k_p), ALU.mult, en="pool")
        f.tt(S, S, tmp, ALU.add)
        f.dma(O["wkv_s"][l], S.rearrange("p i j -> p (i j)"))
        f.tt(tmp, S, bj(r_p), ALU.mult)
        f.reduce(ysm, tmp)
        f.dma(scr_y, ysm)
        ytm = sb3("ytm", [NS, 512])
        f.dma(ytm, scr_y.rearrange("(s h) d -> s (h d)", h=8))
        st8 = sb3("st8", [NS, 8]); st8b = sb3("st8b", [NS, 8])
        yc_t = sb3("yc_t", [NS, 512]); y2 = sb3("y2", [NS, 512]); ybb = sb3("ybb", [NS, 512], BF16)
        v8 = lambda ap: ap.rearrange("p (h d) -> p h d", h=8)
        b8 = lambda ap: ap.unsqueeze(2).broadcast_to([n, 8, 64])
        f.reduce(st8, v8(ytm))
        f.ts(st8, st8, -1.0 / 64, ALU.mult)
        f.tt(v8(yc_t), v8(ytm), b8(st8), ALU.add)
        f.tt(y2, yc_t, yc_t, ALU.mult)
        f.reduce(st8b, v8(y2))
        rsqrt_(st8b, st8b, eps_gn, 1.0 / 64, n)
        f.tt(v8(yc_t), v8(yc_t), b8(st8b), ALU.mult)
        f.tt(yc_t, yc_t, lngf, ALU.mult)
        f.tt(yc_t, yc_t, lnbf, ALU.add)
        f.tt(yc_t, yc_t, bg[:, 0, :], ALU.add)
        f.tt(ybb, yc_t, bg[:, 1, :], ALU.mult)
        ybT_s = sb3("ybT_s", [128, 4, NS], BF16)
        for k in range(4):
            f.transpose(ptb[:, k * NS:(k + 1) * NS], ybb[:, k * 128:(k + 1) * 128], ident_b[0:n, 0:n])
        f.copy(ybT_s, ptb[:, 0:4 * NS].rearrange("p (k t) -> p k t", k=4), en="act")
        ph = banks[5]
        for m in range(8):
            for k in range(4):
                f.mm(ph[:, m * NS:(m + 1) * NS], woBf[:, k, m * 128:(m + 1) * 128], ybT_s[:, k, :], start=(k == 0), stop=(k == 3))
        f.tt(hT[:, :, T:T + NS], hT[:, :, T:T + NS], ph[:, 0:8 * NS].rearrange("p (c t) -> p c t", c=8), ALU.add)
        f.barrier()
    f.recs.clear()


from concourse.bass_utils import run_bass_kernel_spmd

_cache = {}

def run(inputs, L, T, NS_total, NPG, stages=("A", "B", "C", "F", "P"), trace=False):
    NC = 8
    NS = NS_total // NC
    NPHYS = inputs["cache_k"].shape[1]
    wsh = {n: list(inputs[n].shape) for n in WNAMES}
    key = (L, T, NS, NPG, NPHYS, tuple(stages))
    if key not in _cache:
        _cache[key] = build(L, T, NS, NPG, NPHYS, wsh, stages)[0]
    nc = _cache[key]
    cn = make_consts(NPG)
    f32 = lambda a: np.ascontiguousarray(a, dtype=np.float32)
    ck = f32(inputs["cache_k"]).reshape(L, NPHYS, 128 * 256)
    cv = f32(inputs["cache_v"]).reshape(L, NPHYS, 128 * 256)
    wts = {n: f32(inputs[n]) for n in WNAMES}
    cns = {"c_" + n: f32(v) for n, v in cn.items()}
    in_maps = []
    for c in range(NC):
        sl = slice(c * NS, (c + 1) * NS)
        m = {
            "xp": f32(inputs["x_prompt"][c]),
            "xs": f32(inputs["x_sample"][sl, 0]),
            "cache_k": ck, "cache_v": cv,
            "swkv": f32(inputs["state_wkv"][:, sl]).reshape(L, NS * 8, 4096),
            "sshift": f32(inputs["state_shift"][:, sl]),
            "ptab": np.ascontiguousarray(inputs["page_table"][sl]).astype(np.int32).reshape(NS * NPG, 1),
            "pp": f32(inputs["p_prompt"][:, c]),
            "psm": f32(inputs["p_sample"][:, sl, 0]),
        }
        m.update(wts)
        m.update(cns)
        in_maps.append(m)
    res = run_bass_kernel_spmd(nc, in_maps, core_ids=list(range(NC)), **({"trace": True} if trace else {}))
    R = res.results
    cat = lambda k, ax: np.concatenate([np.asarray(r[k]) for r in R], axis=ax)
    stk = lambda k, ax: np.stack([np.asarray(r[k]) for r in R], axis=ax)
    y_p = stk("y_p", 0)
    y_s = cat("y_s", 0)[:, None, :]
    k_p = stk("k_p", 1).reshape(L, NC, T, 4, 64)
    v_p = stk("v_p", 1).reshape(L, NC, T, 4, 64)
    wkv_p = stk("wkv_p", 1)
    shift_p = stk("shift_p", 1)
    k_s = cat("k_s", 1).reshape(L, NS_total, 1, 4, 64)
    v_s = cat("v_s", 1).reshape(L, NS_total, 1, 4, 64)
    wkv_s = cat("wkv_s", 1).reshape(L, NS_total, 8, 64, 64)
    shift_s = cat("shift_s", 1)
    av_s = cat("av_s", 1)[:, :, None, :]
    out = (y_p, y_s, k_p, v_p, wkv_p, shift_p, k_s, v_s, wkv_s, shift_s, av_s)
    return tuple(np.ascontiguousarray(o, dtype=np.float32) for o in out), res


def kernel(**inputs):
    inputs = {k: np.asarray(v) for k, v in inputs.items()}
    L = inputs["w_in"].shape[0]
    T = inputs["x_prompt"].shape[1]
    NS_total = inputs["x_sample"].shape[0]
    NPG = inputs["page_table"].shape[1]
    out, _ = run(inputs, L, T, NS_total, NPG)
    return out
```

```python
import numpy as np
import concourse.bass as bass
import concourse.mybir as mybir

F32 = mybir.dt.float32
BF16 = mybir.dt.bfloat16
I32 = mybir.dt.int32
AF = mybir.ActivationFunctionType
ALU = mybir.AluOpType
AX = mybir.AxisListType


class _Rec:
    __slots__ = ("lw", "rd")

    def __init__(self):
        self.lw = None
        self.rd = {}


class _Sem:
    def __init__(self, h, name):
        self.h = h
        self.name = name
        self.n = 0


class K:
    __slots__ = ("ap", "key")

    def __init__(self, ap, key):
        self.ap = ap
        self.key = key


def _ap(x):
    return x.ap if isinstance(x, K) else x


def _key(x):
    if isinstance(x, K):
        return x.key
    return x.tensor.name


class FW:
    def __init__(self, nc, n_dma_sems=16):
        self.nc = nc
        self.recs = {}
        self.engs = {}
        for nm, e in (("pe", nc.tensor), ("act", nc.scalar), ("dve", nc.vector),
                      ("pool", nc.gpsimd), ("sp", nc.sync)):
            s = _Sem(nc.alloc_semaphore("s_" + nm), nm)
            self.engs[nm] = (e, s)
        self.waited = {nm: {} for nm in self.engs}
        self.dsems = {}
        self.dnext = {}
        self.sems = {s.name: s for (_, s) in self.engs.values()}
        for q in ("sp", "pool", "act"):
            self.dsems[q] = [_Sem(nc.alloc_semaphore("d%s%d" % (q, i)), "d%s%d" % (q, i)) for i in range(n_dma_sems)]
            self.dnext[q] = 0
            for s in self.dsems[q]:
                self.sems[s.name] = s
        self.n_inst = 0
        self.n_pe = 0
        self.psum_names = set()

    def rec(self, key):
        r = self.recs.get(key)
        if r is None:
            r = self.recs[key] = _Rec()
        return r

    def _need(self, en, writes, reads):
        need = {}

        def add(sv, same_ok):
            if sv is None:
                return
            sn, v = sv
            if sn == en and same_ok:
                return
            if need.get(sn, 0) < v:
                need[sn] = v

        pe = en == "pe"
        for x in reads:
            r = self.rec(_key(x))
            add(r.lw, pe)
            if _ap(x).tensor.name in self.psum_names:
                for sn, v in r.rd.items():
                    if sn != en:
                        add((sn, v), False)
        for x in writes:
            r = self.rec(_key(x))
            add(r.lw, pe)
            for sn, v in r.rd.items():
                add((sn, v), pe)
        return need

    def _emit_waits(self, en, need):
        e, _ = self.engs[en]
        w = self.waited[en]
        for sn, v in need.items():
            if w.get(sn, 0) >= v:
                continue
            e.wait_ge(self.sems[sn].h, v)
            w[sn] = v

    def op(self, en, fn, writes, reads, inc=True):
        need = self._need(en, writes, reads)
        self._emit_waits(en, need)
        e, s = self.engs[en]
        ins = fn(e)
        if inc:
            s.n += 1
            ins.then_inc(s.h, 1)
            cid = s.n
        else:
            cid = s.n + 1
        self.n_inst += 1
        if en == "pe":
            self.n_pe += 1
        for x in reads:
            self.rec(_key(x)).rd[en] = cid
        for x in writes:
            r = self.rec(_key(x))
            r.lw = (en, cid)
            r.rd = {}
        return ins

    def dma(self, out, in_, q="sp", fn=None, extra_reads=()):
        writes = [out]
        reads = [in_] + list(extra_reads)
        need = self._need("dma", writes, reads)
        ds = self.dsems[q][self.dnext[q]]
        self.dnext[q] = (self.dnext[q] + 1) % len(self.dsems[q])
        if ds.n > 0:
            need[ds.name] = max(need.get(ds.name, 0), ds.n)
        self._emit_waits(q, need)
        e, _ = self.engs[q]
        if fn is None:
            ins = e.dma_start(out=_ap(out), in_=_ap(in_))
        else:
            ins = fn(e)
        ds.n += 16
        ins.then_inc(ds.h, 16)
        self.n_inst += 1
        for x in reads:
            self.rec(_key(x)).rd[ds.name] = ds.n
        r = self.rec(_key(out))
        r.lw = (ds.name, ds.n)
        r.rd = {}
        return ins

    def barrier(self):
        vals = {s.name: s.n for s in self.sems.values() if s.n > 0}
        for en in self.engs:
            self._emit_waits(en, dict(vals))

    def finish(self):
        vals = {s.name: s.n for s in self.sems.values() if s.n > 0}
        self._emit_waits("sp", vals)

    def mm(self, out, lhsT, rhs, start=True, stop=True, **kw):
        return self.op("pe", lambda e: e.matmul(_ap(out), _ap(lhsT), _ap(rhs), start=start, stop=stop, **kw),
                       [out], [lhsT, rhs] + ([] if start else [out]), inc=bool(stop))

    def transpose(self, out, in_, ident):
        return self.op("pe", lambda e: e.transpose(_ap(out), _ap(in_), _ap(ident)), [out], [in_, ident])

    def act(self, out, in_, func, bias=None, scale=1.0, en="act", accum_out=None):
        reads = [in_]
        kw = {}
        if bias is not None:
            if not isinstance(bias, (int, float)):
                reads.append(bias)
                kw["bias"] = _ap(bias)
            else:
                kw["bias"] = float(bias)
        if not isinstance(scale, (int, float)):
            reads.append(scale)
            kw["scale"] = _ap(scale)
        else:
            kw["scale"] = float(scale)
        writes = [out]
        if accum_out is not None:
            writes.append(accum_out)
            kw["accum_out"] = _ap(accum_out)
        return self.op(en, lambda e: e.activation(_ap(out), _ap(in_), func, **kw), writes, reads)

    def tt(self, out, a, b, op, en="dve"):
        return self.op(en, lambda e: e.tensor_tensor(_ap(out), _ap(a), _ap(b), op), [out], [a, b])

    def ts(self, out, a, s1, op0, s2=None, op1=None, en="dve", accum_out=None):
        reads = [a]
        v1 = s1
        v2 = s2
        if not isinstance(s1, (int, float)):
            reads.append(s1)
            v1 = _ap(s1)
        if s2 is not None and not isinstance(s2, (int, float)):
            reads.append(s2)
            v2 = _ap(s2)
        kw = {}
        writes = [out]
        if accum_out is not None:
            writes.append(accum_out)
            kw["accum_out"] = _ap(accum_out)
        if op1 is None:
            return self.op(en, lambda e: e.tensor_scalar(_ap(out), _ap(a), v1, None, op0, **kw), writes, reads)
        return self.op(en, lambda e: e.tensor_scalar(_ap(out), _ap(a), v1, v2, op0, op1, **kw), writes, reads)

    def stt(self, out, a, s, b, op0, op1, en="dve"):
        reads = [a, b]
        sv = s
        if not isinstance(s, (int, float)):
            reads.append(s)
            sv = _ap(s)
        return self.op(en, lambda e: e.scalar_tensor_tensor(_ap(out), _ap(a), sv, _ap(b), op0, op1), [out], reads)

    def copy(self, out, in_, en="dve"):
        if en == "act":
            return self.op(en, lambda e: e.copy(_ap(out), _ap(in_)), [out], [in_])
        return self.op(en, lambda e: e.tensor_copy(_ap(out), _ap(in_)), [out], [in_])

    def reduce(self, out, in_, op=None, axis=None, en="dve"):
        op = op or ALU.add
        axis = axis or AX.X
        return self.op(en, lambda e: e.tensor_reduce(_ap(out), _ap(in_), axis, op), [out], [in_])

    def memset(self, out, val, en="pool"):
        return self.op(en, lambda e: e.memset(_ap(out), val), [out], [])


import math
import os
import numpy as np

D = 1024
HD = 64
DFF = 2816
SB_SCALE = HD ** -0.5
EXPM05 = math.exp(-0.5)


def make_consts(NPG):
    c = {}
    i = np.arange(128)
    c["ident"] = np.eye(128, dtype=np.float32)
    c["ones"] = np.ones((128, 128), np.float32)
    c["ntincl"] = -(i[:, None] >= i[None, :]).astype(np.float32)
    ms = (i[:, None] < i[None, :]).astype(np.float32)
    c["mstrict"] = np.tile(ms, (1, 4))
    c["triljt"] = (i[:, None] <= i[None, :]).astype(np.float32)
    s = np.arange(64)
    su = (s[:, None] < s[None, :]).astype(np.float32)
    iu = (s[:, None] <= s[None, :]).astype(np.float32)
    c["maskA"] = np.tile(np.concatenate([-su, iu], 1)[:, None, :], (1, 8, 1)).reshape(64, 1024)
    c["maskB"] = np.tile(np.concatenate([su, iu], 1)[:, None, :], (1, 8, 1)).reshape(64, 1024)
    sl = -(s[None, :] < s[:, None]).astype(np.float32)
    c["maskC"] = np.tile(sl[:, None, :], (1, 8, 1)).reshape(64, 512)
    c["eye8"] = np.tile(np.eye(64, dtype=np.float32)[:, None, :], (1, 8, 1)).reshape(64, 512)
    c["triblk"] = ((i[:, None] // 64 == i[None, :] // 64) & (i[:, None] <= i[None, :])).astype(np.float32)
    c["chind"] = (i[:, None] // 64 == np.arange(2)[None, :]).astype(np.float32)
    c["shiftA"] = (i[None, :] == i[:, None] + 1).astype(np.float32)
    sb = np.zeros((128, 128), np.float32)
    sb[127, 0] = 1.0
    c["shiftB"] = sb
    G = min(128, 16 * NPG)
    g = np.arange(G)
    c["pgsuf"] = ((g[:, None] // NPG == g[None, :] // NPG) & (g[:, None] > g[None, :])).astype(np.float32)
    SG = G // NPG
    c["seqind"] = (g[:, None] // NPG == np.arange(SG)[None, :]).astype(np.float32)
    return c


WNAMES = ["mix_norm", "w_in", "a_ln_g", "a_ln_b", "a_ws", "a_bs", "b_mu", "b_w0", "b_w2", "b_a0", "b_a2", "b_g2",
          "b_kk", "b_ka", "b_rk", "b_ln_g", "b_ln_b", "c_qn", "c_kn", "c_bias", "w_out", "ffn_norm", "w_gate",
          "w_up", "w_down", "ple_norm", "w_ple_gate", "w_ple"]


def build(L, T, NS, NPG, NPHYS, wshapes, stages=("A", "B", "C", "F", "P")):
    nc = bass.Bass("TRN2", target_bir_lowering=False)
    NTK = T // 128
    NT = T + NS
    G = min(128, NS * NPG)
    SG = G // NPG
    NG = NS // SG

    def din(name, shape, dt=F32):
        return nc.dram_tensor(name, list(shape), dt, kind="ExternalInput").ap()

    def dout(name, shape):
        return nc.dram_tensor(name, list(shape), F32, kind="ExternalOutput").ap()

    def dscr(name, shape, dt=F32):
        return nc.dram_tensor(name, list(shape), dt, kind="Internal").ap()

    I = {}
    I["xp"] = din("xp", [T, D])
    I["xs"] = din("xs", [NS, D])
    I["cache_k"] = din("cache_k", [L, NPHYS, 128 * 256])
    I["cache_v"] = din("cache_v", [L, NPHYS, 128 * 256])
    I["swkv"] = din("swkv", [L, NS * 8, 4096])
    I["sshift"] = din("sshift", [L, NS, 1792])
    I["ptab"] = din("ptab", [NS * NPG, 1], I32)
    I["pp"] = din("pp", [L, T, 256])
    I["psm"] = din("psm", [L, NS, 256])
    W = {n: din(n, wshapes[n]) for n in WNAMES}
    CN = make_consts(NPG)
    C = {n: din("c_" + n, CN[n].shape) for n in CN}

    O = {}
    O["y_p"] = dout("y_p", [T, D])
    O["y_s"] = dout("y_s", [NS, D])
    O["k_p"] = dout("k_p", [L, T, 256])
    O["v_p"] = dout("v_p", [L, T, 256])
    O["wkv_p"] = dout("wkv_p", [L, 8, 64, 64])
    O["shift_p"] = dout("shift_p", [L, 1792])
    O["k_s"] = dout("k_s", [L, NS, 256])
    O["v_s"] = dout("v_s", [L, NS, 256])
    O["wkv_s"] = dout("wkv_s", [L, NS * 8, 4096])
    O["shift_s"] = dout("shift_s", [L, NS, 1792])
    O["av_s"] = dout("av_s", [L, NS, 256])

    scr_q = dscr("scr_q", [NS, 256])
    scr_b = dscr("scr_b", [NS, 8, 8, 64])
    scr_y = dscr("scr_y", [NS * 8, 64])
    scr_f = dscr("scr_f", [NS, 2, 512])

    f = FW(nc)
    f.marks = []
    def mark(lbl):
        f.marks.append((lbl, f.n_pe))
    import contextlib
    es = contextlib.ExitStack()
    es.enter_context(nc.allow_non_contiguous_dma(reason="small parameter loads"))

    def sb(name, shape, dt=F32):
        return nc.alloc_sbuf_tensor(name, list(shape), dt).ap()

    banks = [nc.alloc_psum_tensor("bank%d" % i, [128, 512], F32).ap() for i in range(7)]
    ptb_ = nc.alloc_psum_tensor("ptb", [128, 1024], BF16).ap()

    def bankb(i):
        assert i == 7
        return ptb_

    f.psum_names = {b.tensor.name for b in banks} | {ptb_.tensor.name}

    ident_f = sb("ident_f", [128, 128]); f.dma(ident_f, C["ident"])
    ident_b = sb("ident_b", [128, 128], BF16); f.dma(ident_b, C["ident"], q="pool")
    ones_b = sb("ones_b", [128, 128], BF16); f.dma(ones_b, C["ones"], q="pool")
    negones_b = sb("negones_b", [128, 128], BF16)
    ntincl_b = sb("ntincl_b", [128, 128], BF16); f.dma(ntincl_b, C["ntincl"], q="pool")
    mstrict_b = sb("mstrict_b", [128, 512], BF16); f.dma(mstrict_b, C["mstrict"], q="pool")
    triljt = sb("triljt", [128, 128]); f.dma(triljt, C["triljt"])
    maskA = sb("maskA", [64, 1024]); f.dma(maskA, C["maskA"])
    maskB = sb("maskB", [64, 1024]); f.dma(maskB, C["maskB"])
    maskC = sb("maskC", [64, 512]); f.dma(maskC, C["maskC"])
    eye8 = sb("eye8", [64, 512]); f.dma(eye8, C["eye8"])
    triblk = sb("triblk", [128, 128]); f.dma(triblk, C["triblk"])
    chind = sb("chind", [128, 2]); f.dma(chind, C["chind"])
    shiftA = sb("shiftA", [128, 128], BF16); f.dma(shiftA, C["shiftA"], q="pool")
    shiftB = sb("shiftB", [128, 128], BF16); f.dma(shiftB, C["shiftB"], q="pool")
    pgsuf = sb("pgsuf", [G, G]); f.dma(pgsuf, C["pgsuf"])
    seqind = sb("seqind", [G, SG]); f.dma(seqind, C["seqind"])
    f.ts(negones_b, ones_b, -1.0, ALU.mult)
    onesf = sb("onesf", [128, 128]); f.dma(onesf, C["ones"])
    eps_rms = sb("eps_rms", [128, 1]); f.memset(eps_rms, 1e-6)
    eps_ln = sb("eps_ln", [128, 1]); f.memset(eps_ln, 1e-5)
    eps_gn = sb("eps_gn", [128, 1]); f.memset(eps_gn, 64e-5)
    eps_kk = sb("eps_kk", [128, 1]); f.memset(eps_kk, 1e-12)
    one_c = sb("one_c", [128, 1]); f.memset(one_c, 1.0)
    zero_c = sb("zero_c", [128, 1]); f.memset(zero_c, 0.0)

    hT = sb("hT", [128, 8, NT])
    class _Holder:
        pass
    hn = _Holder()
    hn.cm = None

    def hn_alloc(tag):
        hn.cm = nc.sbuf_tensor("hnT_" + tag, [128, 8, NT], BF16)
        hn.t = hn.cm.__enter__().ap()

    def hn_free():
        f.barrier()
        hn.cm.__exit__(None, None, None)
        hn.cm = None
        f.recs.clear()

    pb_scr = dscr("pb_scr", [NT, 1792])

    def rsqrt_(out, in_, eps_t, scale, np_):
        f.act(out, in_, AF.Ln, bias=eps_t[0:np_, :], scale=scale)
        f.act(out, out, AF.Exp, scale=-0.5)

    def sigmoid_(out, in_, np_, scale=1.0, tmp=None):
        f.act(out, in_, AF.Exp, scale=-scale)
        f.ts(out, out, 1.0, ALU.add)
        f.op("dve", lambda e: e.reciprocal(_ap(out), _ap(out)), [out], [out])

    tblocks = []
    c0 = 0
    while c0 < NT:
        n = min(512, NT - c0)
        tblocks.append((c0, n))
        c0 += n

    xt_cm = nc.sbuf_tensor("xt", [128, D], F32)
    xt = xt_cm.__enter__().ap()
    for i in range(NTK + 1):
        n = 128 if i < NTK else NS
        src = I["xp"][i * 128:(i + 1) * 128, :] if i < NTK else I["xs"]
        f.dma(xt[0:n, :], src)
        for half in range(2):
            pst = banks[half]
            for c in range(4):
                m = half * 4 + c
                f.transpose(pst[:, c * 128:c * 128 + n], xt[0:n, m * 128:(m + 1) * 128], ident_f[0:n, 0:n])
            f.copy(hT[:, half * 4:(half + 1) * 4, i * 128:i * 128 + n],
                   pst.rearrange("p (c t) -> p c t", c=4)[:, :, 0:n], en="act" if half else "dve")
    f.barrier()
    xt_cm.__exit__(None, None, None)
    f.recs.clear()

    def rmsnorm_all(gname, l):
        tag = "_%s_%d" % (gname, l)
        with nc.sbuf_tensor("gvec" + tag, [128, 8], F32) as gvec_h, nc.sbuf_tensor("sqb" + tag, [128, 8, 512], BF16) as sqb_h, \
                nc.sbuf_tensor("rstd_t" + tag, [128, 512], F32) as rstd_h:
            _rmsnorm_body(gname, l, gvec_h.ap(), sqb_h.ap(), rstd_h.ap())
            f.barrier()
        f.recs.clear()

    def _rmsnorm_body(gname, l, gvec, sqb, rstd_t):
        f.dma(gvec, W[gname][l].rearrange("(c p) -> p c", p=128))
        for (c0, n) in tblocks:
            f.act(sqb[:, :, 0:n], hT[:, :, c0:c0 + n], AF.Square)
            ps = banks[0]
            for k in range(8):
                f.mm(ps[:, 0:n], ones_b, sqb[:, k, 0:n], start=(k == 0), stop=(k == 7))
            rsqrt_(rstd_t[:, 0:n], ps[:, 0:n], eps_rms, 1.0 / D, 128)
            for k in range(8):
                f.stt(hn.t[:, k, c0:c0 + n], hT[:, k, c0:c0 + n], gvec[:, k:k + 1], rstd_t[:, 0:n],
                      ALU.mult, ALU.mult)

    def bcload(dst, row_ap, q="sp"):
        P = dst.shape[0]
        src = row_ap.unsqueeze(0).broadcast_to([P] + list(row_ap.shape))
        f.dma(dst, src, q=q)

    def proj_T(ps, ntok, col0, wt, wc0, wc1):
        for k in range(8):
            f.mm(ps[0:ntok, 0:wc1 - wc0], hn.t[:, k, col0:col0 + ntok], wt[:, k, wc0:wc1], start=(k == 0), stop=(k == 7))

    def add_h(m, c0, n, ps):
        f.tt(hT[:, m, c0:c0 + n], hT[:, m, c0:c0 + n], ps, ALU.add)

    for l in range(L):
        mark('L%d norm' % l)
        hn_alloc('m%d' % l)
        rmsnorm_all("mix_norm", l)
        mark('L%d pass1-prompt' % l)
        f.barrier()
        with contextlib.ExitStack() as p1:
            def sb1(name, shape, dt=F32):
                return p1.enter_context(nc.sbuf_tensor(name + "_%d" % l, list(shape), dt)).ap()
            p1a = contextlib.ExitStack()
            p1b = contextlib.ExitStack()

            def sb1a(name, shape, dt=F32):
                return p1a.enter_context(nc.sbuf_tensor(name + "_%d" % l, list(shape), dt)).ap()

            def sb1b(name, shape, dt=F32):
                return p1b.enter_context(nc.sbuf_tensor(name + "_%d" % l, list(shape), dt)).ap()
            wAC = sb1("wAC", [128, 8, 1280], BF16)
            win = W["w_in"][l].rearrange("(k p) n -> p k n", p=128)
            f.dma(wAC[:, :, 0:512], win[:, :, 0:512], q="pool")
            f.dma(wAC[:, :, 512:1280], win[:, :, 2304:3072], q="pool")
            woA = sb1("woA", [128, 2, D], BF16)
            f.dma(woA, W["w_out"][l][0:256, :].rearrange("(k p) n -> p k n", p=128), q="pool")
            lng = sb1("lng", [128, 256]); bcload(lng, W["a_ln_g"][l])
            lnb = sb1("lnb", [128, 256]); bcload(lnb, W["a_ln_b"][l])
            bs_t = sb1("bs_t", [128, 4]); f.dma(bs_t, W["a_bs"][l].rearrange("g i -> i g"))
            ws0 = sb1("ws0", [NS, 4]); f.dma(ws0, W["a_ws"][l][:, 0, 0:1].rearrange("g o -> o g").broadcast_to([NS, 4]))
            bs0 = sb1("bs0", [NS, 4]); f.dma(bs0, W["a_bs"][l][:, 0:1].rearrange("g o -> o g").broadcast_to([NS, 4]))
            gqk = sb1("gqk", [128, 512])
            f.dma(gqk[:, 0:256].rearrange("p (h d) -> p h d", h=4),
                  W["c_qn"][l].unsqueeze(0).unsqueeze(0).broadcast_to([128, 4, 64]))
            f.dma(gqk[:, 256:512].rearrange("p (h d) -> p h d", h=4),
                  W["c_kn"][l].unsqueeze(0).unsqueeze(0).broadcast_to([128, 4, 64]))
            f.ts(gqk[:, 0:256], gqk[:, 0:256], SB_SCALE, ALU.mult)
            cb1 = sb1("cb1", [1, 4]); f.dma(cb1, W["c_bias"][l].unsqueeze(0))
            biasrow = sb1("biasrow", [1, 4, 128], BF16)
            f.copy(biasrow, cb1.unsqueeze(2).broadcast_to([1, 4, 128]))
            cbG = sb1("cbG", [G, 4]); bcload(cbG, W["c_bias"][l])

            u_sb = sb1("u_sb", [128, 256])
            vc = sb1("vc", [128, 256])
            st1 = sb1("st1", [128, 8])
            st2 = sb1("st2", [128, 8])
            ya = sb1("ya", [128, 256])
            yab = sb1("yab", [128, 256], BF16)
            sq1 = sb1("sq1", [128, 512])
            qkn = sb1("qkn", [128, 512])
            qkb = sb1("qkb", [128, 512], BF16)
            vf = sb1("vf", [128, 256])
            woC = sb1a("woC", [64, 4, D], BF16)
            WmT = sb1a("WmT", [128, 4, 128], BF16)
            wsl = sb1a("wsl", [128, 4, 128])
            KT = sb1a("KT", [128, 2, T], BF16)
            Vall = sb1a("Vall", [128, NTK, 256], BF16)
            QT = sb1a("QT", [128, 2, 128], BF16)
            vnb = sb1a("vnb", [128, 256], BF16)
            yaT = sb1a("yaT", [128, 2, 128], BF16)
            e_t2 = [sb1a("e_t%d" % j, [128, 512], BF16) for j in range(2)]
            sp_all = sb1a("sp_all", [128, NTK, 512], BF16)
            att_t2 = [sb1a("att_t%d" % j, [128, 512], BF16) for j in range(2)]
            Cs2 = [sb1a("Cs%d" % j, [128, 512], BF16) for j in range(2)]
            ycT = sb1a("ycT", [64, 512], BF16)
            f.dma(woC, W["w_out"][l][768:1024, :].rearrange("(h p) n -> p h n", p=64), q="pool")
            f.dma(wsl, W["a_ws"][l].rearrange("g i j -> i g j"))
            for g in range(4):
                f.transpose(banks[0][:, g * 128:(g + 1) * 128], wsl[:, g, :], ident_f)
            f.tt(WmT, banks[0].rearrange("p (g i) -> p g i", g=4), triljt.unsqueeze(1).broadcast_to([128, 4, 128]), ALU.mult)

            def mixer_A(n, col0, sample):
                ps = banks[0]
                BIS3 = int(os.environ.get("BIS3", "99"))
                proj_T(ps, n, col0, wAC, 0, 512)
                if BIS3 < 1: return
                f.copy(u_sb[0:n], ps[0:n, 0:256], en="act")
                f.reduce(st1[0:n, 0:1], ps[0:n, 256:512])
                f.ts(st1[0:n, 0:1], st1[0:n, 0:1], -1.0 / 256, ALU.mult)
                if BIS3 < 2: return
                f.ts(vc[0:n], ps[0:n, 256:512], st1[0:n, 0:1], ALU.add)
                f.tt(ya[0:n], vc[0:n], vc[0:n], ALU.mult)
                f.reduce(st1[0:n, 1:2], ya[0:n])
                if BIS3 < 3: return
                rsqrt_(st1[0:n, 1:2], st1[0:n, 1:2], eps_ln, 1.0 / 256, n)
                f.ts(vc[0:n], vc[0:n], st1[0:n, 1:2], ALU.mult)
                if BIS3 < 4: return
                f.tt(vc[0:n], vc[0:n], lng[0:n], ALU.mult)
                f.tt(vc[0:n], vc[0:n], lnb[0:n], ALU.add)
                if BIS3 < 5: return
                if sample:
                    f.dma(O["av_s"][l], vc[0:n])
                    v3 = vc[0:n].rearrange("p (g d) -> p g d", g=4)
                    f.tt(ya[0:n].rearrange("p (g d) -> p g d", g=4), v3, ws0.unsqueeze(2).broadcast_to([NS, 4, 64]), ALU.mult)
                    f.tt(ya[0:n].rearrange("p (g d) -> p g d", g=4), ya[0:n].rearrange("p (g d) -> p g d", g=4),
                         bs0.unsqueeze(2).broadcast_to([NS, 4, 64]), ALU.add)
                    f.tt(yab[0:n], ya[0:n], u_sb[0:n], ALU.mult)
                else:
                    f.copy(vnb, vc, en="pool")
                    if BIS3 < 6: return
                    pm = banks[1]
                    for g in range(4):
                        f.mm(pm[:, g * 64:(g + 1) * 64], WmT[:, g, :], vnb[:, g * 64:(g + 1) * 64])
                    if BIS3 < 7: return
                    f.tt(ya.rearrange("p (g d) -> p g d", g=4), pm[:, 0:256].rearrange("p (g d) -> p g d", g=4),
                         bs_t.unsqueeze(2).broadcast_to([128, 4, 64]), ALU.add)
                    f.tt(yab, ya, u_sb, ALU.mult)

            def qkv(n, col0):
                pq = banks[2]
                pv = banks[3]
                proj_T(pq, n, col0, wAC, 512, 1024)
                proj_T(pv, n, col0, wAC, 1024, 1280)
                f.act(sq1[0:n], pq[0:n], AF.Square)
                f.reduce(st2[0:n], sq1[0:n].rearrange("p (h d) -> p h d", h=8))
                rsqrt_(st2[0:n], st2[0:n], eps_rms, 1.0 / 64, n)
                f.tt(qkn[0:n].rearrange("p (h d) -> p h d", h=8), pq[0:n].rearrange("p (h d) -> p h d", h=8),
                     st2[0:n].unsqueeze(2).broadcast_to([n, 8, 64]), ALU.mult)
                f.tt(qkn[0:n], qkn[0:n], gqk[0:n], ALU.mult, en="pool")
                f.copy(qkb[0:n], qkn[0:n], en="pool")
                f.copy(vf[0:n], pv[0:n, 0:256], en="act")

            BIS = int(os.environ.get("BIS", "9"))
            for i in range(NTK if (("A" in stages or "C" in stages) and BIS >= 2) else 0):
                t0 = i * 128
                col0 = t0
                BIS2 = int(os.environ.get("BIS2", "9"))
                if "A" in stages:
                    mixer_A(128, col0, False)
                    if BIS2 >= 2:
                        pt = bankb(7)
                        for k in range(2):
                            f.transpose(pt[:, k * 128:(k + 1) * 128], yab[:, k * 128:(k + 1) * 128], ident_b)
                        f.copy(yaT, pt[:, 0:256].rearrange("p (k t) -> p k t", k=2), en="act")
                    else:
                        f.memset(yaT, 0.0)
                if BIS2 < 3:
                    continue
                if "C" in stages:
                    qkv(128, col0)
                    f.dma(O["k_p"][l, t0:t0 + 128, :], qkn[:, 256:512])
                    f.dma(O["v_p"][l, t0:t0 + 128, :], vf)
                    f.copy(Vall[:, i, :], vf, en="pool")
                    pt = bankb(7)
                    for k in range(2):
                        f.transpose(pt[:, 256 + k * 128:256 + (k + 1) * 128], qkb[:, k * 128:(k + 1) * 128], ident_b)
                        f.transpose(pt[:, 512 + k * 128:512 + (k + 1) * 128], qkb[:, 256 + k * 128:256 + (k + 1) * 128], ident_b)
                    f.copy(QT, pt[:, 256:512].rearrange("p (k t) -> p k t", k=2), en="act")
                    f.copy(KT[:, :, t0:t0 + 128], pt[:, 512:768].rearrange("p (k t) -> p k t", k=2), en="dve")
                    pO = banks[6]

                    def qk_into(ps, kb, last):
                        for h in range(4):
                            r0 = (h % 2) * 64
                            f.mm(ps[:, h * 128:(h + 1) * 128], KT[r0:r0 + 64, h // 2, kb * 128:(kb + 1) * 128],
                                 QT[r0:r0 + 64, h // 2, :], start=(h == 0), stop=False, skip_group_check=True)
                            f.mm(ps[:, h * 128:(h + 1) * 128], ones_b[0:1, :], biasrow[0:1, h, :], start=False,
                                 stop=(last and h == 3), skip_group_check=True)

                    def spk(kb):
                        return K(sp_all[:, kb, :], "sp_all%d" % kb)

                    for n1, kb in enumerate(range(i, -1, -1)):
                        pS = banks[4 + n1 % 2]
                        et = e_t2[n1 % 2]
                        qk_into(pS, kb, True)
                        f.act(et, pS, AF.Exp)
                        f.act(spk(kb), et, AF.Ln, bias=one_c)
                        if kb == i:
                            f.tt(spk(kb), spk(kb), mstrict_b, ALU.mult)
                    pend = None
                    for n2, kb in enumerate(range(i, -1, -1)):
                        pE = banks[4 + n2 % 2]
                        at = att_t2[n2 % 2]
                        cs_cur, cs_nxt = Cs2[n2 % 2], Cs2[(n2 + 1) % 2]
                        qk_into(pE, kb, False)
                        f.mm(pE, ntincl_b, spk(kb), start=False, stop=(kb == i), skip_group_check=True)
                        if kb < i:
                            f.mm(pE, negones_b, cs_cur, start=False, stop=True, skip_group_check=True)
                        if pend is not None:
                            pend()
                        f.act(at, pE, AF.Exp)
                        if kb == i:
                            f.tt(at, at, mstrict_b, ALU.mult)
                        if kb > 0:
                            if kb == i:
                                f.copy(cs_nxt, spk(kb), en="pool")
                            else:
                                f.tt(cs_nxt, cs_cur, spk(kb), ALU.add, en="pool")

                        def av(kb=kb, at=at):
                            for h in range(4):
                                f.mm(pO[0:64, h * 128:(h + 1) * 128], Vall[:, kb, h * 64:(h + 1) * 64],
                                     at[:, h * 128:(h + 1) * 128], start=(kb == i and h == 0),
                                     stop=(h == 3), skip_group_check=True)
                        pend = av
                    pend()
                    f.copy(ycT, pO[0:64, :], en="act")
                for half in range(2):
                    ph = banks[half]
                    for mm_ in range(4):
                        m = half * 4 + mm_
                        ops = []
                        if "A" in stages:
                            ops += [(woA[:, k, m * 128:(m + 1) * 128], yaT[:, k, :]) for k in range(2)]
                        if "C" in stages:
                            ops += [(woC[:, h, m * 128:(m + 1) * 128], ycT[:, h * 128:(h + 1) * 128]) for h in range(4)]
                        for j, (a, b) in enumerate(ops):
                            f.mm(ph[:, mm_ * 128:(mm_ + 1) * 128], a, b, start=(j == 0), stop=(j == len(ops) - 1))
                    f.tt(hT[:, half * 4:(half + 1) * 4, t0:t0 + 128], hT[:, half * 4:(half + 1) * 4, t0:t0 + 128],
                         ph.rearrange("p (c t) -> p c t", c=4), ALU.add)

            f.barrier()
            p1a.close()
            f.recs.clear()
            mark('L%d pass1-sample' % l)
            col0 = T
            yaT_s = sb1b("yaT_s", [128, 2, NS], BF16)
            ycT_s = sb1b("ycT_s", [128, 2, NS], BF16)
            f.memset(yaT_s, 0.0)
            f.memset(ycT_s, 0.0)
            if "A" in stages and BIS >= 3:
                mixer_A(NS, col0, True)
                pt = bankb(7)
                for k in range(2):
                    f.transpose(pt[:, k * 128:k * 128 + NS], yab[0:NS, k * 128:(k + 1) * 128], ident_b[0:NS, 0:NS])
                f.copy(yaT_s, pt[:, 0:256].rearrange("p (k t) -> p k t", k=2)[:, :, 0:NS], en="act")
            if "C" in stages:
                qkv(NS, col0)
                f.dma(O["k_s"][l], qkn[0:NS, 256:512])
                f.dma(O["v_s"][l], vf[0:NS])
                f.dma(scr_q, qkn[0:NS, 0:256])
                PSL = 8
                NSL = 128 // PSL
                qrep = sb1b("qrep", [G, 256])
                idx = sb1b("idx", [G, 1], I32)
                idx8 = sb1b("idx8", [G, 1], I32)
                kv_t = [sb1b("kv%d" % j, [G, PSL * 256]) for j in range(3)]
                prod = sb1b("prod", [G, PSL * 256])
                s_all = sb1b("s_all", [G, 128, 4])
                e_d = sb1b("e_d", [G, 128, 4])
                sp_d = sb1b("sp_d", [G, 128, 4])
                cum_d = sb1b("cum_d", [G, 128, 4])
                tot_d = sb1b("tot_d", [G, 4])
                R_d = sb1b("R_d", [G, 4])
                att_d = sb1b("att_d", [G, 128, 4])
                yacc = sb1b("yacc", [G, 256])
                ypart = sb1b("ypart", [G, 256])
                for g in range(NG):
                    f.dma(idx, I["ptab"][g * G:(g + 1) * G, :])
                    f.ts(idx8, idx, NSL, ALU.mult)
                    f.dma(qrep, scr_q[g * SG:(g + 1) * SG, :].unsqueeze(1).broadcast_to([SG, NPG, 256]))
                    for sl in range(NSL):
                        kt = kv_t[sl % 3]
                        src = bass.AP(I["cache_k"].tensor, 0, [[PSL * 256, NPHYS * NSL], [1, PSL * 256]])
                        eo = l * NPHYS * 32768 + sl * PSL * 256
                        f.dma(kt, I["cache_k"], q="pool", extra_reads=[idx8],
                              fn=lambda e, kt=kt, src=src, eo=eo: e.indirect_dma_start(
                                  out=kt, out_offset=None, in_=src,
                                  in_offset=bass.IndirectOffsetOnAxis(ap=idx8[:, :], axis=0), element_offset=eo))
                        f.tt(prod.rearrange("p (s c) -> p s c", s=PSL), kt.rearrange("p (s c) -> p s c", s=PSL),
                             qrep.unsqueeze(1).broadcast_to([G, PSL, 256]), ALU.mult)
                        f.reduce(s_all[:, sl * PSL:(sl + 1) * PSL, :], prod.rearrange("p (s h d) -> p s h d", s=PSL, h=4))
                    f.tt(s_all, s_all, cbG.unsqueeze(1).broadcast_to([G, 128, 4]), ALU.add)
                    f.act(e_d, s_all, AF.Exp)
                    f.act(sp_d, e_d, AF.Ln, bias=one_c[0:G])
                    f.copy(cum_d[:, 127:128, :], sp_d[:, 127:128, :])
                    cur, nxt = sp_d, cum_d
                    sh = 1
                    bufs = [cum_d, e_d]
                    bi = 0
                    src_t = sp_d
                    while sh < 128:
                        dst_t = bufs[bi]
                        f.tt(dst_t[:, 0:128 - sh, :], src_t[:, 0:128 - sh, :], src_t[:, sh:128, :], ALU.add)
                        f.copy(dst_t[:, 128 - sh:128, :], src_t[:, 128 - sh:128, :], en="pool")
                        src_t = dst_t
                        bi ^= 1
                        sh *= 2
                    cumI = src_t
                    pR = banks[4]
                    f.copy(tot_d, cumI[:, 0, :])
                    f.mm(pR[0:G, 0:4], pgsuf, tot_d)
                    f.copy(R_d, pR[0:G, 0:4], en="act")
                    f.tt(att_d, s_all, cumI, ALU.subtract)
                    f.tt(att_d, att_d, R_d.unsqueeze(1).broadcast_to([G, 128, 4]), ALU.subtract)
                    f.act(att_d, att_d, AF.Exp)
                    for sl in range(NSL):
                        vt = kv_t[sl % 3]
                        src = bass.AP(I["cache_v"].tensor, 0, [[PSL * 256, NPHYS * NSL], [1, PSL * 256]])
                        eo = l * NPHYS * 32768 + sl * PSL * 256
                        f.dma(vt, I["cache_v"], q="pool", extra_reads=[idx8],
                              fn=lambda e, vt=vt, src=src, eo=eo: e.indirect_dma_start(
                                  out=vt, out_offset=None, in_=src,
                                  in_offset=bass.IndirectOffsetOnAxis(ap=idx8[:, :], axis=0), element_offset=eo))
                        f.tt(prod.rearrange("p (s h d) -> p s h d", s=PSL, h=4),
                             vt.rearrange("p (s h d) -> p s h d", s=PSL, h=4),
                             att_d[:, sl * PSL:(sl + 1) * PSL, :].unsqueeze(3).broadcast_to([G, PSL, 4, 64]), ALU.mult)
                        dst = yacc if sl == 0 else ypart
                        f.reduce(dst, prod.rearrange("p (s c) -> p c s", s=PSL))
                        if sl > 0:
                            f.tt(yacc, yacc, ypart, ALU.add, en="pool")
                    pY = banks[5]
                    for k in range(2):
                        f.mm(pY[:, k * SG:(k + 1) * SG], yacc[:, k * 128:(k + 1) * 128], seqind)
                    f.copy(ycT_s[:, :, g * SG:(g + 1) * SG], pY[:, 0:2 * SG].rearrange("p (k s) -> p k s", k=2), en="act")
            woC2 = sb1b("woC2", [128, 2, D], BF16)
            f.dma(woC2, W["w_out"][l][768:1024, :].rearrange("(k p) n -> p k n", p=128), q="pool")
            ph = banks[0]
            for m in range(8):
                ops = []
                if "A" in stages:
                    ops += [(woA[:, k, m * 128:(m + 1) * 128], yaT_s[:, k, :]) for k in range(2)]
                if "C" in stages:
                    ops += [(woC2[:, k, m * 128:(m + 1) * 128], ycT_s[:, k, :]) for k in range(2)]
                for j, (a, b) in enumerate(ops):
                    f.mm(ph[:, m * NS:(m + 1) * NS], a, b, start=(j == 0), stop=(j == len(ops) - 1))
            if "A" in stages or "C" in stages:
                f.tt(hT[:, :, T:T + NS], hT[:, :, T:T + NS], ph[:, 0:8 * NS].rearrange("p (c t) -> p c t", c=8), ALU.add)
            f.barrier()
            p1b.close()
        f.recs.clear()

        mark('L%d pass15' % l)
        if "B" in stages:
            with contextlib.ExitStack() as p15:
                wBf = p15.enter_context(nc.sbuf_tensor("wBf_%d" % l, [128, 8, 1792], BF16)).ap()
                stg = [p15.enter_context(nc.sbuf_tensor("stg%d_%d" % (j, l), [128, 1792], F32)).ap() for j in range(2)]
                f.dma(wBf, W["w_in"][l].rearrange("(k p) n -> p k n", p=128)[:, :, 512:2304], q="pool")
                cnt15 = 0
                for i in range(NTK + 1):
                    n = 128 if i < NTK else NS
                    st_ = stg[i % 2]
                    for (c0, ncol) in [(0, 512), (512, 512), (1024, 512), (1536, 256)]:
                        ps = banks[cnt15 % 4]
                        proj_T(ps, n, i * 128, wBf, c0, c0 + ncol)
                        f.copy(st_[0:n, c0:c0 + ncol], ps[0:n, 0:ncol], en="act" if cnt15 % 2 else "dve")
                        cnt15 += 1
                    f.dma(pb_scr[i * 128:i * 128 + n, :], st_[0:n, :])
                f.dma(O["shift_p"][l].unsqueeze(0), pb_scr[T - 1:T, :])
                f.dma(O["shift_s"][l], pb_scr[T:T + NS, :])
                f.barrier()
            f.recs.clear()
        hn_free()
        mark('L%d pass2' % l)
        if "B" in stages:
            with contextlib.ExitStack() as p2:
                def sb2(name, shape, dt=F32):
                    return p2.enter_context(nc.sbuf_tensor(name + "_%d" % l, list(shape), dt)).ap()
                _rwkv_pass(nc, f, l, L, T, NS, NTK, W, I, O, sb2, banks, bankb, hT, pb_scr, scr_b, scr_y, scr_f,
                           dict(ident_f=ident_f, ident_b=ident_b, maskA=maskA, maskB=maskB, maskC=maskC, eye8=eye8,
                                triblk=triblk, chind=chind, shiftA=shiftA, shiftB=shiftB, eps_gn=eps_gn, eps_kk=eps_kk,
                                one_c=one_c, onesf=onesf), rsqrt_, sigmoid_, bcload, proj_T)
                f.barrier()
            f.recs.clear()

        mark('L%d ffn' % l)
        if "F" in stages:
            hn_alloc('f%d' % l)
            rmsnorm_all("ffn_norm", l)
            with contextlib.ExitStack() as p3:
                def sb3(name, shape, dt=F32):
                    return p3.enter_context(nc.sbuf_tensor(name + "_%d" % l, list(shape), dt)).ap()
                wg = [sb3("wg%d" % j, [128, 8, 512], BF16) for j in range(2)]
                wu = [sb3("wu%d" % j, [128, 8, 512], BF16) for j in range(2)]
                wd = [sb3("wd%d" % j, [128, 4, D], BF16) for j in range(2)]
                silu_t2 = [sb3("silu_t%d" % j, [128, 512]) for j in range(2)]
                actT = [sb3("actT%d" % j, [128, 4, 512], BF16) for j in range(2)]
                nblk = (DFF + 511) // 512
                gsrc = W["w_gate"][l].rearrange("(k p) n -> p k n", p=128)
                usrc = W["w_up"][l].rearrange("(k p) n -> p k n", p=128)
                steps = []
                for b in range(nblk):
                    for (c0, n) in tblocks:
                        steps.append((b, c0, n))
                loaded = set()

                def load_block(b):
                    if b in loaded or b >= nblk:
                        return
                    loaded.add(b)
                    c0f = b * 512
                    nf = min(512, DFF - c0f)
                    j = b % 2
                    f.dma(wg[j][:, :, 0:nf], gsrc[:, :, c0f:c0f + nf], q="pool")
                    f.dma(wu[j][:, :, 0:nf], usrc[:, :, c0f:c0f + nf], q="pool")
                    f.dma(wd[j][:, 0:nf // 128, :], W["w_down"][l][c0f:c0f + nf, :].rearrange("(c p) n -> p c n", p=128), q="pool")

                def gateup(si):
                    b, c0, n = steps[si]
                    load_block(b)
                    j = b % 2
                    ncf = min(512, DFF - b * 512) // 128
                    aT = actT[si % 2]
                    for c in range(ncf):
                        pg = banks[(2 * c) % 4]
                        pu = banks[(2 * c + 1) % 4]
                        for k in range(8):
                            f.mm(pg[:, 0:n], wg[j][:, k, c * 128:(c + 1) * 128], hn.t[:, k, c0:c0 + n],
                                 start=(k == 0), stop=(k == 7))
                        for k in range(8):
                            f.mm(pu[:, 0:n], wu[j][:, k, c * 128:(c + 1) * 128], hn.t[:, k, c0:c0 + n],
                                 start=(k == 0), stop=(k == 7))
                        silu_t = silu_t2[c % 2]
                        f.act(silu_t[:, 0:n], pg[:, 0:n], AF.Silu)
                        f.tt(aT[:, c, 0:n], silu_t[:, 0:n], pu[:, 0:n], ALU.mult)

                def down(si):
                    b, c0, n = steps[si]
                    j = b % 2
                    ncf = min(512, DFF - b * 512) // 128
                    aT = actT[si % 2]
                    for m in range(8):
                        pd = banks[4 + m % 3]
                        for c in range(ncf):
                            f.mm(pd[:, 0:n], wd[j][:, c, m * 128:(m + 1) * 128], aT[:, c, 0:n],
                                 start=(c == 0), stop=(c == ncf - 1))
                        add_h(m, c0, n, pd[:, 0:n])

                for si in range(len(steps) + 1):
                    if si < len(steps):
                        gateup(si)
                    if si > 0:
                        down(si - 1)
                f.barrier()
            f.recs.clear()
            hn_free()

        mark('L%d ple' % l)
        if "P" in stages:
            hn_alloc('p%d' % l)
            rmsnorm_all("ple_norm", l)
            with contextlib.ExitStack() as p4:
                def sb4(name, shape, dt=F32):
                    return p4.enter_context(nc.sbuf_tensor(name + "_%d" % l, list(shape), dt)).ap()
                wpg = sb4("wpg", [128, 8, D], BF16)
                wpl = sb4("wpl", [128, 2, D], BF16)
                f.dma(wpg, W["w_ple_gate"][l].rearrange("(k p) n -> p k n", p=128), q="pool")
                f.dma(wpl, W["w_ple"][l].rearrange("(k p) n -> p k n", p=128), q="pool")
                peT = sb4("peT", [128, 2, NT], BF16)
                pet = [sb4("pet%d" % j, [128, 256]) for j in range(2)]
                for i in range(NTK + 1):
                    n = 128 if i < NTK else NS
                    src = I["pp"][l, i * 128:(i + 1) * 128, :] if i < NTK else I["psm"][l]
                    pe_ = pet[i % 2]
                    f.dma(pe_[0:n], src)
                    pt = banks[i % 2]
                    for k in range(2):
                        f.transpose(pt[:, k * 128:k * 128 + n], pe_[0:n, k * 128:(k + 1) * 128], ident_f[0:n, 0:n])
                    f.copy(peT[:, :, i * 128:i * 128 + n], pt[:, 0:256].rearrange("p (k t) -> p k t", k=2)[:, :, 0:n],
                           en="act" if i % 2 else "dve")
                gate_t = [sb4("gate_t%d" % j, [128, 512]) for j in range(2)]
                cnt = 0
                for (c0, n) in tblocks:
                    for m in range(8):
                        pa = banks[2 + (cnt % 2) * 2]
                        pb_ = banks[3 + (cnt % 2) * 2]
                        gt = gate_t[cnt % 2]
                        cnt += 1
                        for k in range(8):
                            f.mm(pa[:, 0:n], wpg[:, k, m * 128:(m + 1) * 128], hn.t[:, k, c0:c0 + n],
                                 start=(k == 0), stop=(k == 7))
                        for k in range(2):
                            f.mm(pb_[:, 0:n], wpl[:, k, m * 128:(m + 1) * 128], peT[:, k, c0:c0 + n],
                                 start=(k == 0), stop=(k == 1))
                        f.act(gt[:, 0:n], pa[:, 0:n], AF.Sigmoid)
                        f.tt(gt[:, 0:n], gt[:, 0:n], pb_[:, 0:n], ALU.mult)
                        f.tt(hT[:, m, c0:c0 + n], hT[:, m, c0:c0 + n], gt[:, 0:n], ALU.add, en="pool")
                f.barrier()
            f.recs.clear()
            hn_free()

    mark('final')
    yt = [sb("yt%d" % j, [128, D]) for j in range(2)]
    for i in range(NTK + 1):
        n = 128 if i < NTK else NS
        y_ = yt[i % 2]
        for half in range(2):
            pst = banks[(i % 2) * 2 + half]
            for c in range(4):
                m = half * 4 + c
                f.transpose(pst[0:n, c * 128:(c + 1) * 128], hT[:, m, i * 128:i * 128 + n], ident_f)
            f.copy(y_[0:n, half * 512:(half + 1) * 512], pst[0:n, :], en="act" if half else "dve")
        dst = O["y_p"][i * 128:(i + 1) * 128, :] if i < NTK else O["y_s"]
        f.dma(dst, y_[0:n])
    f.finish()
    es.close()
    return nc, f


def _rwkv_pass(nc, f, l, L, T, NS, NTK, W, I, O, sb2_outer, banks, bankb, hT, pb_scr, scr_b, scr_y, scr_f, CT, rsqrt_, sigmoid_,
               bcload, proj_T):
    import contextlib
    CH = BF16
    NH = 8
    CW = NH * 64
    ident_f, ident_b = CT["ident_f"], CT["ident_b"]
    maskA, maskB, maskC, eye8 = CT["maskA"], CT["maskB"], CT["maskC"], CT["eye8"]
    triblk, chind, shiftA, shiftB = CT["triblk"], CT["chind"], CT["shiftA"], CT["shiftB"]
    eps_gn, eps_kk, one_c = CT["eps_gn"], CT["eps_kk"], CT["one_c"]
    ptb = bankb(7)
    win = W["w_in"][l].rearrange("(k p) n -> p k n", p=128)

    def v3(ap, h=NH):
        return ap.rearrange("p (h d) -> p h d", h=h)

    def bc3(ap, n, h=NH):
        return ap.unsqueeze(2).broadcast_to([n, h, 64])

    for hg in range(1):
      with contextlib.ExitStack() as ph_:
        def sb2(name, shape, dt=F32):
            return ph_.enter_context(nc.sbuf_tensor(name + "_%d_%d" % (l, hg), list(shape), dt)).ap()
        blocks = [(0, 512), (512, 512), (1024, 512), (1536, 256)]
        mu = sb2("mu", [128, 1792])
        bcload(mu, W["b_mu"][l])
        woB = sb2("woB", [128, 4, D], BF16)
        f.dma(woB, W["w_out"][l][256:768, :].rearrange("(k p) n -> p k n", p=128), q="pool")
        cs = slice(0, CW)
        w2 = sb2("w2", [64, CW], BF16); f.dma(w2, W["b_w2"][l][:, cs], q="pool")
        a2 = sb2("a2", [64, CW], BF16); f.dma(a2, W["b_a2"][l][:, cs], q="pool")
        g2 = sb2("g2", [128, CW], BF16); f.dma(g2, W["b_g2"][l][:, cs], q="pool")
        w0 = sb2("w0", [128, CW]); bcload(w0, W["b_w0"][l][cs])
        a0 = sb2("a0", [128, CW]); bcload(a0, W["b_a0"][l][cs])
        kkp = sb2("kkp", [128, CW]); bcload(kkp, W["b_kk"][l][cs])
        kap = sb2("kap", [128, CW]); bcload(kap, W["b_ka"][l][cs])
        rkp = sb2("rkp", [128, CW]); bcload(rkp, W["b_rk"][l].rearrange("h d -> (h d)")[cs])
        lng = sb2("blng", [128, CW]); bcload(lng, W["b_ln_g"][l][cs])
        lnb = sb2("blnb", [128, CW]); bcload(lnb, W["b_ln_b"][l][cs])

        ST = sb2("ST", [64, NH, 64]); f.memset(ST, 0.0)
        STb = sb2("STb", [64, NH, 64], BF16); f.memset(STb, 0.0)
        xs_bufs = [sb2("xs%d" % j, [128, 1792]) for j in range(1)]
        cur_b = [sb2("cur_b%d" % j, [128, 1792], BF16) for j in range(2)]
        f.memset(cur_b[1], 0.0)
        lb = sb2("lb", [128, 128], BF16)
        lb2 = sb2("lb2", [128, 128], BF16)
        lT = sb2("lT", [128, 3, 128], BF16)
        logw = sb2("logw", [128, CW])
        a_t = sb2("a_t", [128, CW])
        PKf = sb2("PKf", [128, 2, CW])
        PKf1 = sb2("PKf1", [64, 2, CW])
        PKb = sb2("PKb", [128, 3, CW], BF16)
        PKb1 = sb2("PKb1", [64, 3, CW], BF16)
        kk = sb2("kk", [128, CW])
        kmod = sb2("kmod", [128, CW])
        t1 = sb2("t1", [128, CW])
        t2 = sb2("t2", [128, CW])
        st8 = sb2("st8", [128, NH])
        st8b = sb2("st8b", [128, NH])
        eL = sb2("eL", [128, CW])
        enL = sb2("enL", [128, CW])
        fm_b = [sb2("fm_b%d" % j, [128, CW], BF16) for j in range(4)]
        QRT = sb2("QRT", [64, NH, 2, 2, 64], BF16)
        BKT = sb2("BKT", [64, NH, 2, 2, 64], BF16)
        GC = sb2("GC", [64, NH, 2])
        G1s = [sb2("G1_%d" % c, [64, NH, 128], BF16) for c in range(2)]
        G2s = [sb2("G2_%d" % c, [64, NH, 128], BF16) for c in range(2)]
        XAs = [[sb2("XA%d_%d" % (j, c), [64, NH, 64], CH) for j in range(2)] for c in range(2)]
        XTs = [[sb2("XT%d_%d" % (j, c), [64, NH, 64], CH) for j in range(2)] for c in range(2)]
        Pms = [[sb2("Pm%d_%d" % (j, c), [64, NH, 64], CH) for j in range(2)] for c in range(2)]
        TTs = [None, None]
        Wn = sb2("Wn", [64, NH, 64], BF16)
        U = sb2("U", [64, NH, 64], BF16)
        yc_t = sb2("yc_t", [64, CW])
        y2 = sb2("y2", [64, CW])
        ybb = sb2("ybb", [64, CW], BF16)
        ybT = sb2("ybT", [128, 4, 128], BF16)
        ss = sb2("ss", [NS, CW])

        def prep(n, xs):
            f.act(t1[0:n, 0:64], xs[0:n, 3 * CW:3 * CW + 64], AF.Exp, scale=-2.0)
            f.ts(t1[0:n, 0:64], t1[0:n, 0:64], 1.0, ALU.add)
            f.op("dve", lambda e: e.reciprocal(t1[0:n, 0:64], t1[0:n, 0:64]), [t1], [t1])
            f.ts(lb[0:n, 0:64], t1[0:n, 0:64], 2.0, ALU.mult, -1.0, ALU.add)
            f.copy(lb[0:n, 64:128], xs[0:n, 3 * CW + 64:3 * CW + 128], en="pool")
            sigmoid_(t1[0:n, 128:256], xs[0:n, 3 * CW + 128:3 * CW + 256], n)
            f.copy(lb2[0:n], t1[0:n, 128:256], en="pool")
            f.transpose(ptb[0:64, 0:n], lb[0:n, 0:64], ident_b[0:n, 0:n])
            f.transpose(ptb[0:64, 128:128 + n], lb[0:n, 64:128], ident_b[0:n, 0:n])
            f.transpose(ptb[:, 256:256 + n], lb2[0:n, :], ident_b[0:n, 0:n])
            f.copy(lT[0:64, 0:2, 0:n], ptb[0:64, 0:256].rearrange("p (k t) -> p k t", k=2)[:, :, 0:n], en="act")
            f.copy(lT[:, 2, 0:n], ptb[:, 256:256 + n], en="act")
            ps_w, ps_a, ps_g = banks[2], banks[3], banks[4]
            f.mm(ps_w[0:n, 0:CW], lT[0:64, 0, 0:n], w2)
            f.mm(ps_a[0:n, 0:CW], lT[0:64, 1, 0:n], a2)
            f.mm(ps_g[0:n, 0:CW], lT[:, 2, 0:n], g2)
            f.tt(t1[0:n], ps_w[0:n, 0:CW], w0[0:n], ALU.add)
            sigmoid_(t1[0:n], t1[0:n], n)
            f.ts(logw[0:n], t1[0:n], -EXPM05, ALU.mult)
            f.tt(t2[0:n], ps_a[0:n, 0:CW], a0[0:n], ALU.add)
            sigmoid_(a_t[0:n], t2[0:n], n)
            f.copy(PKf[0:n, 1, :], ps_g[0:n, 0:CW], en="act")
            r_ = xs[0:n, 0:CW]
            k_ = xs[0:n, CW:2 * CW]
            v_ = xs[0:n, 2 * CW:3 * CW]
            f.copy(PKb[0:n, 0, :], v_, en="pool")
            f.tt(kk[0:n], k_, kkp[0:n], ALU.mult)
            f.tt(t1[0:n], kk[0:n], kk[0:n], ALU.mult)
            f.reduce(st8[0:n], v3(t1[0:n]))
            rsqrt_(st8[0:n], st8[0:n], eps_kk, 1.0, n)
            f.tt(v3(kk[0:n]), v3(kk[0:n]), bc3(st8[0:n], n), ALU.mult)
            f.ts(t1[0:n], a_t[0:n], -1.0, ALU.add)
            f.tt(t1[0:n], t1[0:n], kap[0:n], ALU.mult)
            f.ts(t1[0:n], t1[0:n], 1.0, ALU.add)
            f.tt(kmod[0:n], k_, t1[0:n], ALU.mult)
            f.tt(t1[0:n], r_, kmod[0:n], ALU.mult)
            f.tt(t1[0:n], t1[0:n], rkp[0:n], ALU.mult)
            f.reduce(st8b[0:n], v3(t1[0:n]))
            f.tt(v3(PKf[0:n, 0, :]), v3(v_), bc3(st8b[0:n], n), ALU.mult)

        def shift_mix(n, xs, c0, ncol, prev):
            f.tt(t2[0:n, 0:ncol], prev, xs[0:n, c0:c0 + ncol], ALU.subtract)
            f.tt(t2[0:n, 0:ncol], t2[0:n, 0:ncol], mu[0:n, c0:c0 + ncol], ALU.mult, en="pool")
            f.tt(xs[0:n, c0:c0 + ncol], t2[0:n, 0:ncol], xs[0:n, c0:c0 + ncol], ALU.add, en="pool")

        def group_out(n, y_ap, bonus_ap, g_ap, out_bf):
            f.reduce(st8[0:n], v3(y_ap))
            f.ts(st8[0:n], st8[0:n], -1.0 / 64, ALU.mult)
            f.tt(v3(yc_t[0:n]), v3(y_ap), bc3(st8[0:n], n), ALU.add)
            f.tt(y2[0:n], yc_t[0:n], yc_t[0:n], ALU.mult)
            f.reduce(st8b[0:n], v3(y2[0:n]))
            rsqrt_(st8b[0:n], st8b[0:n], eps_gn, 1.0 / 64, n)
            f.tt(v3(yc_t[0:n]), v3(yc_t[0:n]), bc3(st8b[0:n], n), ALU.mult)
            f.tt(yc_t[0:n], yc_t[0:n], lng[0:n], ALU.mult)
            f.tt(yc_t[0:n], yc_t[0:n], lnb[0:n], ALU.add)
            f.tt(yc_t[0:n], yc_t[0:n], bonus_ap, ALU.add)
            f.tt(out_bf, yc_t[0:n], g_ap, ALU.mult)

        f.dma(xs_bufs[0], pb_scr[0:128, :])
        for i in range(NTK):
            t0 = i * 128
            cb = cur_b[i % 2]
            pvb = cur_b[(i + 1) % 2]
            xs = xs_bufs[0]
            f.copy(cb[:, 0:1024], xs[:, 0:1024], en="pool")
            f.copy(cb[:, 1024:1792], xs[:, 1024:1792], en="act")
            for j, (c0, ncol) in enumerate(blocks):
                pp = banks[j % 2]
                f.mm(pp[:, 0:ncol], shiftA, cb[:, c0:c0 + ncol], start=True, stop=False)
                f.mm(pp[:, 0:ncol], shiftB, pvb[:, c0:c0 + ncol], start=False, stop=True)
                shift_mix(128, xs, c0, ncol, pp[:, 0:ncol])
            prep(128, xs)
            psL = banks[5]
            f.mm(psL[:, 0:CW], triblk, logw)
            f.act(eL, psL[:, 0:CW], AF.Exp)
            f.act(enL, psL[:, 0:CW], AF.Exp, scale=-1.0)
            f.tt(fm_b[1], xs[:, 0:CW], eL, ALU.mult)
            f.tt(t1, psL[:, 0:CW], logw, ALU.subtract)
            f.act(eL, t1, AF.Exp)
            f.tt(fm_b[0], kk, eL, ALU.mult)
            f.tt(t2, kk, a_t, ALU.mult, en="pool")
            f.tt(fm_b[2], t2, enL, ALU.mult)
            f.tt(fm_b[3], kmod, enL, ALU.mult)
            if i + 1 < NTK:
                f.dma(xs_bufs[0], pb_scr[t0 + 128:t0 + 256, :])
            f.copy(PKb[:, 1, :], fm_b[2], en="pool")
            f.copy(PKb[:, 2, :], fm_b[3], en="pool")
            for which, (src, dst, slot) in enumerate([(fm_b[0], QRT, 0), (fm_b[1], QRT, 1), (fm_b[2], BKT, 0), (fm_b[3], BKT, 1)]):
                for h in range(NH):
                    f.transpose(ptb[0:64, h * 128:(h + 1) * 128], src[:, h * 64:(h + 1) * 64], ident_b)
                f.copy(dst[:, :, :, slot, :], ptb[0:64, 0:NH * 128].rearrange("p (q c t) -> p q c t", q=NH, c=2),
                       en="act" if which % 2 else "dve")
            psG = banks[6]
            for h in range(NH):
                f.mm(psG[0:64, h * 2:(h + 1) * 2], logw[:, h * 64:(h + 1) * 64], chind)
            f.act(GC, psG[0:64, 0:2 * NH].rearrange("p (q c) -> p q c", q=NH), AF.Exp)
            f.dma(PKb1, PKb[64:128])
            f.dma(PKf1, PKf[64:128])

            def stage1(c):
                G1, G2 = G1s[c], G2s[c]
                for h in range(NH):
                    qr = QRT[:, h, c, :, :].rearrange("p w t -> p (w t)")
                    bk, cb_ = h // 4, (h % 4) * 128
                    f.mm(banks[0 + bk][0:64, cb_:cb_ + 128], BKT[:, h, c, 0, :], qr)
                    f.mm(banks[2 + bk][0:64, cb_:cb_ + 128], BKT[:, h, c, 1, :], qr)
                    f.mm(banks[4][0:64, h * 64:(h + 1) * 64], QRT[:, h, c, 0, :], BKT[:, h, c, 0, :])
                yield
                for bk in range(2):
                    f.tt(G1[:, bk * 4:(bk + 1) * 4, :].rearrange("p h t -> p (h t)"), banks[0 + bk][0:64, :],
                         maskA[:, bk * 512:(bk + 1) * 512], ALU.mult, en="dve")
                    f.tt(G2[:, bk * 4:(bk + 1) * 4, :].rearrange("p h t -> p (h t)"), banks[2 + bk][0:64, :],
                         maskB[:, bk * 512:(bk + 1) * 512], ALU.mult, en="dve")
                XA, XT, Pm = XAs[c], XTs[c], Pms[c]
                A_, AT_, P_ = XA[0], XT[0], Pm[0]
                f.copy(A_, G1[:, :, 0:64], en="pool")
                f.tt(AT_.rearrange("p h t -> p (h t)"), banks[4][0:64, 0:CW], maskC[:, 0:CW], ALU.mult)
                f.tt(P_.rearrange("p h t -> p (h t)"), A_.rearrange("p h t -> p (h t)"), eye8[:, 0:CW], ALU.add)
                yield
                psA, psAT, psP = (banks[5], banks[6], banks[4]) if c == 0 else (banks[0], banks[1], banks[2])
                pi = 0
                for k in range(1, 6):
                    An, ATn, Pn = XA[k % 2], XT[k % 2], Pm[(pi + 1) % 2]
                    for h in range(NH):
                        if k < 5:
                            f.mm(psA[0:64, h * 64:(h + 1) * 64], AT_[:, h, :], A_[:, h, :])
                        f.mm(psAT[0:64, h * 64:(h + 1) * 64], A_[:, h, :], AT_[:, h, :])
                    yield
                    if k < 5:
                        f.copy(An.rearrange("p h t -> p (h t)"), psA[0:64, 0:CW], en="act")
                    f.copy(ATn.rearrange("p h t -> p (h t)"), psAT[0:64, 0:CW], en="dve")
                    yield
                    for h in range(NH):
                        f.mm(psP[0:64, h * 64:(h + 1) * 64], ATn[:, h, :], P_[:, h, :])
                    yield
                    f.tt(Pn.rearrange("p h t -> p (h t)"), psP[0:64, 0:CW], P_.rearrange("p h t -> p (h t)"), ALU.add)
                    yield
                    A_, AT_, P_ = An, ATn, Pn
                    pi += 1
                TTs[c] = P_

            def stage2(c):
                Vc = PKb[0:64, 0, :] if c == 0 else PKb1[:, 0, :]
                Bc = PKb[0:64, 1, :] if c == 0 else PKb1[:, 1, :]
                Kc = PKb[0:64, 2, :] if c == 0 else PKb1[:, 2, :]
                bon = PKf[0:64, 0, :] if c == 0 else PKf1[:, 0, :]
                gg = PKf[0:64, 1, :] if c == 0 else PKf1[:, 1, :]
                G1, G2, TT = G1s[c], G2s[c], TTs[c]
                psW, psU, psY, psS = banks[3], banks[4], banks[5], banks[6]
                for h in range(NH):
                    f.mm(psW[0:64, h * 64:(h + 1) * 64], QRT[:, h, c, 0, :], STb[:, h, :], start=True, stop=False)
                    f.mm(psW[0:64, h * 64:(h + 1) * 64], G2[:, h, 0:64], Vc[:, h * 64:(h + 1) * 64], start=False, stop=True)
                f.ts(Wn.rearrange("p h t -> p (h t)"), psW[0:64, 0:CW], -1.0, ALU.mult)
                for h in range(NH):
                    f.mm(psU[0:64, h * 64:(h + 1) * 64], TT[:, h, :], Wn[:, h, :])
                f.copy(U.rearrange("p h t -> p (h t)"), psU[0:64, 0:CW], en="act")
                for h in range(NH):
                    o_ = psY[0:64, h * 64:(h + 1) * 64]
                    f.mm(o_, QRT[:, h, c, 1, :], STb[:, h, :], start=True, stop=False)
                    f.mm(o_, G1[:, h, 64:128], U[:, h, :], start=False, stop=False)
                    f.mm(o_, G2[:, h, 64:128], Vc[:, h * 64:(h + 1) * 64], start=False, stop=True)
                for h in range(NH):
                    o_ = psS[0:64, h * 64:(h + 1) * 64]
                    f.mm(o_, Bc[:, h * 64:(h + 1) * 64], U[:, h, :], start=True, stop=False)
                    f.mm(o_, Kc[:, h * 64:(h + 1) * 64], Vc[:, h * 64:(h + 1) * 64], start=False, stop=True)
                f.tt(ST, ST, psS[0:64, 0:CW].rearrange("p (h d) -> p h d", h=NH), ALU.add)
                f.tt(ST, ST, GC[:, :, c].unsqueeze(2).broadcast_to([64, NH, 64]), ALU.mult)
                f.copy(STb, ST, en="pool")
                group_out(64, psY[0:64, 0:CW], bon, gg, ybb)
                for k in range(4):
                    f.transpose(ptb[:, k * 64:(k + 1) * 64], ybb[:, k * 128:(k + 1) * 128], ident_b[0:64, 0:64])
                f.copy(ybT[:, :, c * 64:(c + 1) * 64], ptb[:, 0:256].rearrange("p (k t) -> p k t", k=4), en="act")

            gens = [stage1(0), stage1(1)]
            alive = [True, True]
            next(gens[0])
            next(gens[0])
            while any(alive):
                for gi in (1, 0):
                    g = gens[gi]
                    if alive[gi]:
                        try:
                            next(g)
                        except StopIteration:
                            alive[gi] = False
            stage2(0)
            stage2(1)
            for half in range(2):
                ph = banks[5 + half]
                for mm_ in range(4):
                    m = half * 4 + mm_
                    for k in range(4):
                        f.mm(ph[:, mm_ * 128:(mm_ + 1) * 128], woB[:, k, m * 128:(m + 1) * 128], ybT[:, k, :],
                             start=(k == 0), stop=(k == 3))
                f.tt(hT[:, half * 4:(half + 1) * 4, t0:t0 + 128], hT[:, half * 4:(half + 1) * 4, t0:t0 + 128],
                     ph.rearrange("p (c t) -> p c t", c=4), ALU.add)
        for h in range(NH):
            f.transpose(banks[0][0:64, h * 64:(h + 1) * 64], ST[:, h, :], ident_f[0:64, 0:64])
        f.copy(y2, banks[0][0:64, 0:CW])
        f.dma(O["wkv_p"][l].rearrange("h i j -> i h j"), y2.rearrange("i (h j) -> i h j", h=NH))

        n = NS
        xs = xs_bufs[0]
        f.dma(xs[0:n, :], pb_scr[T:T + n, :])
        for (c0, ncol) in blocks:
            f.dma(ss[:, 0:ncol], I["sshift"][l][:, c0:c0 + ncol])
            shift_mix(n, xs, c0, ncol, ss[:, 0:ncol])
        prep(n, xs)
        f.act(t1[0:n], logw[0:n], AF.Exp)
        f.tt(t2[0:n], kk[0:n], a_t[0:n], ALU.mult)
        for q, src in enumerate([xs[0:n, 0:CW], t1[0:n], kmod[0:n], xs[0:n, 2 * CW:3 * CW], kk[0:n], t2[0:n]]):
            f.dma(scr_b[:, :, q, :], src.rearrange("s (h d) -> s h d", h=8))
        f.dma(scr_f, PKf[0:n])
        f.barrier()
      f.recs.clear()

    with contextlib.ExitStack() as ps_:
        def sb3(name, shape, dt=F32):
            return ps_.enter_context(nc.sbuf_tensor(name + "_s%d" % l, list(shape), dt)).ap()
        n = NS
        NP = NS * 8
        woBf = sb3("woBf", [128, 4, D], BF16)
        f.dma(woBf, W["w_out"][l][256:768, :].rearrange("(k p) n -> p k n", p=128), q="pool")
        lngf = sb3("lngf", [NS, 512]); bcload(lngf, W["b_ln_g"][l])
        lnbf = sb3("lnbf", [NS, 512]); bcload(lnbf, W["b_ln_b"][l])
        bg = sb3("bg", [NS, 2, 512])
        f.dma(bg, scr_f)
        pkp = sb3("pkp", [NP, 8, 64])
        f.dma(pkp[:, 0:6, :], scr_b.rearrange("s h q d -> (s h) q d")[:, 0:6, :])
        S = sb3("S", [NP, 64, 64])
        tmp = sb3("tmpS", [NP, 64, 64])
        f.dma(S.rearrange("p i j -> p (i j)"), I["swkv"][l])
        r_p, w_p, k_p, v_p, kk_p, b_p = (pkp[:, q, :] for q in range(6))

        def bj(ap):
            return ap.unsqueeze(1).broadcast_to([NP, 64, 64])

        def bi(ap):
            return ap.unsqueeze(2).broadcast_to([NP, 64, 64])

        sa = sb3("sa", [NP, 64])
        ysm = sb3("ysm", [NP, 64])
        f.tt(tmp, S, bj(kk_p), ALU.mult)
        f.reduce(sa, tmp)
        f.tt(S, S, bj(w_p), ALU.mult)
        f.tt(tmp, bi(sa), bj(b_p), ALU.mult, en="pool")
        f.tt(S, S, tmp, ALU.subtract)
        f.tt(tmp, bi(v_p), bj(k_p), ALU.mult, en="pool")
        f.tt(S, S, tmp, ALU.add)
        f.dma(O["wkv_s"][l], S.rearrange("p i j -> p (i j)"))
        f.tt(tmp, S, bj(r_p), ALU.mult)
        f.reduce(ysm, tmp)
        f.dma(scr_y, ysm)
        ytm = sb3("ytm", [NS, 512])
        f.dma(ytm, scr_y.rearrange("(s h) d -> s (h d)", h=8))
        st8 = sb3("st8", [NS, 8]); st8b = sb3("st8b", [NS, 8])
        yc_t = sb3("yc_t", [NS, 512]); y2 = sb3("y2", [NS, 512]); ybb = sb3("ybb", [NS, 512], BF16)
        v8 = lambda ap: ap.rearrange("p (h d) -> p h d", h=8)
        b8 = lambda ap: ap.unsqueeze(2).broadcast_to([n, 8, 64])
        f.reduce(st8, v8(ytm))
        f.ts(st8, st8, -1.0 / 64, ALU.mult)
        f.tt(v8(yc_t), v8(ytm), b8(st8), ALU.add)
        f.tt(y2, yc_t, yc_t, ALU.mult)
        f.reduce(st8b, v8(y2))
        rsqrt_(st8b, st8b, eps_gn, 1.0 / 64, n)
        f.tt(v8(yc_t), v8(yc_t), b8(st8b), ALU.mult)
        f.tt(yc_t, yc_t, lngf, ALU.mult)
        f.tt(yc_t, yc_t, lnbf, ALU.add)
        f.tt(yc_t, yc_t, bg[:, 0, :], ALU.add)
        f.tt(ybb, yc_t, bg[:, 1, :], ALU.mult)
        ybT_s = sb3("ybT_s", [128, 4, NS], BF16)
        for k in range(4):
            f.transpose(ptb[:, k * NS:(k + 1) * NS], ybb[:, k * 128:(k + 1) * 128], ident_b[0:n, 0:n])
        f.copy(ybT_s, ptb[:, 0:4 * NS].rearrange("p (k t) -> p k t", k=4), en="act")
        ph = banks[5]
        for m in range(8):
            for k in range(4):
                f.mm(ph[:, m * NS:(m + 1) * NS], woBf[:, k, m * 128:(m + 1) * 128], ybT_s[:, k, :], start=(k == 0), stop=(k == 3))
        f.tt(hT[:, :, T:T + NS], hT[:, :, T:T + NS], ph[:, 0:8 * NS].rearrange("p (c t) -> p c t", c=8), ALU.add)
        f.barrier()
    f.recs.clear()


from concourse.bass_utils import run_bass_kernel_spmd

_cache = {}

def run(inputs, L, T, NS_total, NPG, stages=("A", "B", "C", "F", "P"), trace=False):
    NC = 8
    NS = NS_total // NC
    NPHYS = inputs["cache_k"].shape[1]
    wsh = {n: list(inputs[n].shape) for n in WNAMES}
    key = (L, T, NS, NPG, NPHYS, tuple(stages))
    if key not in _cache:
        _cache[key] = build(L, T, NS, NPG, NPHYS, wsh, stages)[0]
    nc = _cache[key]
    cn = make_consts(NPG)
    f32 = lambda a: np.ascontiguousarray(a, dtype=np.float32)
    ck = f32(inputs["cache_k"]).reshape(L, NPHYS, 128 * 256)
    cv = f32(inputs["cache_v"]).reshape(L, NPHYS, 128 * 256)
    wts = {n: f32(inputs[n]) for n in WNAMES}
    cns = {"c_" + n: f32(v) for n, v in cn.items()}
    in_maps = []
    for c in range(NC):
        sl = slice(c * NS, (c + 1) * NS)
        m = {
            "xp": f32(inputs["x_prompt"][c]),
            "xs": f32(inputs["x_sample"][sl, 0]),
            "cache_k": ck, "cache_v": cv,
            "swkv": f32(inputs["state_wkv"][:, sl]).reshape(L, NS * 8, 4096),
            "sshift": f32(inputs["state_shift"][:, sl]),
            "ptab": np.ascontiguousarray(inputs["page_table"][sl]).astype(np.int32).reshape(NS * NPG, 1),
            "pp": f32(inputs["p_prompt"][:, c]),
            "psm": f32(inputs["p_sample"][:, sl, 0]),
        }
        m.update(wts)
        m.update(cns)
        in_maps.append(m)
    res = run_bass_kernel_spmd(nc, in_maps, core_ids=list(range(NC)), **({"trace": True} if trace else {}))
    R = res.results
    cat = lambda k, ax: np.concatenate([np.asarray(r[k]) for r in R], axis=ax)
    stk = lambda k, ax: np.stack([np.asarray(r[k]) for r in R], axis=ax)
    y_p = stk("y_p", 0)
    y_s = cat("y_s", 0)[:, None, :]
    k_p = stk("k_p", 1).reshape(L, NC, T, 4, 64)
    v_p = stk("v_p", 1).reshape(L, NC, T, 4, 64)
    wkv_p = stk("wkv_p", 1)
    shift_p = stk("shift_p", 1)
    k_s = cat("k_s", 1).reshape(L, NS_total, 1, 4, 64)
    v_s = cat("v_s", 1).reshape(L, NS_total, 1, 4, 64)
    wkv_s = cat("wkv_s", 1).reshape(L, NS_total, 8, 64, 64)
    shift_s = cat("shift_s", 1)
    av_s = cat("av_s", 1)[:, :, None, :]
    out = (y_p, y_s, k_p, v_p, wkv_p, shift_p, k_s, v_s, wkv_s, shift_s, av_s)
    return tuple(np.ascontiguousarray(o, dtype=np.float32) for o in out), res


def kernel(**inputs):
    inputs = {k: np.asarray(v) for k, v in inputs.items()}
    L = inputs["w_in"].shape[0]
    T = inputs["x_prompt"].shape[1]
    NS_total = inputs["x_sample"].shape[0]
    NPG = inputs["page_table"].shape[1]
    out, _ = run(inputs, L, T, NS_total, NPG)
    return out
```
